# Optimizing a Trainium2 kernel written in Bass

```python
import math
import jax, jax.numpy as jnp
from jax import lax
import numpy as np

D_MODEL = 2048
BATCH = 2
SEQ = 4096
DEPTH = 1
DEC_BATCH = 32
DEC_SEQ = 8
PAST_LEN = 8192
PAGE_SIZE = 128

DH_A = 128
H_A = (D_MODEL // 2) // DH_A
H_IDX = 16
D_IDX = 64
TOPK_MAX = 256
Q_BLOCK = 128
DK_B = 128
DV_B = 256
H_B = (D_MODEL // 2) // DV_B
GATE_RANK = 16
GATE_TAU = 16.0
GLA_CHUNK = 64
ROPE_THETA = 10000.0
EPS = 1e-6
NEG_BIG = -1e30
MIX_WIDTH = H_A * DH_A + H_B * DV_B
IN_SPLITS = (H_A * DH_A, H_A * DH_A, H_A * DH_A, H_A * DH_A,
             H_IDX * D_IDX, D_IDX, H_IDX,
             H_B * DK_B, H_B * DK_B, H_B * DV_B, H_B * DV_B, GATE_RANK)
IN_WIDTH = sum(IN_SPLITS)

kernel_name = 'hymba_dsa_gla_step'


def rms_norm(x, g):
    xf = x.astype(jnp.float32)
    y = xf * lax.rsqrt(jnp.mean(xf * xf, axis=-1, keepdims=True) + EPS)
    return (y * g.astype(jnp.float32)).astype(x.dtype)


def rope(x, pos):
    half = x.shape[-1] // 2
    inv = ROPE_THETA ** (-jnp.arange(half, dtype=jnp.float32) / half)
    ang = pos.astype(jnp.float32)[:, None] * inv[None, :]
    cos = jnp.cos(ang)[:, None, :]
    sin = jnp.sin(ang)[:, None, :]
    x1 = x[..., :half].astype(jnp.float32)
    x2 = x[..., half:].astype(jnp.float32)
    return jnp.concatenate([x1 * cos - x2 * sin, x2 * cos + x1 * sin], axis=-1).astype(x.dtype)


def project(h, w_in, w_gate_up, b_gate, pos):
    B, T, _ = h.shape
    z = jnp.einsum('btd,de->bte', h, w_in)
    offs = np.cumsum(IN_SPLITS)[:-1].tolist()
    qa, ka, va, ga, qi, ki, wi, qb, kb, vb, gb, ab = jnp.split(z, offs, axis=-1)
    qa = rope(qa.reshape(B, T, H_A, DH_A), pos)
    ka = rope(ka.reshape(B, T, H_A, DH_A), pos)
    va = va.reshape(B, T, H_A, DH_A)
    qi = rope(qi.reshape(B, T, H_IDX, D_IDX), pos)
    ki = rope(ki[:, :, None, :], pos)[:, :, 0, :]
    wi = wi * (H_IDX ** -0.5)
    qb = qb.reshape(B, T, H_B, DK_B) * (DK_B ** -0.5)
    kb = kb.reshape(B, T, H_B, DK_B)
    vb = vb.reshape(B, T, H_B, DV_B)
    log_a = jax.nn.log_sigmoid((jnp.einsum('btr,re->bte', ab, w_gate_up) + b_gate).astype(jnp.float32)) / GATE_TAU
    log_a = log_a.reshape(B, T, H_B, DK_B)
    return qa, ka, va, ga, qi, ki, wi, qb, kb, vb, gb, log_a


def index_scores(qi, wi, ki):
    dots = jnp.einsum('bthd,bsd->btsh', qi.astype(jnp.float32), ki.astype(jnp.float32)) * (D_IDX ** -0.5)
    return jnp.einsum('btsh,bth->bts', jax.nn.relu(dots), wi.astype(jnp.float32))


def sparse_attend(q, k_sel, v_sel, valid):
    s = jnp.einsum('bthd,btkhd->bthk', q.astype(jnp.float32), k_sel.astype(jnp.float32)) * (DH_A ** -0.5)
    s = jnp.where(valid[:, :, None, :], s, NEG_BIG)
    p = jax.nn.softmax(s, axis=-1)
    return jnp.einsum('bthk,btkhd->bthd', p, v_sel.astype(jnp.float32)).astype(q.dtype)


def dsa_prompt(qa, ka, va, qi, ki, wi):
    B, S = qa.shape[:2]
    topk = min(TOPK_MAX, S // 4)
    qblk = min(Q_BLOCK, S)
    n_blk = S // qblk
    b_ix = jnp.arange(B)[:, None, None]
    spos = jnp.arange(S)

    def block(i):
        start = i * qblk
        q_b = lax.dynamic_slice_in_dim(qa, start, qblk, axis=1)
        qi_b = lax.dynamic_slice_in_dim(qi, start, qblk, axis=1)
        wi_b = lax.dynamic_slice_in_dim(wi, start, qblk, axis=1)
        tpos = start + jnp.arange(qblk)
        causal = spos[None, :] <= tpos[:, None]
        sc = jnp.where(causal[None], index_scores(qi_b, wi_b, ki), -jnp.inf)
        _, idx = lax.top_k(sc, topk)
        valid = idx <= tpos[None, :, None]
        k_sel = ka[b_ix, idx]
        v_sel = va[b_ix, idx]
        return sparse_attend(q_b, k_sel, v_sel, valid)

    out = lax.map(block, jnp.arange(n_blk))
    return out.transpose(1, 0, 2, 3, 4).reshape(B, S, H_A * DH_A)


def dsa_sample(qa, ka, va, qi, ki, wi, cache_k, cache_v, cache_idx_k, page_table):
    Bd, T = qa.shape[:2]
    n_pages = page_table.shape[1]
    past = n_pages * PAGE_SIZE
    L = past + T
    topk = min(TOPK_MAX, L // 4)
    phys = (page_table[:, :, None] * PAGE_SIZE + jnp.arange(PAGE_SIZE)[None, None, :]).reshape(Bd, past)
    pool_k = cache_k.reshape(-1, H_A, DH_A)
    pool_v = cache_v.reshape(-1, H_A, DH_A)
    pool_ik = cache_idx_k.reshape(-1, D_IDX)
    ki_all = jnp.concatenate([pool_ik[phys].astype(ki.dtype), ki], axis=1)
    tpos = past + jnp.arange(T)
    causal = jnp.arange(L)[None, :] <= tpos[:, None]
    sc = jnp.where(causal[None], index_scores(qi, wi, ki_all), -jnp.inf)
    _, idx = lax.top_k(sc, topk)
    valid = idx <= tpos[None, :, None]
    in_past = (idx < past)[..., None, None]
    b_ix = jnp.arange(Bd)[:, None, None]
    phys_sel = phys[b_ix, jnp.minimum(idx, past - 1)]
    new_sel = jnp.clip(idx - past, 0, T - 1)
    k_sel = jnp.where(in_past, pool_k[phys_sel].astype(ka.dtype), ka[b_ix, new_sel])
    v_sel = jnp.where(in_past, pool_v[phys_sel].astype(va.dtype), va[b_ix, new_sel])
    return sparse_attend(qa, k_sel, v_sel, valid).reshape(Bd, T, H_A * DH_A)


def gla(q, k, v, log_a, s0):
    B, T = q.shape[:2]
    c = math.gcd(T, GLA_CHUNK)
    n = T // c

    def chunks(x):
        return x.astype(jnp.float32).reshape(B, n, c, H_B, x.shape[-1]).transpose(1, 0, 3, 2, 4)

    causal = jnp.tril(jnp.ones((c, c), dtype=bool))[:, :, None]

    def step(S, inp):
        qt, kt, vt, at = inp
        cum = jnp.cumsum(at, axis=2)
        o_inter = jnp.einsum('bhtk,bhkv->bhtv', qt * jnp.exp(cum), S)
        diff = cum[:, :, :, None, :] - cum[:, :, None, :, :]
        decay = jnp.exp(jnp.where(causal, diff, -jnp.inf))
        att = jnp.einsum('bhtk,bhsk,bhtsk->bhts', qt, kt, decay)
        o = o_inter + jnp.einsum('bhts,bhsv->bhtv', att, vt)
        last = cum[:, :, -1:, :]
        S_new = jnp.exp(last)[:, :, 0, :, None] * S + jnp.einsum('bhsk,bhsv->bhkv', kt * jnp.exp(last - cum), vt)
        return S_new, o

    S_fin, o = lax.scan(step, s0.astype(jnp.float32), (chunks(q), chunks(k), chunks(v), chunks(log_a)))
    o = o.transpose(1, 0, 3, 2, 4).reshape(B, T, H_B, DV_B)
    return o, S_fin


def merge_out(att, ga, o_b, gb, gla_norm, w_out):
    B, T = att.shape[:2]
    a = att.reshape(B, T, -1) * jax.nn.silu(ga)
    bpart = rms_norm(o_b, gla_norm).reshape(B, T, -1).astype(a.dtype) * jax.nn.silu(gb)
    return jnp.einsum('bte,ed->btd', jnp.concatenate([a, bpart], axis=-1), w_out)


def setup_inputs(seed: int = 0) -> dict:
    key = jax.random.key(seed)
    ks = jax.random.split(key, 16)
    n_pages = PAST_LEN // PAGE_SIZE
    used = DEC_BATCH * n_pages
    n_pool = used + used // 4
    f32 = jnp.float32
    x_prompt = jax.random.normal(ks[0], (BATCH, SEQ, D_MODEL), f32)
    x_sample = jax.random.normal(ks[1], (DEC_BATCH, DEC_SEQ, D_MODEL), f32)
    cache_k = jax.random.normal(ks[2], (DEPTH, n_pool, PAGE_SIZE, H_A, DH_A), f32)
    cache_v = jax.random.normal(ks[3], (DEPTH, n_pool, PAGE_SIZE, H_A, DH_A), f32)
    cache_idx_k = jax.random.normal(ks[4], (DEPTH, n_pool, PAGE_SIZE, D_IDX), f32)
    state_gla = jax.random.normal(ks[5], (DEPTH, DEC_BATCH, H_B, DK_B, DV_B), f32)
    page_table = jax.random.permutation(ks[6], n_pool)[:used].reshape(DEC_BATCH, n_pages).astype(jnp.int32)
    norm_in = 1.0 + 0.02 * jax.random.normal(ks[7], (DEPTH, D_MODEL), f32)
    w_in = jax.random.normal(ks[8], (DEPTH, D_MODEL, IN_WIDTH), f32) * (D_MODEL ** -0.5)
    w_gate_up = jax.random.normal(ks[9], (DEPTH, GATE_RANK, H_B * DK_B), f32) * (GATE_RANK ** -0.5)
    b_gate = 0.1 * jax.random.normal(ks[10], (DEPTH, H_B * DK_B), f32)
    gla_norm = 1.0 + 0.02 * jax.random.normal(ks[11], (DEPTH, DV_B), f32)
    w_out = jax.random.normal(ks[12], (DEPTH, MIX_WIDTH, D_MODEL), f32) * (MIX_WIDTH ** -0.5)
    norm_f = 1.0 + 0.02 * jax.random.normal(ks[13], (D_MODEL,), f32)
    return {'x_prompt': x_prompt, 'x_sample': x_sample, 'cache_k': cache_k, 'cache_v': cache_v,
            'cache_idx_k': cache_idx_k, 'state_gla': state_gla, 'page_table': page_table,
            'norm_in': norm_in, 'w_in': w_in, 'w_gate_up': w_gate_up, 'b_gate': b_gate,
            'gla_norm': gla_norm, 'w_out': w_out, 'norm_f': norm_f}


def reference(x_prompt, x_sample, cache_k, cache_v, cache_idx_k, state_gla, page_table,
              norm_in, w_in, w_gate_up, b_gate, gla_norm, w_out, norm_f):
    Bp, Tp = x_prompt.shape[:2]
    Bs, Ts = x_sample.shape[:2]
    past = page_table.shape[1] * PAGE_SIZE
    pos_p = jnp.arange(Tp)
    pos_s = past + jnp.arange(Ts)
    xp, xs = x_prompt, x_sample
    kp_l, vp_l, ikp_l, sp_l, ks_l, vs_l, iks_l, ss_l = [], [], [], [], [], [], [], []
    for l in range(DEPTH):
        hp = rms_norm(xp, norm_in[l])
        qa, ka, va, ga, qi, ki, wi, qb, kb, vb, gb, la = project(hp, w_in[l], w_gate_up[l], b_gate[l], pos_p)
        att = dsa_prompt(qa, ka, va, qi, ki, wi)
        s0 = jnp.zeros((Bp, H_B, DK_B, DV_B), jnp.float32)
        ob, s_fin = gla(qb, kb, vb, la, s0)
        xp = xp + merge_out(att, ga, ob, gb, gla_norm[l], w_out[l])
        kp_l.append(ka); vp_l.append(va); ikp_l.append(ki); sp_l.append(s_fin.astype(state_gla.dtype))
        hs = rms_norm(xs, norm_in[l])
        qa, ka, va, ga, qi, ki, wi, qb, kb, vb, gb, la = project(hs, w_in[l], w_gate_up[l], b_gate[l], pos_s)
        att = dsa_sample(qa, ka, va, qi, ki, wi, cache_k[l], cache_v[l], cache_idx_k[l], page_table)
        ob, s_fin = gla(qb, kb, vb, la, state_gla[l])
        xs = xs + merge_out(att, ga, ob, gb, gla_norm[l], w_out[l])
        ks_l.append(ka); vs_l.append(va); iks_l.append(ki); ss_l.append(s_fin.astype(state_gla.dtype))
    y_prompt = rms_norm(xp, norm_f)
    y_sample = rms_norm(xs, norm_f)
    return (y_prompt, y_sample,
            jnp.stack(kp_l), jnp.stack(vp_l), jnp.stack(ikp_l), jnp.stack(sp_l),
            jnp.stack(ks_l), jnp.stack(vs_l), jnp.stack(iks_l), jnp.stack(ss_l))
```

```python
import numpy as np
import concourse.bass as bass
import concourse.mybir as mybir
from concourse.bass_utils import run_bass_kernel_spmd

F32 = mybir.dt.float32
BF16 = mybir.dt.bfloat16
I32 = mybir.dt.int32
ALU = mybir.AluOpType
AF = mybir.ActivationFunctionType
AX = mybir.AxisListType

D = 2048
KC = 16
HA, DH = 8, 128
HIDX, DIDX = 16, 64
HB, DKB, DVB = 4, 128, 256
RANK = 16
TOPK = 256
EPS = 1e-6
NCORES = 8
C_QA, C_KA, C_VA, C_GA, C_QI, C_KI, C_WI, C_QB, C_KB, C_VB, C_GB, C_AB = (
    0, 1024, 2048, 3072, 4096, 5120, 5184, 5200, 5712, 6224, 7248, 8272)
INW = 8288
NEG = -1.0e30
NBIS = 26


def zig(qb):
    r = qb % 8
    return r if r < 4 else 7 - r


class Prog:
    NS = 10

    def __init__(self, nc):
        self.nc = nc
        self.names = ['pe', 'act', 'dve', 'pool', 'sp']
        self.ops = {k: [] for k in self.names}
        self.cnt = {k: 0 for k in self.names}
        self.waited = {k: {} for k in self.names}
        self.last_w = {}
        self.readers = {}
        self.sems = {}
        self.dcount = {k: 0 for k in self.names}
        self.dval = {}
        for k in self.names:
            self.sems['E:' + k] = nc.alloc_semaphore('e_' + k)
        for q in ('sp', 'pool', 'act'):
            for s in range(self.NS):
                key = 'D:%s:%d' % (q, s)
                self.sems[key] = nc.alloc_semaphore('d_%s_%d' % (q, s))
                self.dval[key] = 0

    def _deps(self, eng, reads, writes):
        toks = []
        for b in list(reads) + list(writes):
            t = self.last_w.get(b)
            if t is not None:
                toks.append(t)
        for b in writes:
            toks.extend(self.readers.get(b, ()))
        need = {}
        for (s, v) in toks:
            if eng == 'pe' and s == 'E:pe':
                continue
            if self.waited[eng].get(s, 0) >= v:
                continue
            if need.get(s, 0) < v:
                need[s] = v
        for s, v in need.items():
            self.waited[eng][s] = v
        return list(need.items())

    def _commit(self, tok, reads, writes):
        for b in writes:
            self.last_w[b] = tok
            self.readers[b] = []
        for b in reads:
            self.readers.setdefault(b, []).append(tok)

    def op(self, eng, fn, reads=(), writes=(), inc=True):
        waits = self._deps(eng, reads, writes)
        tok = ('E:' + eng, self.cnt[eng] + 1)
        if inc:
            self.cnt[eng] += 1
        sems = self.sems
        esem = sems['E:' + eng]

        def run(e, waits=waits, fn=fn, inc=inc):
            for s, v in waits:
                e.wait_ge(sems[s], v)
            ins = fn(e)
            if inc:
                ins.then_inc(esem, 1)
        self.ops[eng].append(run)
        self._commit(tok, reads, writes)
        return tok

    def dma(self, q, fn, reads=(), writes=()):
        i = self.dcount[q]
        self.dcount[q] += 1
        key = 'D:%s:%d' % (q, i % self.NS)
        waits = self._deps(q, reads, writes)
        prev = self.dval[key]
        if prev > 0 and self.waited[q].get(key, 0) < prev:
            waits.append((key, prev))
            self.waited[q][key] = prev
        self.dval[key] = prev + 16
        tok = (key, prev + 16)
        sems = self.sems

        def run(e, waits=waits, fn=fn, key=key):
            for s, v in waits:
                e.wait_ge(sems[s], v)
            fn(e).then_inc(sems[key], 16)
        self.ops[q].append(run)
        self._commit(tok, reads, writes)
        return tok

    def barrier(self, final=False):
        allt = [('E:' + k, self.cnt[k]) for k in self.names if self.cnt[k] > 0]
        allt += [(k, v) for k, v in self.dval.items() if v > 0]
        engs = ['sp'] if final else self.names
        for eng in engs:
            waits = []
            for s, v in allt:
                if s == 'E:' + eng:
                    continue
                if self.waited[eng].get(s, 0) < v:
                    waits.append((s, v))
                    self.waited[eng][s] = v
            sems = self.sems

            def run(e, waits=waits):
                for s, v in waits:
                    e.wait_ge(sems[s], v)
            self.ops[eng].append(run)

    def emit(self):
        nc = self.nc
        ops = self.ops
        with nc.Block() as blk:
            @blk.sync
            def _(e):
                for f in ops['sp']:
                    f(e)

            @blk.tensor
            def _(e):
                for f in ops['pe']:
                    f(e)

            @blk.scalar
            def _(e):
                for f in ops['act']:
                    f(e)

            @blk.vector
            def _(e):
                for f in ops['dve']:
                    f(e)

            @blk.gpsimd
            def _(e):
                for f in ops['pool']:
                    f(e)


def build(S, stage=99, NPG=0, NPOOL=0):
    SDSA = NPG > 0
    from contextlib import ExitStack
    NT = S // 128
    NOWN = NT // 4
    NO = NOWN * 128
    GT = min(NT, 8)
    NG = NT // GT
    NCH = S // 64
    LK = [min(S, 512 * (k + 1)) for k in range(NOWN)]
    SOFF = [sum(LK[:k]) for k in range(NOWN)]
    STOT = sum(LK)
    nc = bass.Bass("TRN2", target_bir_lowering=False)
    P = Prog(nc)

    def din(name, shape, dt=F32):
        return nc.dram_tensor(name, list(shape), dt, kind="ExternalInput").ap()

    def dout(name, shape, dt=F32):
        return nc.dram_tensor(name, list(shape), dt, kind="ExternalOutput").ap()

    def dscr(name, shape, dt):
        return nc.dram_tensor(name, list(shape), dt, kind="Internal").ap()

    xp = din("xp", [S, D])
    xo = din("xo", [NO, D])
    w_dsa = din("w_dsa", [D, 5200])
    w_gla = din("w_gla", [D, HB * 768 + 16])
    w_out = din("w_out", [D, D])
    g_in_pk = din("g_in_pk", [128, KC])
    g_f = din("g_f", [1, D])
    gla_g = din("gla_g", [1, DVB])
    wgb = din("wgb", [32, HB * DKB])
    csA_p = din("csA_p", [S, 128])
    csI_p = din("csI_p", [S, 64])
    csA_o = din("csA_o", [NO, 128])
    csI_o = din("csI_o", [NO, 64])
    ident_d = din("ident", [128, 128])
    tri_d = din("tri", [64, 64])
    mask_o = din("mask_o", [NOWN, 128, 512])
    bidx_d = din("bidx", [128, NOWN], I32)
    xsm = din("xsm", [128, D])
    csA_s = din("csA_s", [128, 128])
    csI_s = din("csI_s", [128, 64])
    st_s_in = din("st_s_in", [4, HB, DKB, DVB])
    if SDSA:
        ck = din("ck", [NPOOL * 128, HA * DH])
        cv = din("cv", [NPOOL * 128, HA * DH])
        cik = din("cik", [NPOOL * 128, DIDX])
        pt_s = din("pt_s", [4, NPG], I32)
        iota_p = din("iota_p", [128, 1])
        mask_s = din("mask_s", [128, 128])

    y_p = dout("y_p", [NO, D])
    nk_p = dout("nk_p", [NO, HA * DH])
    nv_p = dout("nv_p", [NO, HA * DH])
    nik_p = dout("nik_p", [NO, DIDX])
    st_p = dout("st_p", [HB, DKB, DVB])
    nk_s = dout("nk_s", [128, HA * DH])
    nv_s = dout("nv_s", [128, HA * DH])
    nik_s = dout("nik_s", [128, DIDX])
    st_s = dout("st_s", [4, HB, DKB, DVB])
    y_s = dout("y_s", [128, D])

    KT = dscr("KT", [HA, 128, S], BF16)
    Vs = dscr("Vs", [S, HA * DH], BF16)
    Vg = dscr("Vg", [S, HB * DVB], BF16)
    Gg = dscr("Gg", [S, HB * DVB], BF16)
    QTg = dscr("QTg", [HB, 128, S], BF16)
    KTg = dscr("KTg", [HB, 128, S], BF16)
    bp_loc = dscr("bp_loc", [S, HB * DVB], BF16)
    QTs = dscr("QTs", [HB, 128, 128], BF16)
    KTs = dscr("KTs", [HB, 128, 128], BF16)
    Vgs = dscr("Vgs", [128, HB * DVB], BF16)
    Ggs = dscr("Ggs", [128, HB * DVB], BF16)
    bps = dscr("bps", [128, HB * DVB], BF16)
    QaS = dscr("QaS", [128, 1024], BF16)
    QiS = dscr("QiS", [128, 1024], BF16)
    GaS = dscr("GaS", [128, 1024], BF16)
    KsN = dscr("KsN", [128, 1024], BF16)
    VsN = dscr("VsN", [128, 1024], BF16)
    KiN = dscr("KiN", [128, 64], BF16)
    WsS = dscr("WsS", [128, HIDX], F32)
    aS = dscr("aS", [128, 1024], BF16)
    gaD = dscr("gaD", [NO, 1024], BF16)

    es0 = ExitStack()

    def sb(es, name, shape, dt):
        return es.enter_context(nc.sbuf_tensor(name, list(shape), dt))

    def ps(es, name, shape, dt=F32):
        return es.enter_context(nc.psum_tensor(name, list(shape), dt))

    def mm(out, lhsT, rhs, start, stop, reads, writes, inc=True):
        P.op('pe', lambda e: e.matmul(out, lhsT, rhs, start=start, stop=stop), reads, writes, inc)

    def tr(out, in_, ident, reads, writes, inc=True):
        P.op('pe', lambda e: e.transpose(out, in_, ident), reads, writes, inc)

    g_pk = sb(es0, "g_pk", [128, KC], F32)
    ident_f = sb(es0, "ident_f", [128, 128], F32)
    ident_b = sb(es0, "ident_b", [128, 128], BF16)
    tri_b = sb(es0, "tri_b", [64, 64], F32)
    abT1 = sb(es0, "abT1", [32, S], BF16)
    eps_t = sb(es0, "eps_t", [128, 1], F32)
    abT1s = sb(es0, "abT1s", [32, 128], BF16)

    P.dma('sp', lambda e: e.dma_start(out=g_pk[:, :], in_=g_in_pk[:, :]), [], ['g_pk'])
    P.dma('sp', lambda e: e.dma_start(out=ident_f[:, :], in_=ident_d[:, :]), [], ['ident_f'])
    P.dma('sp', lambda e: e.dma_start(out=tri_b[:, :], in_=tri_d[:, :]), [], ['tri'])
    P.op('dve', lambda e: e.tensor_copy(ident_b[:, :], ident_f[:, :]), ['ident_f'], ['ident_b'])
    P.op('dve', lambda e: e.memset(abT1[:, :], 1.0), [], ['abT1'])
    P.op('dve', lambda e: e.memset(eps_t[:, :], EPS), [], ['eps'])
    P.op('dve', lambda e: e.memset(abT1s[:, :], 1.0), [], ['abT1s'])

    es1 = ExitStack()
    kiT2 = sb(es1, "kiT2", [128, S], BF16)
    gaS = sb(es1, "gaS", [128, NOWN, 1024], BF16)
    qaT = sb(es1, "qaT", [128, HA, NO], BF16)
    qiT = sb(es1, "qiT", [128, 8, NO], BF16)
    wabs = sb(es1, "wabs", [128, NOWN + 1, HIDX], F32)
    wsgn = sb(es1, "wsgn", [128, NOWN + 1, HIDX], F32)

    ep = ExitStack()
    xT = sb(ep, "xT", [128, KC, (GT + 1) * 128], BF16)
    xs = [sb(ep, "xs%d" % i, [128, D], F32) for i in range(2)]
    xh = sb(ep, "xh", [128, D], BF16)
    junk = sb(ep, "junk", [128, D], BF16)
    ss = sb(ep, "ss", [128, 4], F32)
    wst = sb(ep, "wst", [128, KC, 256], F32)
    wbf = [sb(ep, "wbf%d" % i, [128, KC, 256], BF16) for i in range(2)]
    zf = [sb(ep, "zf%d" % i, [128, 256], F32) for i in range(2)]
    zr = [sb(ep, "zr%d" % i, [128, 256], F32) for i in range(2)]
    zb = [sb(ep, "zb%d" % i, [128, 256], BF16) for i in range(2)]
    tp = [sb(ep, "tp%d" % i, [128, 256], F32) for i in range(2)]
    csA = sb(ep, "csA", [128, GT + 1, 128], F32)
    csI = sb(ep, "csI", [128, GT + 1, 64], F32)
    ktst = [sb(ep, "ktst%d" % i, [128, 2, 128], BF16) for i in range(2)]
    ftst = [sb(ep, "ftst%d" % i, [128, 512], BF16) for i in range(2)]
    pz = [ps(ep, "pz%d" % i, [128, 512], F32) for i in range(2)]
    pt = [ps(ep, "pt%d" % i, [128, 8, 128], BF16) for i in range(2)]
    pkf = [ps(ep, "pk%d" % i, [128, 1024], BF16) for i in range(2)]
    pk = [t[:, 0:256].rearrange("p (a b) -> p a b", a=2) for t in pkf]
    cnt = {'x': 0, 'w': 0, 'z': 0}

    def build_xT(src, row0, col, slot):
        b = cnt['x'] % 2
        cnt['x'] += 1
        xsb = xs[b]
        P.dma('sp', lambda e: e.dma_start(out=xsb[:, :], in_=src[row0:row0 + 128, :]), [], ['xs%d' % b])
        P.op('dve', lambda e: e.scalar_tensor_tensor(out=junk[:, :], in0=xsb[:, :], scalar=1.0, in1=xsb[:, :],
                                                      op0=ALU.mult, op1=ALU.mult, accum_out=ss[:, 0:1]),
             ['xs%d' % b], ['junk', 'ss0'])
        P.op('act', lambda e: e.activation(out=ss[:, 1:2], in_=ss[:, 0:1], func=AF.Sqrt,
                                            bias=eps_t[:, 0:1], scale=1.0 / D), ['ss0', 'eps'], ['ss1'])
        P.op('dve', lambda e: e.reciprocal(ss[:, 2:3], ss[:, 1:2]), ['ss1'], ['ss2'])
        P.op('act', lambda e: e.activation(out=xh[:, :], in_=xsb[:, :], func=AF.Copy, scale=ss[:, 2:3]),
             ['xs%d' % b, 'ss2'], ['xh'])
        for half in range(2):
            pb = pt[half]
            for i in range(8):
                kc = half * 8 + i
                tr(pb[:, i, :], xh[:, kc * 128:(kc + 1) * 128], ident_b[:, :],
                   ['xh', 'ident_b'], ['pt%d' % half], inc=(i == 7))
            eng = 'act' if half == 0 else 'dve'
            dst = xT[:, half * 8:(half + 1) * 8, col * 128:(col + 1) * 128]
            if eng == 'act':
                P.op('act', lambda e, dst=dst, pb=pb: e.copy(dst, pb[:, :, :]), ['pt%d' % half], ['xT'])
            else:
                P.op('dve', lambda e, dst=dst, pb=pb: e.tensor_copy(dst, pb[:, :, :]), ['pt%d' % half], ['xT'])

    def load_w(wsrc, c0, ncol):
        b = cnt['w'] % 2
        cnt['w'] += 1
        src = wsrc.rearrange("(k p) c -> p k c", p=128)[:, :, c0:c0 + ncol]
        P.dma('sp', lambda e: e.dma_start(out=wst[:, :, 0:ncol], in_=src), [], ['wst'])
        wb = wbf[b]
        P.op('pool', lambda e: e.tensor_tensor(out=wb[:, :, 0:ncol], in0=wst[:, :, 0:ncol],
                                               in1=g_pk[:, :, None].to_broadcast([128, KC, ncol]), op=ALU.mult),
             ['wst', 'g_pk'], ['wbf%d' % b])
        return b

    def proj_tok(wb, ncol, col, m=128):
        z = cnt['z'] % 2
        cnt['z'] += 1
        for kc in range(KC):
            mm(pz[z][0:m, 0:ncol], xT[:, kc, col * 128:col * 128 + m], wbf[wb][:, kc, 0:ncol],
               kc == 0, kc == KC - 1, ['xT', 'wbf%d' % wb], ['pz%d' % z], inc=(kc == KC - 1))
        return z

    def rope(z, nh, half, cs_ap, zi):
        w = nh * 2 * half
        src = zf[zi]
        P.op('act', lambda e: e.copy(src[:, 0:w], pz[z][:, 0:w]), ['pz%d' % z], ['zf%d' % zi])
        sv = src[:, 0:w].rearrange("p (h two f) -> p h two f", h=nh, two=2)
        dv = zr[zi][:, 0:w].rearrange("p (h two f) -> p h two f", h=nh, two=2)
        t1 = tp[0][:, 0:nh * half].rearrange("p (h f) -> p h f", h=nh)
        t2 = tp[1][:, 0:nh * half].rearrange("p (h f) -> p h f", h=nh)
        cosb = cs_ap[:, None, 0:half].to_broadcast([128, nh, half])
        sinb = cs_ap[:, None, half:2 * half].to_broadcast([128, nh, half])
        x1, x2 = sv[:, :, 0, :], sv[:, :, 1, :]
        rk = ['zf%d' % zi, 'cs']
        P.op('dve', lambda e: e.tensor_tensor(out=t1, in0=x1, in1=cosb, op=ALU.mult), rk, ['tp0'])
        P.op('dve', lambda e: e.tensor_tensor(out=t2, in0=x2, in1=sinb, op=ALU.mult), rk, ['tp1'])
        P.op('dve', lambda e: e.tensor_tensor(out=dv[:, :, 0, :], in0=t1, in1=t2, op=ALU.subtract),
             ['tp0', 'tp1'], ['zr%d' % zi])
        P.op('dve', lambda e: e.tensor_tensor(out=t1, in0=x2, in1=cosb, op=ALU.mult), rk, ['tp0'])
        P.op('dve', lambda e: e.tensor_tensor(out=t2, in0=x1, in1=sinb, op=ALU.mult), rk, ['tp1'])
        P.op('dve', lambda e: e.tensor_tensor(out=dv[:, :, 1, :], in0=t1, in1=t2, op=ALU.add),
             ['tp0', 'tp1'], ['zr%d' % zi])

    zc = {'i': 0}

    def nextz():
        zc['i'] += 1
        return zc['i'] % 2

    for gi in range(NG):
        P.dma('sp', lambda e, gi=gi: e.dma_start(
            out=csA[:, 0:GT, :], in_=csA_p[gi * GT * 128:(gi + 1) * GT * 128, :].rearrange("(n p) f -> p n f", p=128)),
            [], ['cs'])
        P.dma('sp', lambda e, gi=gi: e.dma_start(
            out=csI[:, 0:GT, :], in_=csI_p[gi * GT * 128:(gi + 1) * GT * 128, :].rearrange("(n p) f -> p n f", p=128)),
            [], ['cs'])
        for t in range(GT):
            build_xT(xp, (gi * GT + t) * 128, t, 0)
        for blk in range(4):
            wb = load_w(w_dsa, C_KA + blk * 256, 256)
            for t in range(GT):
                tt = gi * GT + t
                z = proj_tok(wb, 256, t)
                zi = nextz()
                rope(z, 2, 64, csA[:, t, :], zi)
                P.op('act', lambda e, zi=zi: e.copy(zb[zi][:, :], zr[zi][:, :]), ['zr%d' % zi], ['zb%d' % zi])
                k2 = zi
                for h in range(2):
                    tr(pk[k2][:, h, :], zb[zi][:, h * 128:(h + 1) * 128], ident_b[:, :],
                       ['zb%d' % zi, 'ident_b'], ['pk%d' % k2], inc=(h == 1))
                P.op('act', lambda e, k2=k2: e.copy(ktst[k2][:, :, :], pk[k2][:, :, :]), ['pk%d' % k2], ['ktst%d' % k2])
                P.dma('pool', lambda e, k2=k2, blk=blk, tt=tt: e.dma_start(
                    out=KT[blk * 2:blk * 2 + 2, :, tt * 128:(tt + 1) * 128].rearrange("h d s -> d h s"),
                    in_=ktst[k2][:, :, :]), ['ktst%d' % k2], ['KT'])
        for blk in range(4):
            wb = load_w(w_dsa, C_VA + blk * 256, 256)
            for t in range(GT):
                tt = gi * GT + t
                z = proj_tok(wb, 256, t)
                zi = nextz()
                P.op('act', lambda e, zi=zi, z=z: e.copy(zb[zi][:, :], pz[z][:, 0:256]), ['pz%d' % z], ['zb%d' % zi])
                P.dma('pool', lambda e, zi=zi, blk=blk, tt=tt: e.dma_start(
                    out=Vs[tt * 128:(tt + 1) * 128, blk * 256:(blk + 1) * 256], in_=zb[zi][:, :]),
                    ['zb%d' % zi], ['Vs'])
        wb = load_w(w_dsa, C_KI, 64)
        for t in range(GT):
            tt = gi * GT + t
            z = proj_tok(wb, 64, t)
            zi = nextz()
            rope(z, 1, 32, csI[:, t, :], zi)
            P.op('act', lambda e, zi=zi: e.copy(zb[zi][:, 0:64], zr[zi][:, 0:64]), ['zr%d' % zi], ['zb%d' % zi])
            P.op('act', lambda e, zi=zi: e.copy(zb[zi][:, 64:128], zr[zi][:, 0:64]), ['zr%d' % zi], ['zb%d' % zi])
            tr(pk[zi][:, 0, :], zb[zi][:, 0:128], ident_b[:, :], ['zb%d' % zi, 'ident_b'], ['pk%d' % zi])
            P.op('act', lambda e, zi=zi, tt=tt: e.copy(kiT2[:, tt * 128:(tt + 1) * 128], pk[zi][:, 0, :]),
                 ['pk%d' % zi], ['kiT2'])
        for hh in range(HB):
            wb = load_w(w_gla, hh * 768, 256)
            for which, dstD, scl in ((0, QTg, DKB ** -0.5), (1, KTg, 1.0)):
                for c4 in range(GT * 128 // 512):
                    z = cnt['z'] % 2
                    cnt['z'] += 1
                    for kc in range(KC):
                        mm(pz[z][:, 0:512], wbf[wb][:, kc, which * 128:(which + 1) * 128],
                           xT[:, kc, c4 * 512:(c4 + 1) * 512], kc == 0, kc == KC - 1,
                           ['xT', 'wbf%d' % wb], ['pz%d' % z], inc=(kc == KC - 1))
                    t0 = gi * GT * 128 + c4 * 512
                    fi = nextz()
                    P.op('act', lambda e, z=z, fi=fi, scl=scl: e.activation(
                        out=ftst[fi][:, :], in_=pz[z][:, 0:512], func=AF.Copy, scale=scl),
                        ['pz%d' % z], ['ftst%d' % fi])
                    P.dma('pool', lambda e, fi=fi, dstD=dstD, hh=hh, t0=t0: e.dma_start(
                        out=dstD[hh, :, t0:t0 + 512], in_=ftst[fi][:, :]), ['ftst%d' % fi], ['QKTg'])
            wb = load_w(w_gla, hh * 768 + 256, 256)
            for t in range(GT):
                tt = gi * GT + t
                z = proj_tok(wb, 256, t)
                zi = nextz()
                P.op('act', lambda e, zi=zi, z=z: e.copy(zb[zi][:, :], pz[z][:, 0:256]), ['pz%d' % z], ['zb%d' % zi])
                P.dma('pool', lambda e, zi=zi, tt=tt, hh=hh: e.dma_start(
                    out=Vg[tt * 128:(tt + 1) * 128, hh * DVB:(hh + 1) * DVB], in_=zb[zi][:, :]), ['zb%d' % zi], ['Vg'])
            wb = load_w(w_gla, hh * 768 + 512, 256)
            for t in range(GT):
                tt = gi * GT + t
                z = proj_tok(wb, 256, t)
                zi = nextz()
                P.op('act', lambda e, zi=zi, z=z: e.activation(out=zb[zi][:, :], in_=pz[z][:, 0:256], func=AF.Silu),
                     ['pz%d' % z], ['zb%d' % zi])
                P.dma('pool', lambda e, zi=zi, tt=tt, hh=hh: e.dma_start(
                    out=Gg[tt * 128:(tt + 1) * 128, hh * DVB:(hh + 1) * DVB], in_=zb[zi][:, :]), ['zb%d' % zi], ['Gg'])
        wb = load_w(w_gla, HB * 768, 16)
        for c4 in range(GT * 128 // 512):
            z = cnt['z'] % 2
            cnt['z'] += 1
            for kc in range(KC):
                mm(pz[z][0:16, 0:512], wbf[wb][:, kc, 0:16], xT[:, kc, c4 * 512:(c4 + 1) * 512],
                   kc == 0, kc == KC - 1, ['xT', 'wbf%d' % wb], ['pz%d' % z], inc=(kc == KC - 1))
            t0 = gi * GT * 128 + c4 * 512
            P.op('act', lambda e, z=z, t0=t0: e.copy(abT1[0:16, t0:t0 + 512], pz[z][0:16, 0:512]),
                 ['pz%d' % z], ['abT1'])

    assert NOWN <= GT
    P.dma('sp', lambda e: e.dma_start(out=csA[:, 0:NOWN, :], in_=csA_o.rearrange("(n p) f -> p n f", p=128)), [], ['cs'])
    P.dma('sp', lambda e: e.dma_start(out=csI[:, 0:NOWN, :], in_=csI_o.rearrange("(n p) f -> p n f", p=128)), [], ['cs'])
    P.dma('sp', lambda e: e.dma_start(out=csA[:, NOWN, :], in_=csA_s[:, :]), [], ['cs'])
    P.dma('sp', lambda e: e.dma_start(out=csI[:, NOWN, :], in_=csI_s[:, :]), [], ['cs'])
    for t in range(NOWN):
        build_xT(xo, t * 128, t, 0)
    build_xT(xsm, 0, NOWN, 0)
    wb = load_w(w_dsa, C_WI, 16)
    for t in range(NOWN + 1):
        z = proj_tok(wb, 16, t)
        P.op('act', lambda e, z=z, t=t: e.activation(out=wsgn[:, t, :], in_=pz[z][:, 0:16], func=AF.Sign),
             ['pz%d' % z], ['wsgn'])
        P.op('act', lambda e, z=z, t=t: e.activation(out=wabs[:, t, :], in_=pz[z][:, 0:16], func=AF.Abs,
                                                     scale=(HIDX ** -0.5) * (DIDX ** -0.5)),
             ['pz%d' % z], ['wabs'])
    for blk in range(4):
        wb = load_w(w_dsa, C_QA + blk * 256, 256)
        for t in range(NOWN + 1):
            z = proj_tok(wb, 256, t)
            zi = nextz()
            rope(z, 2, 64, csA[:, t, :], zi)
            P.op('act', lambda e, zi=zi: e.copy(zb[zi][:, :], zr[zi][:, :]), ['zr%d' % zi], ['zb%d' % zi])
            if t == NOWN:
                P.dma('pool', lambda e, zi=zi, blk=blk: e.dma_start(out=QaS[:, blk * 256:(blk + 1) * 256], in_=zb[zi][:, :]),
                      ['zb%d' % zi], ['QaS'])
                continue
            for h in range(2):
                tr(pk[zi][:, h, :], zb[zi][:, h * 128:(h + 1) * 128], ident_b[:, :],
                   ['zb%d' % zi, 'ident_b'], ['pk%d' % zi], inc=(h == 1))
            P.op('act', lambda e, zi=zi, blk=blk, t=t: e.copy(
                qaT[:, blk * 2:blk * 2 + 2, t * 128:(t + 1) * 128], pk[zi][:, :, :]), ['pk%d' % zi], ['qaT'])
    for blk in range(4):
        wb = load_w(w_dsa, C_KA + blk * 256, 256)
        for t in range(NOWN + 1):
            z = proj_tok(wb, 256, t)
            zi = nextz()
            rope(z, 2, 64, csA[:, t, :], zi)
            dst = nk_p[t * 128:(t + 1) * 128, blk * 256:(blk + 1) * 256] if t < NOWN else nk_s[:, blk * 256:(blk + 1) * 256]
            P.dma('pool', lambda e, zi=zi, dst=dst: e.dma_start(out=dst, in_=zr[zi][:, :]), ['zr%d' % zi], ['nk_p'])
            if t == NOWN:
                P.op('act', lambda e, zi=zi: e.copy(zb[zi][:, :], zr[zi][:, :]), ['zr%d' % zi], ['zb%d' % zi])
                P.dma('pool', lambda e, zi=zi, blk=blk: e.dma_start(out=KsN[:, blk * 256:(blk + 1) * 256], in_=zb[zi][:, :]),
                      ['zb%d' % zi], ['KsN'])
    for blk in range(4):
        wb = load_w(w_dsa, C_VA + blk * 256, 256)
        for t in range(NOWN + 1):
            z = proj_tok(wb, 256, t)
            zi = nextz()
            P.op('act', lambda e, zi=zi, z=z: e.copy(zr[zi][:, :], pz[z][:, 0:256]), ['pz%d' % z], ['zr%d' % zi])
            dst = nv_p[t * 128:(t + 1) * 128, blk * 256:(blk + 1) * 256] if t < NOWN else nv_s[:, blk * 256:(blk + 1) * 256]
            P.dma('pool', lambda e, zi=zi, dst=dst: e.dma_start(out=dst, in_=zr[zi][:, :]), ['zr%d' % zi], ['nv_p'])
            if t == NOWN:
                P.op('act', lambda e, zi=zi: e.copy(zb[zi][:, :], zr[zi][:, :]), ['zr%d' % zi], ['zb%d' % zi])
                P.dma('pool', lambda e, zi=zi, blk=blk: e.dma_start(out=VsN[:, blk * 256:(blk + 1) * 256], in_=zb[zi][:, :]),
                      ['zb%d' % zi], ['VsN'])
    for blk in range(4):
        wb = load_w(w_dsa, C_GA + blk * 256, 256)
        for t in range(NOWN + 1):
            z = proj_tok(wb, 256, t)
            if t == NOWN:
                zi = nextz()
                P.op('act', lambda e, z=z, zi=zi: e.activation(out=zb[zi][:, :], in_=pz[z][:, 0:256], func=AF.Silu),
                     ['pz%d' % z], ['zb%d' % zi])
                P.dma('pool', lambda e, zi=zi, blk=blk: e.dma_start(out=GaS[:, blk * 256:(blk + 1) * 256], in_=zb[zi][:, :]),
                      ['zb%d' % zi], ['GaS'])
                continue
            P.op('act', lambda e, z=z, blk=blk, t=t: e.activation(
                out=gaS[:, t, blk * 256:(blk + 1) * 256], in_=pz[z][:, 0:256], func=AF.Silu), ['pz%d' % z], ['gaS'])
    for blk in range(4):
        wb = load_w(w_dsa, C_QI + blk * 256, 256)
        for t in range(NOWN + 1):
            z = proj_tok(wb, 256, t)
            zi = nextz()
            rope(z, 4, 32, csI[:, t, :], zi)
            P.op('dve', lambda e, zi=zi, blk=blk, t=t: e.tensor_tensor(
                out=zb[zi][:, :].rearrange("p (h f) -> p h f", h=4),
                in0=zr[zi][:, :].rearrange("p (h f) -> p h f", h=4),
                in1=wabs[:, t, blk * 4:(blk + 1) * 4, None].to_broadcast([128, 4, 64]), op=ALU.mult),
                ['zr%d' % zi, 'wabs'], ['zb%d' % zi])
            if t == NOWN:
                P.dma('pool', lambda e, zi=zi, blk=blk: e.dma_start(out=QiS[:, blk * 256:(blk + 1) * 256], in_=zb[zi][:, :]),
                      ['zb%d' % zi], ['QiS'])
                continue
            for h in range(2):
                tr(pk[zi][:, h, :], zb[zi][:, h * 128:(h + 1) * 128], ident_b[:, :],
                   ['zb%d' % zi, 'ident_b'], ['pk%d' % zi], inc=(h == 1))
            P.op('act', lambda e, zi=zi, blk=blk, t=t: e.copy(
                qiT[:, blk * 2:blk * 2 + 2, t * 128:(t + 1) * 128], pk[zi][:, :, :]), ['pk%d' % zi], ['qiT'])
    wb = load_w(w_dsa, C_KI, 64)
    for t in range(NOWN + 1):
        z = proj_tok(wb, 64, t)
        zi = nextz()
        rope(z, 1, 32, csI[:, t, :], zi)
        dst = nik_p[t * 128:(t + 1) * 128, :] if t < NOWN else nik_s[:, :]
        P.dma('pool', lambda e, zi=zi, dst=dst: e.dma_start(out=dst, in_=zr[zi][:, 0:64]),
              ['zr%d' % zi], ['nik_p'])
        if t == NOWN:
            P.op('act', lambda e, zi=zi: e.copy(zb[zi][:, 0:64], zr[zi][:, 0:64]), ['zr%d' % zi], ['zb%d' % zi])
            P.dma('pool', lambda e, zi=zi: e.dma_start(out=KiN[:, :], in_=zb[zi][:, 0:64]), ['zb%d' % zi], ['KiN'])
    P.dma('pool', lambda e: e.dma_start(out=WsS[:, :], in_=wsgn[:, NOWN, :]), ['wsgn'], ['WsS'])
    tS = NOWN
    for hh in range(HB):
        wb = load_w(w_gla, hh * 768, 256)
        z = proj_tok(wb, 256, tS)
        zi = nextz()
        P.op('act', lambda e, zi=zi, z=z: e.activation(out=zb[zi][:, 0:128], in_=pz[z][:, 0:128], func=AF.Copy,
                                                        scale=DKB ** -0.5), ['pz%d' % z], ['zb%d' % zi])
        P.op('act', lambda e, zi=zi, z=z: e.copy(zb[zi][:, 128:256], pz[z][:, 128:256]), ['pz%d' % z], ['zb%d' % zi])
        for h2 in range(2):
            tr(pk[zi][:, h2, :], zb[zi][:, h2 * 128:(h2 + 1) * 128], ident_b[:, :],
               ['zb%d' % zi, 'ident_b'], ['pk%d' % zi], inc=(h2 == 1))
        P.op('act', lambda e, zi=zi: e.copy(ktst[zi][:, :, :], pk[zi][:, :, :]), ['pk%d' % zi], ['ktst%d' % zi])
        P.dma('pool', lambda e, zi=zi, hh=hh: e.dma_start(out=QTs[hh, :, :], in_=ktst[zi][:, 0, :]), ['ktst%d' % zi], ['QKTs'])
        P.dma('pool', lambda e, zi=zi, hh=hh: e.dma_start(out=KTs[hh, :, :], in_=ktst[zi][:, 1, :]), ['ktst%d' % zi], ['QKTs'])
        wb = load_w(w_gla, hh * 768 + 256, 256)
        z = proj_tok(wb, 256, tS)
        zi = nextz()
        P.op('act', lambda e, zi=zi, z=z: e.copy(zb[zi][:, :], pz[z][:, 0:256]), ['pz%d' % z], ['zb%d' % zi])
        P.dma('pool', lambda e, zi=zi, hh=hh: e.dma_start(out=Vgs[:, hh * DVB:(hh + 1) * DVB], in_=zb[zi][:, :]),
              ['zb%d' % zi], ['Vgs'])
        wb = load_w(w_gla, hh * 768 + 512, 256)
        z = proj_tok(wb, 256, tS)
        zi = nextz()
        P.op('act', lambda e, zi=zi, z=z: e.activation(out=zb[zi][:, :], in_=pz[z][:, 0:256], func=AF.Silu),
             ['pz%d' % z], ['zb%d' % zi])
        P.dma('pool', lambda e, zi=zi, hh=hh: e.dma_start(out=Ggs[:, hh * DVB:(hh + 1) * DVB], in_=zb[zi][:, :]),
              ['zb%d' % zi], ['Ggs'])
    wb = load_w(w_gla, HB * 768, 16)
    z = cnt['z'] % 2
    cnt['z'] += 1
    for kc in range(KC):
        mm(pz[z][0:16, 0:128], wbf[wb][:, kc, 0:16], xT[:, kc, tS * 128:(tS + 1) * 128],
           kc == 0, kc == KC - 1, ['xT', 'wbf%d' % wb], ['pz%d' % z], inc=(kc == KC - 1))
    P.op('act', lambda e, z=z: e.copy(abT1s[0:16, :], pz[z][0:16, 0:128]), ['pz%d' % z], ['abT1s'])
    P.barrier()
    ep.close()
    if stage < 2:
        P.barrier(final=True)
        P.emit()
        return nc

    ea = ExitStack()
    scores = sb(ea, "scores", [128, STOT], F32)
    lo = sb(ea, "lo", [128, NOWN], F32)
    mid = sb(ea, "mid", [128, NOWN], F32)
    cntt = sb(ea, "cntt", [128, NOWN], F32)
    gw = sb(ea, "gw", [128, NOWN], F32)
    cjunk = sb(ea, "cjunk", [128, S], BF16)
    e2 = ExitStack()
    diag = sb(e2, "diag", [128, HIDX, 128], BF16)
    Rb = [sb(e2, "Rb%d" % i, [128, 512], BF16) for i in range(4)]
    mk = sb(e2, "mk", [128, 512], F32)
    pd = [ps(e2, "pd%d" % i, [128, 512], F32) for i in range(4)]
    pi = [ps(e2, "pi%d" % i, [128, 512], F32) for i in range(2)]
    ci = 0
    for k in range(NOWN):
        for h in range(HIDX):
            P.op('dve', lambda e, h=h, k=k: e.tensor_scalar(out=diag[:, h, :], in0=ident_b[:, :],
                                                            scalar1=wsgn[:, k, h:h + 1], scalar2=None, op0=ALU.mult),
                 ['ident_b', 'wsgn'], ['diag'])
        P.dma('sp', lambda e, k=k: e.dma_start(out=mk[:, :], in_=mask_o[k, :, :]), [], ['mk'])
        nchk = LK[k] // 512
        for c in range(nchk):
            pib = ci % 2
            ci += 1
            for m in range(8):
                a, b = 2 * (m % 2), 2 * (m % 2) + 1
                mm(pd[a][:, :], qiT[0:64, m, k * 128:(k + 1) * 128], kiT2[0:64, c * 512:(c + 1) * 512],
                   True, True, ['qiT', 'kiT2'], ['pd%d' % a])
                mm(pd[b][:, :], qiT[64:128, m, k * 128:(k + 1) * 128], kiT2[64:128, c * 512:(c + 1) * 512],
                   True, True, ['qiT', 'kiT2'], ['pd%d' % b])
                P.op('act', lambda e, a=a: e.activation(out=Rb[a][:, :], in_=pd[a][:, :], func=AF.Relu),
                     ['pd%d' % a], ['Rb%d' % a])
                P.op('dve', lambda e, b=b: e.tensor_scalar(out=Rb[b][:, :], in0=pd[b][:, :], scalar1=0.0,
                                                           scalar2=None, op0=ALU.max),
                     ['pd%d' % b], ['Rb%d' % b])
                mm(pi[pib][:, :], diag[:, 2 * m, :], Rb[a][:, :], m == 0, False,
                   ['diag', 'Rb%d' % a], ['pi%d' % pib], inc=False)
                mm(pi[pib][:, :], diag[:, 2 * m + 1, :], Rb[b][:, :], False, m == 7,
                   ['diag', 'Rb%d' % b], ['pi%d' % pib], inc=True)
            dst = scores[:, SOFF[k] + c * 512:SOFF[k] + (c + 1) * 512]
            if c == nchk - 1:
                P.op('dve', lambda e, dst=dst, pib=pib: e.tensor_tensor(out=dst, in0=pi[pib][:, :], in1=mk[:, :],
                                                                        op=ALU.add),
                     ['pi%d' % pib, 'mk'], ['scores'])
            else:
                P.op('act', lambda e, dst=dst, pib=pib: e.copy(dst, pi[pib][:, :]), ['pi%d' % pib], ['scores'])
    P.barrier()
    e2.close()

    W0 = 64.0
    P.op('dve', lambda e: e.memset(lo[:, :], -W0), [], ['lo'])
    for it in range(NBIS):
        w = W0 / (2 ** it)
        P.op('dve', lambda e, w=w: e.tensor_scalar(out=mid[:, :], in0=lo[:, :], scalar1=w, scalar2=None, op0=ALU.add),
             ['lo'], ['mid'])
        for k in range(NOWN):
            P.op('dve', lambda e, k=k: e.tensor_scalar(
                out=cjunk[:, 0:LK[k]], in0=scores[:, SOFF[k]:SOFF[k] + LK[k]], scalar1=mid[:, k:k + 1], scalar2=0.0,
                op0=ALU.is_ge, op1=ALU.add, accum_out=cntt[:, k:k + 1]), ['scores', 'mid'], ['cjunk', 'cntt'])
        P.op('dve', lambda e, w=w: e.tensor_scalar(out=gw[:, :], in0=cntt[:, :], scalar1=TOPK - 0.5, scalar2=w,
                                                   op0=ALU.is_ge, op1=ALU.mult), ['cntt'], ['gw'])
        P.op('dve', lambda e: e.tensor_tensor(out=lo[:, :], in0=lo[:, :], in1=gw[:, :], op=ALU.add),
             ['gw', 'lo'], ['lo'])

    e4 = ExitStack()
    KTb = [sb(e4, "KTb%d" % i, [128, HA, 512], BF16) for i in range(2)]
    Vb = [sb(e4, "Vb%d" % i, [128, 4, HA, 132], BF16) for i in range(2)]
    mkb = sb(e4, "mkb", [128, 512], BF16)
    mT = [sb(e4, "mT%d" % i, [128, 4, 128], BF16) for i in range(2)]
    Pe = [sb(e4, "Pe%d" % i, [128, 4, 128], BF16) for i in range(2)]
    Pm = [sb(e4, "Pm%d" % i, [128, 4, 128], BF16) for i in range(2)]
    den = sb(e4, "den", [128, HA], F32)
    rec = sb(e4, "rec", [128, HA], F32)
    pS = [ps(e4, "pS%d" % i, [128, 4, 128], F32) for i in range(2)]
    pOf = [ps(e4, "pO%d" % i, [128, 512], F32) for i in range(3)]
    pO = [t[:, 0:396].rearrange("p (a b) -> p a b", a=3) for t in pOf]
    pMf = ps(e4, "pM", [128, 1024], BF16)
    pM = pMf[:, 0:512].rearrange("p (a b) -> p a b", a=4)
    for i in range(2):
        P.op('dve', lambda e, i=i: e.memset(Vb[i][:, :, :, 128:129], 1.0), [], ['Vb%d' % i])
    zeroL = sb(e4, "zeroL", [128, 128], BF16)
    zeroR = sb(e4, "zeroR", [128, 512], BF16)
    P.op('dve', lambda e: e.memset(zeroL[:, :], 0.0), [], ['zeroL'])
    P.op('dve', lambda e: e.memset(zeroR[:, :], 0.0), [], ['zeroR'])
    ld = 0
    for k in range(NOWN):
        nq = LK[k] // 512
        nsb = LK[k] // 128
        for i in range(3):
            mm(pOf[i][:, 0:396], zeroL[:, :], zeroR[:, 0:396], True, False, ['zeroL', 'zeroR'], ['pO'], inc=(i == 2))
        for q4 in range(nq):
            bi = ld % 2
            ld += 1
            P.dma('sp', lambda e, bi=bi, q4=q4: e.dma_start(
                out=KTb[bi][:, :, :], in_=KT[:, :, q4 * 512:(q4 + 1) * 512].rearrange("h d s -> d h s")),
                ['KT'], ['KTb%d' % bi])
            for j4 in range(4):
                P.dma('sp', lambda e, bi=bi, q4=q4, j4=j4: e.dma_start(
                    out=Vb[bi][:, j4, :, 0:128],
                    in_=Vs[q4 * 512 + j4 * 128:q4 * 512 + (j4 + 1) * 128, :].rearrange("p (h d) -> p h d", h=HA)),
                    ['Vs'], ['Vb%d' % bi])
            P.op('dve', lambda e, k=k, q4=q4: e.tensor_scalar(
                out=mkb[:, :], in0=scores[:, SOFF[k] + q4 * 512:SOFF[k] + (q4 + 1) * 512],
                scalar1=lo[:, k:k + 1], scalar2=None, op0=ALU.is_ge), ['scores', 'lo'], ['mkb'])
            for j4 in range(4):
                tr(pM[:, j4, :], mkb[:, j4 * 128:(j4 + 1) * 128], ident_b[:, :], ['mkb', 'ident_b'], ['pM'], inc=(j4 == 3))
            P.op('act', lambda e, bi=bi: e.copy(mT[bi][:, :, :], pM[:, :, :]), ['pM'], ['mT%d' % bi])
            for j4 in range(4):
                sbi = q4 * 4 + j4
                for hg in range(2):
                    for h4 in range(4):
                        h = hg * 4 + h4
                        mm(pS[hg][:, h4, :], KTb[bi][:, h, j4 * 128:(j4 + 1) * 128], qaT[:, h, k * 128:(k + 1) * 128],
                           True, True, ['KTb%d' % bi, 'qaT'], ['pS%d' % hg], inc=(h4 == 3))
                    P.op('act', lambda e, hg=hg: e.activation(out=Pe[hg][:, :, :], in_=pS[hg][:, :, :], func=AF.Exp,
                                                              scale=DH ** -0.5), ['pS%d' % hg], ['Pe%d' % hg])
                    P.op('dve', lambda e, hg=hg, bi=bi, j4=j4: e.tensor_tensor(
                        out=Pm[hg][:, :, :], in0=Pe[hg][:, :, :],
                        in1=mT[bi][:, j4:j4 + 1, :].to_broadcast([128, 4, 128]), op=ALU.mult),
                        ['Pe%d' % hg, 'mT%d' % bi], ['Pm%d' % hg])
                    for h4 in range(4):
                        h = hg * 4 + h4
                        mm(pO[h // 3][:, h % 3, 0:129], Pm[hg][:, h4, :], Vb[bi][:, j4, h, 0:129],
                           False, sbi == nsb - 1, ['Pm%d' % hg, 'Vb%d' % bi], ['pO'], inc=(h4 == 3))
        for h in range(HA):
            P.op('act', lambda e, h=h: e.copy(den[:, h:h + 1], pO[h // 3][:, h % 3, 128:129]), ['pO'], ['den'])
        P.op('dve', lambda e: e.reciprocal(rec[:, :], den[:, :]), ['den'], ['rec'])
        for h in range(HA):
            P.op('dve', lambda e, h=h, k=k: e.scalar_tensor_tensor(
                out=gaS[:, k, h * 128:(h + 1) * 128], in0=pO[h // 3][:, h % 3, 0:128], scalar=rec[:, h:h + 1],
                in1=gaS[:, k, h * 128:(h + 1) * 128], op0=ALU.mult, op1=ALU.mult), ['pO', 'rec', 'gaS'], ['gaS'])
    if stage == 4:
        dbg_a = dout("dbg_a", [NO, 1024], BF16)
        dbg_lo = dout("dbg_lo", [128, NOWN])
        dbg_sc = dout("dbg_sc", [128, STOT])
        dbg_den = dout("dbg_den", [128, HA])
        P.dma('sp', lambda e: e.dma_start(out=dbg_sc[:, :], in_=scores[:, :]), ['scores'], ['dbg_sc'])
        P.dma('sp', lambda e: e.dma_start(out=dbg_den[:, :], in_=den[:, :]), ['den'], ['dbg_den'])
        P.dma('sp', lambda e: e.dma_start(out=dbg_a.rearrange("(n p) f -> p n f", p=128), in_=gaS[:, :, :]), ['gaS'], ['dbg_a'])
        P.dma('sp', lambda e: e.dma_start(out=dbg_lo[:, :], in_=lo[:, :]), ['lo'], ['dbg_lo'])
        P.barrier(final=True)
        P.emit()
        return nc
    P.dma('sp', lambda e: e.dma_start(out=gaD.rearrange("(n p) f -> p n f", p=128), in_=gaS[:, :, :]), ['gaS'], ['gaD'])
    P.barrier()
    e4.close()
    ea.close()
    es1.close()

    eg = ExitStack()
    CH = 64
    wgb_f = sb(eg, "wgb_f", [32, HB * DKB], F32)
    wgb_b = sb(eg, "wgb_b", [32, HB * DKB], BF16)
    qT_g = sb(eg, "qT_g", [128, S], BF16)
    kT_g = sb(eg, "kT_g", [128, S], BF16)
    laT = sb(eg, "laT", [128, S], F32)
    cumT = sb(eg, "cumT", [128, S], F32)
    Et = sb(eg, "Et", [128, S], F32)
    flagT = sb(eg, "flagT", [128, S], F32)
    qdT = sb(eg, "qdT", [128, S], BF16)
    kdT = sb(eg, "kdT", [128, S], BF16)
    klT = sb(eg, "klT", [128, S], BF16)
    elast = sb(eg, "elast", [128, NCH], F32)
    kl = sb(eg, "kl", [64, NCH, 128], BF16)
    vch = sb(eg, "vch", [64, NCH, DVB], BF16)
    attT = sb(eg, "attT", [64, NCH, 64], BF16)
    Sf = sb(eg, "Sf", [128, DVB], F32)
    Sb = sb(eg, "Sb", [128, DVB], BF16)
    ob = [sb(eg, "ob%d" % i, [64, 8, DVB], F32) for i in range(2)]
    osq = sb(eg, "osq", [64, 8, DVB], F32)
    gch = sb(eg, "gch", [64, 8, DVB], BF16)
    bpb = sb(eg, "bpb", [64, 8, DVB], BF16)
    gn = sb(eg, "gn", [64, DVB], F32)
    sq8 = sb(eg, "sq8", [64, 8, 3], F32)
    pzg = ps(eg, "pzg", [128, 512], F32)
    pTg = ps(eg, "pTg", [128, 1024], BF16)
    pA = ps(eg, "pA", [128, 512], F32)
    pU = [ps(eg, "pU%d" % i, [128, 512], F32) for i in range(2)]
    pOg = [ps(eg, "pOg%d" % i, [128, 512], F32) for i in range(2)]

    P.dma('sp', lambda e: e.dma_start(out=wgb_f[:, :], in_=wgb[:, :]), [], ['wgb_f'])
    P.op('dve', lambda e: e.tensor_copy(wgb_b[:, :], wgb_f[:, :]), ['wgb_f'], ['wgb_b'])
    P.dma('sp', lambda e: e.dma_start(out=gn[:, :], in_=gla_g[0:1, :].partition_broadcast(64)), [], ['gn'])
    P.op('dve', lambda e: e.memset(flagT[:, :], 1.0), [], ['flagT'])
    P.op('dve', lambda e: e.memset(flagT[:, :].rearrange("p (n c) -> p n c", c=CH)[:, :, 0:1], 0.0), [], ['flagT'])
    for hh in range(HB):
        P.dma('sp', lambda e, hh=hh: e.dma_start(out=qT_g[:, :], in_=QTg[hh, :, :]), ['QKTg'], ['qkT_g'])
        P.dma('sp', lambda e, hh=hh: e.dma_start(out=kT_g[:, :], in_=KTg[hh, :, :]), ['QKTg'], ['qkT_g'])
        P.dma('sp', lambda e, hh=hh: e.dma_start(out=vch[:, :, :], in_=Vg[:, hh * DVB:(hh + 1) * DVB].rearrange("(n c) v -> c n v", c=CH)), ['Vg'], ['vch'])
        for c4 in range(S // 512):
            mm(pzg[:, :], wgb_b[:, hh * DKB:(hh + 1) * DKB], abT1[:, c4 * 512:(c4 + 1) * 512], True, True, ['wgb_b', 'abT1'], ['pzg'])
            P.op('act', lambda e, c4=c4: e.activation(out=Et[:, c4 * 512:(c4 + 1) * 512], in_=pzg[:, :], func=AF.Sigmoid),
                 ['pzg'], ['Et'])
        P.op('act', lambda e: e.activation(out=laT[:, :], in_=Et[:, :], func=AF.Ln), ['Et'], ['laT'])
        P.op('dve', lambda e: e.tensor_scalar(out=laT[:, :], in0=laT[:, :], scalar1=1.0 / 16.0, scalar2=None, op0=ALU.mult),
             ['laT'], ['laT'])
        P.op('dve', lambda e: e.tensor_tensor_scan(out=cumT[:, :], data0=flagT[:, :], data1=laT[:, :], initial=0.0,
                                                    op0=ALU.mult, op1=ALU.add), ['flagT', 'laT'], ['cumT'])
        cum3 = cumT[:, :].rearrange("p (n c) -> p n c", c=CH)
        P.op('act', lambda e: e.activation(out=Et[:, :], in_=cumT[:, :], func=AF.Exp), ['cumT'], ['Et'])
        P.op('dve', lambda e: e.tensor_tensor(out=qdT[:, :], in0=qT_g[:, :], in1=Et[:, :], op=ALU.mult),
             ['qkT_g', 'Et'], ['qdT'])
        P.op('act', lambda e: e.activation(out=Et[:, :], in_=cumT[:, :], func=AF.Exp, scale=-1.0), ['cumT', 'qdT'], ['Et'])
        P.op('dve', lambda e: e.tensor_tensor(out=kdT[:, :], in0=kT_g[:, :], in1=Et[:, :], op=ALU.mult),
             ['qkT_g', 'Et'], ['kdT'])
        P.op('dve', lambda e: e.tensor_tensor(out=Et[:, :].rearrange("p (n c) -> p n c", c=CH),
                                              in0=cum3[:, :, CH - 1:CH].to_broadcast([128, NCH, CH]), in1=cum3,
                                              op=ALU.subtract), ['cumT', 'kdT'], ['Et'])
        P.op('act', lambda e: e.activation(out=Et[:, :], in_=Et[:, :], func=AF.Exp), ['Et'], ['Et'])
        P.op('dve', lambda e: e.tensor_tensor(out=klT[:, :], in0=kT_g[:, :], in1=Et[:, :], op=ALU.mult),
             ['qkT_g', 'Et'], ['klT'])
        P.op('act', lambda e: e.activation(out=elast[:, :].rearrange("p (n o) -> p n o", o=1), in_=cum3[:, :, CH - 1:CH],
                                           func=AF.Exp), ['cumT'], ['elast'])
        pT3 = pTg[0:64, :].rearrange("p (a b) -> p a b", a=8)
        for n8 in range(NCH // 8):
            for i in range(8):
                n = n8 * 8 + i
                tr(pT3[:, i, :], klT[:, n * CH:(n + 1) * CH], ident_b[:, :], ['klT', 'ident_b'], ['pTg'], inc=(i == 7))
            P.op('act', lambda e, n8=n8: e.copy(kl[:, n8 * 8:(n8 + 1) * 8, :], pT3[:, :, :]), ['pTg'], ['kl'])
        pA3 = pA[0:64, :].rearrange("p (a b) -> p a b", a=8)
        for n8 in range(NCH // 8):
            for i in range(8):
                n = n8 * 8 + i
                mm(pA3[:, i, :], kdT[:, n * CH:(n + 1) * CH], qdT[:, n * CH:(n + 1) * CH], True, True,
                   ['kdT', 'qdT'], ['pA'], inc=(i == 7))
            P.op('dve', lambda e, n8=n8: e.tensor_tensor(out=attT[:, n8 * 8:(n8 + 1) * 8, :], in0=pA3[:, :, :],
                                                         in1=tri_b[:, None, :].to_broadcast([64, 8, 64]), op=ALU.mult),
                 ['pA', 'tri'], ['attT'])
        for n in range(NCH):
            u = n % 2
            mm(pU[u][:, 0:DVB], kl[:, n, :], vch[:, n, :], True, True, ['kl', 'vch'], ['pU%d' % u])
            mm(pOg[u][0:64, 0:DVB], attT[:, n, :], vch[:, n, :], True, n == 0, ['attT', 'vch'], ['pOg%d' % u],
               inc=(n == 0))
            if n > 0:
                mm(pOg[u][0:64, 0:DVB], qdT[:, n * CH:(n + 1) * CH], Sb[:, :], False, True, ['qdT', 'Sb'], ['pOg%d' % u])
                P.op('dve', lambda e, n=n, u=u: e.scalar_tensor_tensor(out=Sf[:, :], in0=Sf[:, :], scalar=elast[:, n:n + 1],
                                                                       in1=pU[u][:, 0:DVB], op0=ALU.mult, op1=ALU.add),
                     ['Sf', 'elast', 'pU%d' % u], ['Sf'])
            else:
                P.op('dve', lambda e, u=u: e.tensor_copy(Sf[:, :], pU[u][:, 0:DVB]), ['pU%d' % u], ['Sf'])
            if n < NCH - 1:
                P.op('act', lambda e: e.copy(Sb[:, :], Sf[:, :]), ['Sf'], ['Sb'])
            bsel = (n // 8) % 2
            P.op('act', lambda e, n=n, u=u, bsel=bsel: e.copy(ob[bsel][:, n % 8, :], pOg[u][0:64, 0:DVB]),
                 ['pOg%d' % u], ['ob%d' % bsel])
            if n % 8 == 7:
                n0 = n - 7
                o8 = ob[bsel]
                P.dma('sp', lambda e, n0=n0, hh=hh: e.dma_start(
                    out=gch[:, :, :], in_=Gg[n0 * CH:(n0 + 8) * CH, hh * DVB:(hh + 1) * DVB].rearrange("(n c) v -> c n v", c=CH)), ['Gg'], ['gch'])
                P.op('dve', lambda e, o8=o8: e.tensor_tensor(out=osq[:, :, :], in0=o8[:, :, :], in1=o8[:, :, :], op=ALU.mult),
                     ['ob%d' % bsel], ['osq'])
                P.op('dve', lambda e: e.tensor_reduce(out=sq8[:, :, 0], in_=osq[:, :, :], axis=AX.X, op=ALU.add),
                     ['osq'], ['sq8a'])
                P.op('act', lambda e: e.activation(out=sq8[:, :, 1], in_=sq8[:, :, 0], func=AF.Sqrt, bias=eps_t[0:64, 0:1],
                                                   scale=1.0 / DVB), ['sq8a', 'eps'], ['sq8b'])
                P.op('dve', lambda e: e.reciprocal(sq8[:, :, 2], sq8[:, :, 1]), ['sq8b'], ['sq8c'])
                P.op('dve', lambda e, o8=o8: e.tensor_tensor(out=osq[:, :, :], in0=o8[:, :, :],
                                                             in1=sq8[:, :, 2:3].to_broadcast([64, 8, DVB]), op=ALU.mult),
                     ['ob%d' % bsel, 'sq8c'], ['osq'])
                P.op('dve', lambda e: e.tensor_tensor(out=osq[:, :, :], in0=osq[:, :, :],
                                                      in1=gn[:, None, :].to_broadcast([64, 8, DVB]), op=ALU.mult),
                     ['osq', 'gn'], ['osq'])
                P.op('dve', lambda e: e.tensor_tensor(out=bpb[:, :, :], in0=osq[:, :, :], in1=gch[:, :, :], op=ALU.mult),
                     ['osq', 'gch'], ['bpb'])
                P.dma('sp', lambda e, n0=n0, hh=hh: e.dma_start(
                    out=bp_loc[n0 * CH:(n0 + 8) * CH, hh * DVB:(hh + 1) * DVB].rearrange("(n c) v -> c n v", c=CH), in_=bpb[:, :, :]),
                    ['bpb'], ['bp_loc'])
        P.dma('sp', lambda e, hh=hh: e.dma_start(out=st_p[hh, :, :], in_=Sf[:, :]), ['Sf'], ['st_p'])
    if stage == 5:
        dbg_b = dout("dbg_b", [S, HB * DVB], BF16)
        P.dma('sp', lambda e: e.dma_start(out=dbg_b[:, :], in_=bp_loc[:, :]), ['bp_loc'], ['dbg_b'])
        P.barrier(final=True)
        P.emit()
        return nc
    P.barrier()
    eg.close()

    egs = ExitStack()
    wgbS_f = sb(egs, "wgbS_f", [32, HB * DKB], F32)
    wgbS_b = sb(egs, "wgbS_b", [32, HB * DKB], BF16)
    gnS = sb(egs, "gnS", [64, DVB], F32)
    pzS = ps(egs, "pzS", [128, 512], F32)
    pTS = ps(egs, "pTS", [128, 1024], BF16)
    pAS = ps(egs, "pAS", [128, 512], F32)
    pUS = [ps(egs, "pUS%d" % i, [128, 512], F32) for i in range(2)]
    pOS = [ps(egs, "pOS%d" % i, [128, 512], F32) for i in range(2)]
    P.dma('sp', lambda e: e.dma_start(out=wgbS_f[:, :], in_=wgb[:, :]), [], ['wgbS_f'])
    P.op('dve', lambda e: e.tensor_copy(wgbS_b[:, :], wgbS_f[:, :]), ['wgbS_f'], ['wgbS_b'])
    P.dma('sp', lambda e: e.dma_start(out=gnS[:, :], in_=gla_g[0:1, :].partition_broadcast(64)), [], ['gnS'])
    CS, NSQ, TS = 8, 4, 32
    qTs = sb(egs, "qTs", [128, TS], BF16)
    kTs = sb(egs, "kTs", [128, TS], BF16)
    vs = sb(egs, "vs", [CS, NSQ, DVB], BF16)
    gs = sb(egs, "gs", [CS, NSQ, DVB], BF16)
    laS = sb(egs, "laS", [128, TS], F32)
    cumS = sb(egs, "cumS", [128, TS], F32)
    EtS = sb(egs, "EtS", [128, TS], F32)
    flagS = sb(egs, "flagS", [128, TS], F32)
    qdS = sb(egs, "qdS", [128, TS], BF16)
    kdS = sb(egs, "kdS", [128, TS], BF16)
    klS = sb(egs, "klS", [128, TS], BF16)
    elS = sb(egs, "elS", [128, NSQ], F32)
    klSt = sb(egs, "klSt", [CS, NSQ, 128], BF16)
    attS = sb(egs, "attS", [CS, NSQ, CS], BF16)
    S0f = [sb(egs, "S0f%d" % i, [128, DVB], F32) for i in range(2)]
    S0b = [sb(egs, "S0b%d" % i, [128, DVB], BF16) for i in range(2)]
    SfS = [sb(egs, "SfS%d" % i, [128, DVB], F32) for i in range(2)]
    obS = sb(egs, "obS", [CS, NSQ, DVB], F32)
    oqS = sb(egs, "oqS", [CS, NSQ, DVB], F32)
    bpS = sb(egs, "bpS", [CS, NSQ, DVB], BF16)
    sqS = sb(egs, "sqS", [CS, NSQ, 3], F32)
    cum3s = cumS[:, :].rearrange("p (n c) -> p n c", c=CS)
    pT3s = pTS[0:CS, 0:NSQ * 128].rearrange("p (a b) -> p a b", a=NSQ)
    pA3s = pAS[0:CS, 0:NSQ * CS].rearrange("p (a b) -> p a b", a=NSQ)
    P.op('dve', lambda e: e.memset(flagS[:, :], 1.0), [], ['flagS'])
    P.op('dve', lambda e: e.memset(flagS[:, :].rearrange("p (n c) -> p n c", c=CS)[:, :, 0:1], 0.0), [], ['flagS'])
    si = 0
    for hh in range(HB):
        P.dma('sp', lambda e, hh=hh: e.dma_start(out=qTs[:, :], in_=QTs[hh, :, 0:TS]), ['QKTs'], ['qTs'])
        P.dma('sp', lambda e, hh=hh: e.dma_start(out=kTs[:, :], in_=KTs[hh, :, 0:TS]), ['QKTs'], ['kTs'])
        P.dma('sp', lambda e, hh=hh: e.dma_start(
            out=vs[:, :, :], in_=Vgs[0:TS, hh * DVB:(hh + 1) * DVB].rearrange("(n c) v -> c n v", c=CS)), ['Vgs'], ['vs'])
        P.dma('sp', lambda e, hh=hh: e.dma_start(
            out=gs[:, :, :], in_=Ggs[0:TS, hh * DVB:(hh + 1) * DVB].rearrange("(n c) v -> c n v", c=CS)), ['Ggs'], ['gs'])
        mm(pzS[:, 0:TS], wgbS_b[:, hh * DKB:(hh + 1) * DKB], abT1s[:, 0:TS], True, True, ['wgbS_b', 'abT1s'], ['pzS'])
        P.op('act', lambda e: e.activation(out=cumS[:, :], in_=pzS[:, 0:TS], func=AF.Sigmoid), ['pzS'], ['cumS'])
        P.op('act', lambda e: e.activation(out=laS[:, :], in_=cumS[:, :], func=AF.Ln), ['cumS'], ['laS'])
        P.op('dve', lambda e: e.tensor_scalar(out=laS[:, :], in0=laS[:, :], scalar1=1.0 / 16.0, scalar2=None, op0=ALU.mult),
             ['laS'], ['laS'])
        P.op('dve', lambda e: e.tensor_tensor_scan(out=cumS[:, :], data0=flagS[:, :], data1=laS[:, :], initial=0.0,
                                                    op0=ALU.mult, op1=ALU.add), ['flagS', 'laS'], ['cumS'])
        P.op('act', lambda e: e.activation(out=EtS[:, :], in_=cumS[:, :], func=AF.Exp), ['cumS'], ['EtS'])
        P.op('dve', lambda e: e.tensor_tensor(out=qdS[:, :], in0=qTs[:, :], in1=EtS[:, :], op=ALU.mult), ['qTs', 'EtS'], ['qdS'])
        P.op('act', lambda e: e.activation(out=EtS[:, :], in_=cumS[:, :], func=AF.Exp, scale=-1.0), ['cumS'], ['EtS'])
        P.op('dve', lambda e: e.tensor_tensor(out=kdS[:, :], in0=kTs[:, :], in1=EtS[:, :], op=ALU.mult), ['kTs', 'EtS'], ['kdS'])
        P.op('dve', lambda e: e.tensor_tensor(out=EtS[:, :].rearrange("p (n c) -> p n c", c=CS),
                                              in0=cum3s[:, :, CS - 1:CS].to_broadcast([128, NSQ, CS]), in1=cum3s,
                                              op=ALU.subtract), ['cumS'], ['EtS'])
        P.op('act', lambda e: e.activation(out=EtS[:, :], in_=EtS[:, :], func=AF.Exp), ['EtS'], ['EtS'])
        P.op('dve', lambda e: e.tensor_tensor(out=klS[:, :], in0=kTs[:, :], in1=EtS[:, :], op=ALU.mult), ['kTs', 'EtS'], ['klS'])
        P.op('act', lambda e: e.activation(out=elS[:, :].rearrange("p (n o) -> p n o", o=1), in_=cum3s[:, :, CS - 1:CS],
                                           func=AF.Exp), ['cumS'], ['elS'])
        for n in range(NSQ):
            tr(pT3s[:, n, :], klS[:, n * CS:(n + 1) * CS], ident_b[:, :], ['klS', 'ident_b'], ['pTS'], inc=(n == NSQ - 1))
        P.op('act', lambda e: e.copy(klSt[:, :, :], pT3s[:, :, :]), ['pTS'], ['klSt'])
        for n in range(NSQ):
            mm(pA3s[:, n, :], kdS[:, n * CS:(n + 1) * CS], qdS[:, n * CS:(n + 1) * CS], True, True,
               ['kdS', 'qdS'], ['pAS'], inc=(n == NSQ - 1))
        P.op('dve', lambda e: e.tensor_tensor(out=attS[:, :, :], in0=pA3s[:, :, :],
                                              in1=tri_b[0:CS, None, 0:CS].to_broadcast([CS, NSQ, CS]), op=ALU.mult),
             ['pAS', 'tri'], ['attS'])
        for n in range(NSQ):
            u = si % 2
            si += 1
            P.dma('sp', lambda e, n=n, hh=hh, u=u: e.dma_start(out=S0f[u][:, :], in_=st_s_in[n, hh, :, :]), [], ['S0f%d' % u])
            P.op('act', lambda e, u=u: e.copy(S0b[u][:, :], S0f[u][:, :]), ['S0f%d' % u], ['S0b%d' % u])
            mm(pUS[u][:, 0:DVB], klSt[:, n, :], vs[:, n, :], True, True, ['klSt', 'vs'], ['pUS%d' % u])
            mm(pOS[u][0:CS, 0:DVB], attS[:, n, :], vs[:, n, :], True, False, ['attS', 'vs'], ['pOS%d' % u], inc=False)
            mm(pOS[u][0:CS, 0:DVB], qdS[:, n * CS:(n + 1) * CS], S0b[u][:, :], False, True, ['qdS', 'S0b%d' % u], ['pOS%d' % u])
            P.op('dve', lambda e, n=n, u=u: e.scalar_tensor_tensor(out=SfS[u][:, :], in0=S0f[u][:, :], scalar=elS[:, n:n + 1],
                                                                   in1=pUS[u][:, 0:DVB], op0=ALU.mult, op1=ALU.add),
                 ['S0f%d' % u, 'elS', 'pUS%d' % u], ['SfS%d' % u])
            P.dma('sp', lambda e, n=n, hh=hh, u=u: e.dma_start(out=st_s[n, hh, :, :], in_=SfS[u][:, :]), ['SfS%d' % u], ['st_s'])
            P.op('act', lambda e, n=n, u=u: e.copy(obS[:, n, :], pOS[u][0:CS, 0:DVB]), ['pOS%d' % u], ['obS'])
        P.op('dve', lambda e: e.tensor_tensor(out=oqS[:, :, :], in0=obS[:, :, :], in1=obS[:, :, :], op=ALU.mult), ['obS'], ['oqS'])
        P.op('dve', lambda e: e.tensor_reduce(out=sqS[:, :, 0], in_=oqS[:, :, :], axis=AX.X, op=ALU.add), ['oqS'], ['sqSa'])
        P.op('act', lambda e: e.activation(out=sqS[:, :, 1], in_=sqS[:, :, 0], func=AF.Sqrt, bias=eps_t[0:CS, 0:1],
                                           scale=1.0 / DVB), ['sqSa', 'eps'], ['sqSb'])
        P.op('dve', lambda e: e.reciprocal(sqS[:, :, 2], sqS[:, :, 1]), ['sqSb'], ['sqSc'])
        P.op('dve', lambda e: e.tensor_tensor(out=oqS[:, :, :], in0=obS[:, :, :],
                                              in1=sqS[:, :, 2:3].to_broadcast([CS, NSQ, DVB]), op=ALU.mult),
             ['obS', 'sqSc'], ['oqS'])
        P.op('dve', lambda e: e.tensor_tensor(out=oqS[:, :, :], in0=oqS[:, :, :],
                                              in1=gnS[0:CS, None, :].to_broadcast([CS, NSQ, DVB]), op=ALU.mult),
             ['oqS', 'gnS'], ['oqS'])
        P.op('dve', lambda e: e.tensor_tensor(out=bpS[:, :, :], in0=oqS[:, :, :], in1=gs[:, :, :], op=ALU.mult),
             ['oqS', 'gs'], ['bpS'])
        P.dma('sp', lambda e, hh=hh: e.dma_start(
            out=bps[0:TS, hh * DVB:(hh + 1) * DVB].rearrange("(n c) v -> c n v", c=CS), in_=bpS[:, :, :]), ['bpS'], ['bps'])
    P.barrier()
    egs.close()

    if SDSA:
        ed = ExitStack()
        LB = NPG + 1
        LP = LB * 128
        widths = [512] * (LP // 512) + ([LP % 512] if LP % 512 else [])
        kiT2s = sb(ed, "kiT2s", [128, LP], BF16)
        scS = sb(ed, "scS", [128, LP], F32)
        cjS = sb(ed, "cjS", [128, LP], BF16)
        ptb = sb(ed, "ptb", [128, NPG], I32)
        idxi = sb(ed, "idxi", [128, NPG], I32)
        iot = sb(ed, "iot", [128, 1], F32)
        mkS = sb(ed, "mkS", [128, 128], F32)
        kig = [sb(ed, "kig%d" % i, [128, DIDX], F32) for i in range(2)]
        kib = [sb(ed, "kib%d" % i, [128, 128], BF16) for i in range(2)]
        kinb = sb(ed, "kinb", [128, DIDX], BF16)
        qtok = sb(ed, "qtok", [128, 1024], BF16)
        qiTs = sb(ed, "qiTs", [128, 8, 128], BF16)
        qaTs = sb(ed, "qaTs", [128, HA, 128], BF16)
        wsS = sb(ed, "wsS", [128, HIDX], F32)
        diagS = sb(ed, "diagS", [128, HIDX, 128], BF16)
        RbS = [sb(ed, "RbS%d" % i, [128, 512], BF16) for i in range(4)]
        loS = sb(ed, "loS", [128, 1], F32)
        midS = sb(ed, "midS", [128, 1], F32)
        cnS = sb(ed, "cnS", [128, 1], F32)
        gwS = sb(ed, "gwS", [128, 1], F32)
        kpg = [sb(ed, "kpg%d" % i, [128, HA * DH], F32) for i in range(2)]
        vpg = [sb(ed, "vpg%d" % i, [128, HA * DH], F32) for i in range(2)]
        kpb = [sb(ed, "kpb%d" % i, [128, HA * DH], BF16) for i in range(2)]
        KTbS = [sb(ed, "KTbS%d" % i, [128, HA, 128], BF16) for i in range(2)]
        VbS = [sb(ed, "VbS%d" % i, [128, HA, 132], BF16) for i in range(2)]
        mkbS = sb(ed, "mkbS", [128, 128], BF16)
        mTs = [sb(ed, "mTs%d" % i, [128, 128], BF16) for i in range(2)]
        PeS = [sb(ed, "PeS%d" % i, [128, 4, 128], BF16) for i in range(2)]
        PmS = [sb(ed, "PmS%d" % i, [128, 4, 128], BF16) for i in range(2)]
        denS = sb(ed, "denS", [128, HA], F32)
        recS = sb(ed, "recS", [128, HA], F32)
        gaT = sb(ed, "gaT", [128, 1024], BF16)
        zLs = sb(ed, "zLs", [128, 128], BF16)
        zRs = sb(ed, "zRs", [128, 512], BF16)
        bfA = ps(ed, "bfA", [128, 1024], BF16)
        bfB = ps(ed, "bfB", [128, 1024], BF16)
        Fb = [ps(ed, "Fb%d" % i, [128, 512], F32) for i in range(6)]
        pk8 = bfA[:, :].rearrange("p (a b) -> p a b", a=8)
        pKT8 = pk8
        pdS = Fb[0:4]
        piS = Fb[4:6]
        pS4 = [t[:, :].rearrange("p (a b) -> p a b", a=4) for t in Fb[0:2]]
        pOfs = Fb[2:5]
        pOs = [t[:, 0:396].rearrange("p (a b) -> p a b", a=3) for t in pOfs]
        pMs = bfB
        P.dma('sp', lambda e: e.dma_start(out=iot[:, :], in_=iota_p[:, :]), [], ['iot'])
        P.dma('sp', lambda e: e.dma_start(out=mkS[:, :], in_=mask_s[:, :]), [], ['mkS'])
        for tl, ky in ((kinb, 'kinb'), (qtok, 'qtok'), (wsS, 'wsS'), (gaT, 'gaT'), (zLs, 'zLs'), (zRs, 'zRs')):
            P.op('dve', lambda e, tl=tl: e.memset(tl[:, :], 0.0), [], [ky])
        for i in range(2):
            P.op('dve', lambda e, i=i: e.memset(kpb[i][:, :], 0.0), [], ['kpb%d' % i])
            P.op('dve', lambda e, i=i: e.memset(VbS[i][:, :, 0:128], 0.0), [], ['VbS%d' % i])
            P.op('dve', lambda e, i=i: e.memset(VbS[i][:, :, 128:129], 1.0), [], ['VbS%d' % i])
        for s in range(4):
            r0 = 8 * s
            P.dma('sp', lambda e, s=s: e.dma_start(out=ptb[:, :], in_=pt_s[s:s + 1, :].partition_broadcast(128)), [], ['ptb'])
            P.op('dve', lambda e: e.tensor_scalar(out=idxi[:, :], in0=ptb[:, :], scalar1=128.0, scalar2=iot[:, 0:1],
                                                  op0=ALU.mult, op1=ALU.add), ['ptb', 'iot'], ['idxi'])
            for srcD, dstT, ky in ((QiS, qiTs, 'qiTs'), (QaS, qaTs, 'qaTs')):
                P.dma('sp', lambda e, srcD=srcD, r0=r0: e.dma_start(out=qtok[0:8, :], in_=srcD[r0:r0 + 8, :]), [srcD is QiS and 'QiS' or 'QaS'], ['qtok'])
                for i in range(8):
                    tr(pk8[:, i, :], qtok[:, i * 128:(i + 1) * 128], ident_b[:, :], ['qtok', 'ident_b'], ['pkS'], inc=(i == 7))
                P.op('act', lambda e, dstT=dstT: e.copy(dstT[:, :, :], pk8[:, :, :]), ['pkS'], [ky])
            P.dma('sp', lambda e, r0=r0: e.dma_start(out=wsS[0:8, :], in_=WsS[r0:r0 + 8, :]), ['WsS'], ['wsS'])
            for h in range(HIDX):
                P.op('dve', lambda e, h=h: e.tensor_scalar(out=diagS[:, h, :], in0=ident_b[:, :], scalar1=wsS[:, h:h + 1],
                                                           scalar2=None, op0=ALU.mult), ['ident_b', 'wsS'], ['diagS'])
            for bl in range(LB):
                u = bl % 2
                if bl < NPG:
                    P.dma('pool', lambda e, u=u, bl=bl: e.indirect_dma_start(
                        out=kig[u][:, :], out_offset=None, in_=cik[:, :],
                        in_offset=bass.IndirectOffsetOnAxis(ap=idxi[:, bl:bl + 1], axis=0)), ['idxi'], ['kig%d' % u])
                    P.op('act', lambda e, u=u: e.copy(kib[u][:, 0:64], kig[u][:, :]), ['kig%d' % u], ['kib%d' % u])
                    P.op('act', lambda e, u=u: e.copy(kib[u][:, 64:128], kig[u][:, :]), ['kig%d' % u], ['kib%d' % u])
                else:
                    P.dma('sp', lambda e, r0=r0: e.dma_start(out=kinb[0:8, :], in_=KiN[r0:r0 + 8, :]), ['KiN'], ['kinb'])
                    P.op('act', lambda e, u=u: e.copy(kib[u][:, 0:64], kinb[:, :]), ['kinb'], ['kib%d' % u])
                    P.op('act', lambda e, u=u: e.copy(kib[u][:, 64:128], kinb[:, :]), ['kinb'], ['kib%d' % u])
                tr(pk8[:, 0, :], kib[u][:, :], ident_b[:, :], ['kib%d' % u, 'ident_b'], ['pkS'])
                P.op('act', lambda e, bl=bl: e.copy(kiT2s[:, bl * 128:(bl + 1) * 128], pk8[:, 0, :]), ['pkS'], ['kiT2s'])
            c0 = 0
            for ci_, wdt in enumerate(widths):
                pib = ci_ % 2
                for m in range(8):
                    a, b = 2 * (m % 2), 2 * (m % 2) + 1
                    mm(pdS[a][:, 0:wdt], qiTs[0:64, m, :], kiT2s[0:64, c0:c0 + wdt], True, True, ['qiTs', 'kiT2s'], ['pdS%d' % a])
                    mm(pdS[b][:, 0:wdt], qiTs[64:128, m, :], kiT2s[64:128, c0:c0 + wdt], True, True, ['qiTs', 'kiT2s'], ['pdS%d' % b])
                    P.op('act', lambda e, a=a, wdt=wdt: e.activation(out=RbS[a][:, 0:wdt], in_=pdS[a][:, 0:wdt], func=AF.Relu),
                         ['pdS%d' % a], ['RbS%d' % a])
                    P.op('dve', lambda e, b=b, wdt=wdt: e.tensor_scalar(out=RbS[b][:, 0:wdt], in0=pdS[b][:, 0:wdt], scalar1=0.0,
                                                                        scalar2=None, op0=ALU.max), ['pdS%d' % b], ['RbS%d' % b])
                    mm(piS[pib][:, 0:wdt], diagS[:, 2 * m, :], RbS[a][:, 0:wdt], m == 0, False, ['diagS', 'RbS%d' % a], ['piS%d' % pib], inc=False)
                    mm(piS[pib][:, 0:wdt], diagS[:, 2 * m + 1, :], RbS[b][:, 0:wdt], False, m == 7, ['diagS', 'RbS%d' % b], ['piS%d' % pib])
                P.op('act', lambda e, pib=pib, c0=c0, wdt=wdt: e.copy(scS[:, c0:c0 + wdt], piS[pib][:, 0:wdt]), ['piS%d' % pib], ['scS'])
                c0 += wdt
            P.op('dve', lambda e: e.tensor_tensor(out=scS[:, NPG * 128:LP], in0=scS[:, NPG * 128:LP], in1=mkS[:, :], op=ALU.add),
                 ['scS', 'mkS'], ['scS'])
            P.barrier()
            P.op('dve', lambda e: e.memset(loS[:, :], -64.0), [], ['loS'])
            for it in range(NBIS):
                w = 64.0 / (2 ** it)
                P.op('dve', lambda e, w=w: e.tensor_scalar(out=midS[:, :], in0=loS[:, :], scalar1=w, scalar2=None, op0=ALU.add),
                     ['loS'], ['midS'])
                P.op('dve', lambda e: e.tensor_scalar(out=cjS[:, :], in0=scS[:, :], scalar1=midS[:, 0:1], scalar2=0.0,
                                                      op0=ALU.is_ge, op1=ALU.add, accum_out=cnS[:, 0:1]), ['scS', 'midS'], ['cjS', 'cnS'])
                P.op('dve', lambda e, w=w: e.tensor_scalar(out=gwS[:, :], in0=cnS[:, :], scalar1=TOPK - 0.5, scalar2=w,
                                                           op0=ALU.is_ge, op1=ALU.mult), ['cnS'], ['gwS'])
                P.op('dve', lambda e: e.tensor_tensor(out=loS[:, :], in0=loS[:, :], in1=gwS[:, :], op=ALU.add), ['gwS', 'loS'], ['loS'])
            P.dma('sp', lambda e, r0=r0: e.dma_start(out=gaT[0:8, :], in_=GaS[r0:r0 + 8, :]), ['GaS'], ['gaT'])
            for i in range(3):
                mm(pOfs[i][:, 0:396], zLs[:, :], zRs[:, 0:396], True, False, ['zLs', 'zRs'], ['pOs'], inc=(i == 2))
            for bl in range(LB):
                u = bl % 2
                if bl < NPG:
                    P.dma('pool', lambda e, u=u, bl=bl: e.indirect_dma_start(
                        out=kpg[u][:, :], out_offset=None, in_=ck[:, :],
                        in_offset=bass.IndirectOffsetOnAxis(ap=idxi[:, bl:bl + 1], axis=0)), ['idxi'], ['kpg%d' % u])
                    P.dma('pool', lambda e, u=u, bl=bl: e.indirect_dma_start(
                        out=vpg[u][:, :], out_offset=None, in_=cv[:, :],
                        in_offset=bass.IndirectOffsetOnAxis(ap=idxi[:, bl:bl + 1], axis=0)), ['idxi'], ['vpg%d' % u])
                    P.op('act', lambda e, u=u: e.copy(kpb[u][:, :], kpg[u][:, :]), ['kpg%d' % u], ['kpb%d' % u])
                    P.op('dve', lambda e, u=u: e.tensor_copy(VbS[u][:, :, 0:128], vpg[u][:, :].rearrange("p (h d) -> p h d", h=HA)),
                         ['vpg%d' % u], ['VbS%d' % u])
                else:
                    P.op('dve', lambda e, u=u: e.memset(kpb[u][:, :], 0.0), [], ['kpb%d' % u])
                    P.op('dve', lambda e, u=u: e.memset(VbS[u][:, :, 0:128], 0.0), [], ['VbS%d' % u])
                    P.dma('sp', lambda e, u=u, r0=r0: e.dma_start(out=kpb[u][0:8, :], in_=KsN[r0:r0 + 8, :]), ['KsN'], ['kpb%d' % u])
                    P.dma('sp', lambda e, u=u, r0=r0: e.dma_start(out=VbS[u][0:8, :, 0:128],
                                                           in_=VsN[r0:r0 + 8, :].rearrange("p (h d) -> p h d", h=HA)),
                          ['VsN'], ['VbS%d' % u])
                for h in range(HA):
                    tr(pKT8[:, h, :], kpb[u][:, h * 128:(h + 1) * 128], ident_b[:, :], ['kpb%d' % u, 'ident_b'], ['pKT'], inc=(h == HA - 1))
                P.op('act', lambda e, u=u: e.copy(KTbS[u][:, :, :], pKT8[:, :, :]), ['pKT'], ['KTbS%d' % u])
                P.op('dve', lambda e, bl=bl: e.tensor_scalar(out=mkbS[:, :], in0=scS[:, bl * 128:(bl + 1) * 128], scalar1=loS[:, 0:1],
                                                             scalar2=None, op0=ALU.is_ge), ['scS', 'loS'], ['mkbS'])
                tr(pMs[:, 0:128], mkbS[:, :], ident_b[:, :], ['mkbS', 'ident_b'], ['pMs'])
                P.op('act', lambda e, u=u: e.copy(mTs[u][:, :], pMs[:, 0:128]), ['pMs'], ['mTs%d' % u])
                for hg in range(2):
                    for h4 in range(4):
                        h = hg * 4 + h4
                        mm(pS4[hg][:, h4, :], KTbS[u][:, h, :], qaTs[:, h, :], True, True, ['KTbS%d' % u, 'qaTs'], ['pSs%d' % hg], inc=(h4 == 3))
                    P.op('act', lambda e, hg=hg: e.activation(out=PeS[hg][:, :, :], in_=pS4[hg][:, :, :], func=AF.Exp, scale=DH ** -0.5),
                         ['pSs%d' % hg], ['PeS%d' % hg])
                    P.op('dve', lambda e, hg=hg, u=u: e.tensor_tensor(out=PmS[hg][:, :, :], in0=PeS[hg][:, :, :],
                                                                      in1=mTs[u][:, None, :].to_broadcast([128, 4, 128]), op=ALU.mult),
                         ['PeS%d' % hg, 'mTs%d' % u], ['PmS%d' % hg])
                    for h4 in range(4):
                        h = hg * 4 + h4
                        mm(pOs[h // 3][:, h % 3, 0:129], PmS[hg][:, h4, :], VbS[u][:, h, 0:129], False, bl == LB - 1,
                           ['PmS%d' % hg, 'VbS%d' % u], ['pOs'], inc=(h4 == 3))
            for h in range(HA):
                P.op('act', lambda e, h=h: e.copy(denS[:, h:h + 1], pOs[h // 3][:, h % 3, 128:129]), ['pOs'], ['denS'])
            P.op('dve', lambda e: e.reciprocal(recS[:, :], denS[:, :]), ['denS'], ['recS'])
            for h in range(HA):
                P.op('dve', lambda e, h=h: e.scalar_tensor_tensor(
                    out=gaT[:, h * 128:(h + 1) * 128], in0=pOs[h // 3][:, h % 3, 0:128], scalar=recS[:, h:h + 1],
                    in1=gaT[:, h * 128:(h + 1) * 128], op0=ALU.mult, op1=ALU.mult), ['pOs', 'recS', 'gaT'], ['gaT'])
            P.dma('sp', lambda e, r0=r0: e.dma_start(out=aS[r0:r0 + 8, :], in_=gaT[0:8, :]), ['gaT'], ['aS'])
            P.barrier()
        P.barrier()
        ed.close()

    eo = ExitStack()
    NTL = NOWN + (1 if SDSA else 0)
    mergedT = sb(eo, "mergedT", [128, KC, NTL * 128], BF16)
    aSt = sb(eo, "aSt", [128, 1024], BF16)
    bSt = sb(eo, "bSt", [128, 4, DVB], BF16)
    gaS6 = sb(eo, "gaS6", [128, NOWN, 1024], BF16)
    P.dma('sp', lambda e: e.dma_start(out=gaS6[:, :, :], in_=gaD.rearrange("(n p) f -> p n f", p=128)), ['gaD'], ['gaS6'])
    yacc = sb(eo, "yacc", [128, NTL, D], F32)
    wost = sb(eo, "wost", [128, KC, 256], F32)
    wobf = [sb(eo, "wobf%d" % i, [128, KC, 256], BF16) for i in range(2)]
    bidx = sb(eo, "bidx_sb", [128, NOWN], I32)
    Bt = [sb(eo, "Bt%d" % i, [128, 4, DVB], BF16) for i in range(2)]
    gfb = sb(eo, "gfb", [128, D], F32)
    yjunk = sb(eo, "yjunk", [128, D], BF16)
    ys = sb(eo, "ys", [128, 4], F32)
    yo = [sb(eo, "yo%d" % i, [128, D], F32) for i in range(2)]
    pT6 = [ps(eo, "pT6%d" % i, [128, 1024], BF16) for i in range(2)]
    py = [ps(eo, "py%d" % i, [128, 512], F32) for i in range(2)]
    P.dma('sp', lambda e: e.dma_start(out=bidx[:, :], in_=bidx_d[:, :]), [], ['bidx'])
    P.dma('sp', lambda e: e.dma_start(out=gfb[:, :], in_=g_f[0:1, :].partition_broadcast(128)), [], ['gfb'])
    P.dma('sp', lambda e: e.dma_start(out=yacc[:, 0:NOWN, :], in_=xo.rearrange("(n p) f -> p n f", p=128)), [], ['yacc'])
    if SDSA:
        P.dma('sp', lambda e: e.dma_start(out=yacc[:, NOWN, :], in_=xsm[:, :]), [], ['yacc'])
        P.op('dve', lambda e: e.memset(aSt[:, :], 0.0), [], ['aSt'])
        P.op('dve', lambda e: e.memset(bSt[:, :, :], 0.0), [], ['bSt'])
        P.dma('sp', lambda e: e.dma_start(out=aSt[0:32, :], in_=aS[0:32, :]), ['aS'], ['aSt'])
        P.dma('sp', lambda e: e.dma_start(out=bSt[0:32, :, :].rearrange("p h v -> p (h v)"), in_=bps[0:32, :]), ['bps'], ['bSt'])
    for k in range(NTL):
        bt = Bt[k % 2] if k < NOWN else bSt
        if k < NOWN:
            P.dma('pool', lambda e, bt=bt, k=k: e.indirect_dma_start(
                out=bt[:, :, :].rearrange("p h v -> p (h v)"), out_offset=None, in_=bp_loc[:, :],
                in_offset=bass.IndirectOffsetOnAxis(ap=bidx[:, k:k + 1], axis=0)),
                ['bp_loc', 'bidx'], ['Bt%d' % (k % 2)])
        for half in range(2):
            p6 = pT6[half].rearrange("p (a b) -> p a b", a=8)
            for i in range(8):
                if half == 0:
                    src_ap = gaS6[:, k, i * 128:(i + 1) * 128] if k < NOWN else aSt[:, i * 128:(i + 1) * 128]
                    rk = ['gaS6' if k < NOWN else 'aSt', 'ident_b']
                else:
                    src_ap = bt[:, i // 2, (i % 2) * 128:(i % 2 + 1) * 128]
                    rk = ['Bt%d' % (k % 2) if k < NOWN else 'bSt', 'ident_b']
                tr(p6[:, i, :], src_ap, ident_b[:, :], rk, ['pT6%d' % half], inc=(i == 7))
            P.op('act' if half == 0 else 'dve',
                 (lambda e, p6=p6, k=k, half=half: e.copy(mergedT[:, half * 8:(half + 1) * 8, k * 128:(k + 1) * 128], p6[:, :, :]))
                 if half == 0 else
                 (lambda e, p6=p6, k=k, half=half: e.tensor_copy(mergedT[:, half * 8:(half + 1) * 8, k * 128:(k + 1) * 128], p6[:, :, :])),
                 ['pT6%d' % half], ['mergedT'])
    wi_ = 0
    yi = 0
    for nb in range(D // 256):
        wb = wi_ % 2
        wi_ += 1
        P.dma('sp', lambda e, nb=nb: e.dma_start(
            out=wost[:, :, :], in_=w_out.rearrange("(k p) c -> p k c", p=128)[:, :, nb * 256:(nb + 1) * 256]), [], ['wost'])
        P.op('pool', lambda e, wb=wb: e.tensor_copy(wobf[wb][:, :, :], wost[:, :, :]), ['wost'], ['wobf%d' % wb])
        for k in range(NTL):
            pb = yi % 2
            yi += 1
            for kc in range(KC):
                mm(py[pb][:, 0:256], mergedT[:, kc, k * 128:(k + 1) * 128], wobf[wb][:, kc, :], kc == 0, kc == KC - 1,
                   ['mergedT', 'wobf%d' % wb], ['py%d' % pb], inc=(kc == KC - 1))
            P.op('dve', lambda e, pb=pb, k=k, nb=nb: e.tensor_tensor(
                out=yacc[:, k, nb * 256:(nb + 1) * 256], in0=yacc[:, k, nb * 256:(nb + 1) * 256], in1=py[pb][:, 0:256],
                op=ALU.add), ['py%d' % pb, 'yacc'], ['yacc'])
    for k in range(NTL):
        o = yo[k % 2]
        P.op('dve', lambda e, k=k: e.scalar_tensor_tensor(out=yjunk[:, :], in0=yacc[:, k, :], scalar=1.0, in1=yacc[:, k, :],
                                                          op0=ALU.mult, op1=ALU.mult, accum_out=ys[:, 0:1]),
             ['yacc'], ['yjunk', 'ys0'])
        P.op('act', lambda e: e.activation(out=ys[:, 1:2], in_=ys[:, 0:1], func=AF.Sqrt, bias=eps_t[:, 0:1], scale=1.0 / D),
             ['ys0', 'eps'], ['ys1'])
        P.op('dve', lambda e: e.reciprocal(ys[:, 2:3], ys[:, 1:2]), ['ys1'], ['ys2'])
        P.op('dve', lambda e, k=k, o=o: e.scalar_tensor_tensor(out=o[:, :], in0=yacc[:, k, :], scalar=ys[:, 2:3], in1=gfb[:, :],
                                                               op0=ALU.mult, op1=ALU.mult), ['yacc', 'ys2', 'gfb'], ['yo%d' % (k % 2)])
        ydst = y_p[k * 128:(k + 1) * 128, :] if k < NOWN else y_s[:, :]
        P.dma('sp', lambda e, ydst=ydst, o=o: e.dma_start(out=ydst, in_=o[:, :]), ['yo%d' % (k % 2)], ['y_p'])
    P.barrier(final=True)
    eo.close()
    P.emit()
    return nc


def _rope_tab(pos, half):
    inv = (10000.0 ** (-np.arange(half, dtype=np.float32) / half)).astype(np.float32)
    ang = pos.astype(np.float32)[:, None] * inv[None, :]
    return np.concatenate([np.cos(ang), np.sin(ang)], axis=1).astype(np.float32)


def own_tiles(S, j):
    return [qb for qb in range(S // 128) if zig(qb) == j]


def make_maps(S, x_prompt, norm_in, w_in, w_gate_up, b_gate, gla_norm, w_out, norm_f, x_sample=None, past=8192,
              state_gla=None, cache_k=None, cache_v=None, cache_idx_k=None, page_table=None):
    NT = S // 128
    NOWN = NT // 4
    w_in = np.asarray(w_in[0], np.float32)
    w_dsa = np.ascontiguousarray(w_in[:, :5200])
    g_in_pk = np.ascontiguousarray(np.asarray(norm_in[0], np.float32).reshape(KC, 128).T)
    pos = np.arange(S)
    csA_p = _rope_tab(pos, 64)
    csI_p = _rope_tab(pos, 32)
    ident = np.eye(128, dtype=np.float32)
    tri = np.triu(np.ones((64, 64), np.float32))
    sd = {}
    if cache_k is not None:
        npool = cache_k.shape[1]
        sd = dict(ck=np.asarray(cache_k[0], np.float32).reshape(npool * 128, HA * DH),
                  cv=np.asarray(cache_v[0], np.float32).reshape(npool * 128, HA * DH),
                  cik=np.asarray(cache_idx_k[0], np.float32).reshape(npool * 128, DIDX),
                  iota_p=np.arange(128, dtype=np.float32).reshape(128, 1))
        ms = np.full((128, 128), NEG, np.float32)
        for q in range(8):
            ms[q, :q + 1] = 0.0
        ms[8:, 0] = 0.0
        sd['mask_s'] = ms
    maps = []
    for c in range(NCORES):
        b, j = c // 4, c % 4
        tiles = own_tiles(S, j)
        rows = np.concatenate([np.arange(t * 128, (t + 1) * 128) for t in tiles])
        xb = np.asarray(x_prompt[b], np.float32)
        w_gla = np.concatenate(
            [w_in[:, o + hh * n:o + (hh + 1) * n] for hh in range(HB)
             for (o, n) in ((C_QB, 128), (C_KB, 128), (C_VB, 256), (C_GB, 256))] + [w_in[:, C_AB:C_AB + 16]], axis=1)
        wgb = np.zeros((32, HB * DKB), np.float32)
        wgb[:16] = np.asarray(w_gate_up[0], np.float32)
        wgb[16] = np.asarray(b_gate[0], np.float32)
        mask_o = np.zeros((NOWN, 128, 512), np.float32)
        for k, qb in enumerate(tiles):
            s0 = min(512 * k, S - 512)
            spos = s0 + np.arange(512)[None, :]
            tpos = qb * 128 + np.arange(128)[:, None]
            mask_o[k] = np.where(spos <= tpos, 0.0, NEG)
        bidx = np.zeros((128, NOWN), np.int32)
        for k, qb in enumerate(tiles):
            bidx[:, k] = qb * 128 + np.arange(128)
        xsm = np.zeros((128, D), np.float32)
        if x_sample is not None:
            xsm[:32] = np.asarray(x_sample, np.float32)[4 * c:4 * c + 4].reshape(32, D)
        spos = np.zeros(128, np.int64)
        spos[:32] = past + (np.arange(32) % 8)
        st_in = np.zeros((4, HB, DKB, DVB), np.float32)
        if state_gla is not None:
            st_in = np.ascontiguousarray(np.asarray(state_gla[0], np.float32)[4 * c:4 * c + 4])
        extra = dict(sd)
        if page_table is not None and cache_k is not None:
            extra['pt_s'] = np.ascontiguousarray(np.asarray(page_table, np.int32)[4 * c:4 * c + 4])
        maps.append(dict(
            st_s_in=st_in, xsm=xsm, **extra, csA_s=_rope_tab(spos, 64), csI_s=_rope_tab(spos, 32),
            xp=xb, xo=np.ascontiguousarray(xb[rows]), w_dsa=w_dsa, w_gla=np.ascontiguousarray(w_gla),
            w_out=np.asarray(w_out[0], np.float32), g_in_pk=g_in_pk,
            g_f=np.asarray(norm_f, np.float32).reshape(1, D), gla_g=np.asarray(gla_norm[0], np.float32).reshape(1, DVB),
            wgb=wgb, csA_p=csA_p, csI_p=csI_p, csA_o=np.ascontiguousarray(csA_p[rows]),
            csI_o=np.ascontiguousarray(csI_p[rows]), ident=ident, tri=tri, mask_o=mask_o, bidx=bidx))
    return maps


_STAGE = 99


def kernel(x_prompt, x_sample, cache_k, cache_v, cache_idx_k, state_gla, page_table,
           norm_in, w_in, w_gate_up, b_gate, gla_norm, w_out, norm_f):
    B, S = x_prompt.shape[0], x_prompt.shape[1]
    Bd, Ts = x_sample.shape[0], x_sample.shape[1]
    past = page_table.shape[1] * 128
    npg, npool = page_table.shape[1], cache_k.shape[1]
    nc = build(S, stage=_STAGE, NPG=npg, NPOOL=npool)
    maps = make_maps(S, x_prompt, norm_in, w_in, w_gate_up, b_gate, gla_norm, w_out, norm_f,
                     x_sample=x_sample, past=past, state_gla=state_gla,
                     cache_k=cache_k, cache_v=cache_v, cache_idx_k=cache_idx_k, page_table=page_table)
    res = run_bass_kernel_spmd(nc, maps, core_ids=list(range(NCORES))).results
    y_prompt = np.zeros((B, S, D), np.float32)
    nk = np.zeros((1, B, S, HA, DH), np.float32)
    nv = np.zeros((1, B, S, HA, DH), np.float32)
    nik = np.zeros((1, B, S, DIDX), np.float32)
    st = np.zeros((1, B, HB, DKB, DVB), np.float32)
    y_sample = np.zeros((Bd, Ts, D), np.float32)
    nks = np.zeros((1, Bd, Ts, HA, DH), np.float32)
    nvs = np.zeros((1, Bd, Ts, HA, DH), np.float32)
    niks = np.zeros((1, Bd, Ts, DIDX), np.float32)
    sts = np.zeros((1, Bd, HB, DKB, DVB), np.float32)
    for c in range(NCORES):
        b, j = c // 4, c % 4
        r = res[c]
        for k, qb in enumerate(own_tiles(S, j)):
            sl = slice(qb * 128, (qb + 1) * 128)
            y_prompt[b, sl] = r["y_p"][k * 128:(k + 1) * 128]
            nk[0, b, sl] = r["nk_p"][k * 128:(k + 1) * 128].reshape(128, HA, DH)
            nv[0, b, sl] = r["nv_p"][k * 128:(k + 1) * 128].reshape(128, HA, DH)
            nik[0, b, sl] = r["nik_p"][k * 128:(k + 1) * 128]
        if j == 0:
            st[0, b] = r["st_p"]
        nks[0, 4 * c:4 * c + 4] = r["nk_s"][:32].reshape(4, Ts, HA, DH)
        nvs[0, 4 * c:4 * c + 4] = r["nv_s"][:32].reshape(4, Ts, HA, DH)
        niks[0, 4 * c:4 * c + 4] = r["nik_s"][:32].reshape(4, Ts, DIDX)
        sts[0, 4 * c:4 * c + 4] = r["st_s"]
        y_sample[4 * c:4 * c + 4] = r["y_s"][:32].reshape(4, Ts, D)
    return (y_prompt, y_sample, nk, nv, nik, st, nks, nvs, niks, sts)
```

```python
import numpy as np
import concourse.bass as bass
import concourse.mybir as mybir
from concourse.bass_utils import run_bass_kernel_spmd

F32 = mybir.dt.float32
BF16 = mybir.dt.bfloat16
I32 = mybir.dt.int32
ALU = mybir.AluOpType
AF = mybir.ActivationFunctionType
AX = mybir.AxisListType

D = 2048
KC = 16
HA, DH = 8, 128
HIDX, DIDX = 16, 64
HB, DKB, DVB = 4, 128, 256
RANK = 16
TOPK = 256
EPS = 1e-6
NCORES = 8
C_QA, C_KA, C_VA, C_GA, C_QI, C_KI, C_WI, C_QB, C_KB, C_VB, C_GB, C_AB = (
    0, 1024, 2048, 3072, 4096, 5120, 5184, 5200, 5712, 6224, 7248, 8272)
INW = 8288
NEG = -1.0e30
NBIS = 26


def zig(qb):
    r = qb % 8
    return r if r < 4 else 7 - r


class Prog:
    NS = 10

    def __init__(self, nc):
        self.nc = nc
        self.names = ['pe', 'act', 'dve', 'pool', 'sp']
        self.ops = {k: [] for k in self.names}
        self.cnt = {k: 0 for k in self.names}
        self.waited = {k: {} for k in self.names}
        self.last_w = {}
        self.readers = {}
        self.sems = {}
        self.dcount = {k: 0 for k in self.names}
        self.dval = {}
        for k in self.names:
            self.sems['E:' + k] = nc.alloc_semaphore('e_' + k)
        for q in ('sp', 'pool', 'act'):
            for s in range(self.NS):
                key = 'D:%s:%d' % (q, s)
                self.sems[key] = nc.alloc_semaphore('d_%s_%d' % (q, s))
                self.dval[key] = 0

    def _deps(self, eng, reads, writes):
        toks = []
        for b in list(reads) + list(writes):
            t = self.last_w.get(b)
            if t is not None:
                toks.append(t)
        for b in writes:
            toks.extend(self.readers.get(b, ()))
        need = {}
        for (s, v) in toks:
            if eng == 'pe' and s == 'E:pe':
                continue
            if self.waited[eng].get(s, 0) >= v:
                continue
            if need.get(s, 0) < v:
                need[s] = v
        for s, v in need.items():
            self.waited[eng][s] = v
        return list(need.items())

    def _commit(self, tok, reads, writes):
        for b in writes:
            self.last_w[b] = tok
            self.readers[b] = []
        for b in reads:
            self.readers.setdefault(b, []).append(tok)

    def op(self, eng, fn, reads=(), writes=(), inc=True):
        waits = self._deps(eng, reads, writes)
        tok = ('E:' + eng, self.cnt[eng] + 1)
        if inc:
            self.cnt[eng] += 1
        sems = self.sems
        esem = sems['E:' + eng]

        def run(e, waits=waits, fn=fn, inc=inc):
            for s, v in waits:
                e.wait_ge(sems[s], v)
            ins = fn(e)
            if inc:
                ins.then_inc(esem, 1)
        self.ops[eng].append(run)
        self._commit(tok, reads, writes)
        return tok

    def dma(self, q, fn, reads=(), writes=()):
        i = self.dcount[q]
        self.dcount[q] += 1
        key = 'D:%s:%d' % (q, i % self.NS)
        waits = self._deps(q, reads, writes)
        prev = self.dval[key]
        if prev > 0 and self.waited[q].get(key, 0) < prev:
            waits.append((key, prev))
            self.waited[q][key] = prev
        self.dval[key] = prev + 16
        tok = (key, prev + 16)
        sems = self.sems

        def run(e, waits=waits, fn=fn, key=key):
            for s, v in waits:
                e.wait_ge(sems[s], v)
            fn(e).then_inc(sems[key], 16)
        self.ops[q].append(run)
        self._commit(tok, reads, writes)
        return tok

    def barrier(self, final=False):
        allt = [('E:' + k, self.cnt[k]) for k in self.names if self.cnt[k] > 0]
        allt += [(k, v) for k, v in self.dval.items() if v > 0]
        engs = ['sp'] if final else self.names
        for eng in engs:
            waits = []
            for s, v in allt:
                if s == 'E:' + eng:
                    continue
                if self.waited[eng].get(s, 0) < v:
                    waits.append((s, v))
                    self.waited[eng][s] = v
            sems = self.sems

            def run(e, waits=waits):
                for s, v in waits:
                    e.wait_ge(sems[s], v)
            self.ops[eng].append(run)

    def emit(self):
        nc = self.nc
        ops = self.ops
        with nc.Block() as blk:
            @blk.sync
            def _(e):
                for f in ops['sp']:
                    f(e)

            @blk.tensor
            def _(e):
                for f in ops['pe']:
                    f(e)

            @blk.scalar
            def _(e):
                for f in ops['act']:
                    f(e)

            @blk.vector
            def _(e):
                for f in ops['dve']:
                    f(e)

            @blk.gpsimd
            def _(e):
                for f in ops['pool']:
                    f(e)


def build(S, stage=99, NPG=0, NPOOL=0):
    SDSA = NPG > 0
    from contextlib import ExitStack
    NT = S // 128
    NOWN = NT // 4
    NO = NOWN * 128
    GT = min(NT, 8)
    NG = NT // GT
    NCH = S // 64
    LK = [min(S, 512 * (k + 1)) for k in range(NOWN)]
    SOFF = [sum(LK[:k]) for k in range(NOWN)]
    STOT = sum(LK)
    nc = bass.Bass("TRN2", target_bir_lowering=False)
    P = Prog(nc)

    def din(name, shape, dt=F32):
        return nc.dram_tensor(name, list(shape), dt, kind="ExternalInput").ap()

    def dout(name, shape, dt=F32):
        return nc.dram_tensor(name, list(shape), dt, kind="ExternalOutput").ap()

    def dscr(name, shape, dt):
        return nc.dram_tensor(name, list(shape), dt, kind="Internal").ap()

    xp = din("xp", [S, D])
    xo = din("xo", [NO, D])
    w_dsa = din("w_dsa", [D, 5200])
    w_gla = din("w_gla", [D, HB * 768 + 16])
    w_out = din("w_out", [D, D])
    g_in_pk = din("g_in_pk", [128, KC])
    g_f = din("g_f", [1, D])
    gla_g = din("gla_g", [1, DVB])
    wgb = din("wgb", [32, HB * DKB])
    csA_p = din("csA_p", [S, 128])
    csI_p = din("csI_p", [S, 64])
    csA_o = din("csA_o", [NO, 128])
    csI_o = din("csI_o", [NO, 64])
    ident_d = din("ident", [128, 128])
    tri_d = din("tri", [64, 64])
    mask_o = din("mask_o", [NOWN, 128, 512])
    bidx_d = din("bidx", [128, NOWN], I32)
    xsm = din("xsm", [128, D])
    csA_s = din("csA_s", [128, 128])
    csI_s = din("csI_s", [128, 64])
    st_s_in = din("st_s_in", [4, HB, DKB, DVB])
    if SDSA:
        ck = din("ck", [NPOOL * 128, HA * DH])
        cv = din("cv", [NPOOL * 128, HA * DH])
        cik = din("cik", [NPOOL * 128, DIDX])
        pt_s = din("pt_s", [4, NPG], I32)
        iota_p = din("iota_p", [128, 1])
        mask_s = din("mask_s", [128, 128])

    y_p = dout("y_p", [NO, D])
    nk_p = dout("nk_p", [NO, HA * DH])
    nv_p = dout("nv_p", [NO, HA * DH])
    nik_p = dout("nik_p", [NO, DIDX])
    st_p = dout("st_p", [HB, DKB, DVB])
    nk_s = dout("nk_s", [128, HA * DH])
    nv_s = dout("nv_s", [128, HA * DH])
    nik_s = dout("nik_s", [128, DIDX])
    st_s = dout("st_s", [4, HB, DKB, DVB])
    y_s = dout("y_s", [128, D])

    KT = dscr("KT", [HA, 128, S], BF16)
    Vs = dscr("Vs", [S, HA * DH], BF16)
    Vg = dscr("Vg", [S, HB * DVB], BF16)
    Gg = dscr("Gg", [S, HB * DVB], BF16)
    QTg = dscr("QTg", [HB, 128, S], BF16)
    KTg = dscr("KTg", [HB, 128, S], BF16)
    bp_loc = dscr("bp_loc", [S, HB * DVB], BF16)
    QTs = dscr("QTs", [HB, 128, 128], BF16)
    KTs = dscr("KTs", [HB, 128, 128], BF16)
    Vgs = dscr("Vgs", [128, HB * DVB], BF16)
    Ggs = dscr("Ggs", [128, HB * DVB], BF16)
    bps = dscr("bps", [128, HB * DVB], BF16)
    QaS = dscr("QaS", [128, 1024], BF16)
    QiS = dscr("QiS", [128, 1024], BF16)
    GaS = dscr("GaS", [128, 1024], BF16)
    KsN = dscr("KsN", [128, 1024], BF16)
    VsN = dscr("VsN", [128, 1024], BF16)
    KiN = dscr("KiN", [128, 64], BF16)
    WsS = dscr("WsS", [128, HIDX], F32)
    aS = dscr("aS", [128, 1024], BF16)
    gaD = dscr("gaD", [NO, 1024], BF16)

    es0 = ExitStack()

    def sb(es, name, shape, dt):
        return es.enter_context(nc.sbuf_tensor(name, list(shape), dt))

    def ps(es, name, shape, dt=F32):
        return es.enter_context(nc.psum_tensor(name, list(shape), dt))

    def mm(out, lhsT, rhs, start, stop, reads, writes, inc=True):
        P.op('pe', lambda e: e.matmul(out, lhsT, rhs, start=start, stop=stop), reads, writes, inc)

    def tr(out, in_, ident, reads, writes, inc=True):
        P.op('pe', lambda e: e.transpose(out, in_, ident), reads, writes, inc)

    g_pk = sb(es0, "g_pk", [128, KC], F32)
    ident_f = sb(es0, "ident_f", [128, 128], F32)
    ident_b = sb(es0, "ident_b", [128, 128], BF16)
    tri_b = sb(es0, "tri_b", [64, 64], F32)
    abT1 = sb(es0, "abT1", [32, S], BF16)
    eps_t = sb(es0, "eps_t", [128, 1], F32)
    abT1s = sb(es0, "abT1s", [32, 128], BF16)

    P.dma('sp', lambda e: e.dma_start(out=g_pk[:, :], in_=g_in_pk[:, :]), [], ['g_pk'])
    P.dma('sp', lambda e: e.dma_start(out=ident_f[:, :], in_=ident_d[:, :]), [], ['ident_f'])
    P.dma('sp', lambda e: e.dma_start(out=tri_b[:, :], in_=tri_d[:, :]), [], ['tri'])
    P.op('dve', lambda e: e.tensor_copy(ident_b[:, :], ident_f[:, :]), ['ident_f'], ['ident_b'])
    P.op('dve', lambda e: e.memset(abT1[:, :], 1.0), [], ['abT1'])
    P.op('dve', lambda e: e.memset(eps_t[:, :], EPS), [], ['eps'])
    P.op('dve', lambda e: e.memset(abT1s[:, :], 1.0), [], ['abT1s'])

    es1 = ExitStack()
    kiT2 = sb(es1, "kiT2", [128, S], BF16)
    gaS = sb(es1, "gaS", [128, NOWN, 1024], BF16)
    qaT = sb(es1, "qaT", [128, HA, NO], BF16)
    qiT = sb(es1, "qiT", [128, 8, NO], BF16)
    wabs = sb(es1, "wabs", [128, NOWN + 1, HIDX], F32)
    wsgn = sb(es1, "wsgn", [128, NOWN + 1, HIDX], F32)

    ep = ExitStack()
    xT = sb(ep, "xT", [128, KC, (GT + 1) * 128], BF16)
    xs = [sb(ep, "xs%d" % i, [128, D], F32) for i in range(2)]
    xh = sb(ep, "xh", [128, D], BF16)
    junk = sb(ep, "junk", [128, D], BF16)
    ss = sb(ep, "ss", [128, 4], F32)
    wst2 = [sb(ep, "wst%d" % i, [128, KC, 256], F32) for i in range(2)]
    wbf = [sb(ep, "wbf%d" % i, [128, KC, 256], BF16) for i in range(2)]
    zf = [sb(ep, "zf%d" % i, [128, 256], F32) for i in range(2)]
    zr = [sb(ep, "zr%d" % i, [128, 256], F32) for i in range(2)]
    zb = [sb(ep, "zb%d" % i, [128, 256], BF16) for i in range(2)]
    tp = [sb(ep, "tp%d" % i, [128, 256], F32) for i in range(2)]
    csA = sb(ep, "csA", [128, GT + 1, 128], F32)
    csI = sb(ep, "csI", [128, GT + 1, 64], F32)
    ktst = [sb(ep, "ktst%d" % i, [128, 2, 128], BF16) for i in range(2)]
    ftst = [sb(ep, "ftst%d" % i, [128, 512], BF16) for i in range(2)]
    pz = [ps(ep, "pz%d" % i, [128, 512], F32) for i in range(2)]
    pt = [ps(ep, "pt%d" % i, [128, 8, 128], BF16) for i in range(2)]
    pkf = [ps(ep, "pk%d" % i, [128, 1024], BF16) for i in range(2)]
    pk = [t[:, 0:256].rearrange("p (a b) -> p a b", a=2) for t in pkf]
    cnt = {'x': 0, 'w': 0, 'z': 0}

    def build_xT(src, row0, col, slot):
        b = cnt['x'] % 2
        cnt['x'] += 1
        xsb = xs[b]
        P.dma('sp', lambda e: e.dma_start(out=xsb[:, :], in_=src[row0:row0 + 128, :]), [], ['xs%d' % b])
        P.op('dve', lambda e: e.scalar_tensor_tensor(out=junk[:, :], in0=xsb[:, :], scalar=1.0, in1=xsb[:, :],
                                                      op0=ALU.mult, op1=ALU.mult, accum_out=ss[:, 0:1]),
             ['xs%d' % b], ['junk', 'ss0'])
        P.op('act', lambda e: e.activation(out=ss[:, 1:2], in_=ss[:, 0:1], func=AF.Sqrt,
                                            bias=eps_t[:, 0:1], scale=1.0 / D), ['ss0', 'eps'], ['ss1'])
        P.op('dve', lambda e: e.reciprocal(ss[:, 2:3], ss[:, 1:2]), ['ss1'], ['ss2'])
        P.op('act', lambda e: e.activation(out=xh[:, :], in_=xsb[:, :], func=AF.Copy, scale=ss[:, 2:3]),
             ['xs%d' % b, 'ss2'], ['xh'])
        for half in range(2):
            pb = pt[half]
            for i in range(8):
                kc = half * 8 + i
                tr(pb[:, i, :], xh[:, kc * 128:(kc + 1) * 128], ident_b[:, :],
                   ['xh', 'ident_b'], ['pt%d' % half], inc=(i == 7))
            eng = 'act' if half == 0 else 'dve'
            dst = xT[:, half * 8:(half + 1) * 8, col * 128:(col + 1) * 128]
            if eng == 'act':
                P.op('act', lambda e, dst=dst, pb=pb: e.copy(dst, pb[:, :, :]), ['pt%d' % half], ['xT'])
            else:
                P.op('dve', lambda e, dst=dst, pb=pb: e.tensor_copy(dst, pb[:, :, :]), ['pt%d' % half], ['xT'])

    def load_w(wsrc, c0, ncol):
        b = cnt['w'] % 2
        cnt['w'] += 1
        src = wsrc.rearrange("(k p) c -> p k c", p=128)[:, :, c0:c0 + ncol]
        wst = wst2[b]
        P.dma('sp', lambda e: e.dma_start(out=wst[:, :, 0:ncol], in_=src), [], ['wst%d' % b])
        wb = wbf[b]
        P.op('pool', lambda e: e.tensor_tensor(out=wb[:, :, 0:ncol], in0=wst[:, :, 0:ncol],
                                               in1=g_pk[:, :, None].to_broadcast([128, KC, ncol]), op=ALU.mult),
             ['wst%d' % b, 'g_pk'], ['wbf%d' % b])
        return b

    def proj_tok(wb, ncol, col, m=128):
        z = cnt['z'] % 2
        cnt['z'] += 1
        for kc in range(KC):
            mm(pz[z][0:m, 0:ncol], xT[:, kc, col * 128:col * 128 + m], wbf[wb][:, kc, 0:ncol],
               kc == 0, kc == KC - 1, ['xT', 'wbf%d' % wb], ['pz%d' % z], inc=(kc == KC - 1))
        return z

    def rope(z, nh, half, cs_ap, zi):
        w = nh * 2 * half
        src = zf[zi]
        P.op('act', lambda e: e.copy(src[:, 0:w], pz[z][:, 0:w]), ['pz%d' % z], ['zf%d' % zi])
        sv = src[:, 0:w].rearrange("p (h two f) -> p h two f", h=nh, two=2)
        dv = zr[zi][:, 0:w].rearrange("p (h two f) -> p h two f", h=nh, two=2)
        t1 = tp[0][:, 0:nh * half].rearrange("p (h f) -> p h f", h=nh)
        t2 = tp[1][:, 0:nh * half].rearrange("p (h f) -> p h f", h=nh)
        cosb = cs_ap[:, None, 0:half].to_broadcast([128, nh, half])
        sinb = cs_ap[:, None, half:2 * half].to_broadcast([128, nh, half])
        x1, x2 = sv[:, :, 0, :], sv[:, :, 1, :]
        rk = ['zf%d' % zi, 'cs']
        P.op('dve', lambda e: e.tensor_tensor(out=t1, in0=x1, in1=cosb, op=ALU.mult), rk, ['tp0'])
        P.op('dve', lambda e: e.tensor_tensor(out=t2, in0=x2, in1=sinb, op=ALU.mult), rk, ['tp1'])
        P.op('dve', lambda e: e.tensor_tensor(out=dv[:, :, 0, :], in0=t1, in1=t2, op=ALU.subtract),
             ['tp0', 'tp1'], ['zr%d' % zi])
        P.op('dve', lambda e: e.tensor_tensor(out=t1, in0=x2, in1=cosb, op=ALU.mult), rk, ['tp0'])
        P.op('dve', lambda e: e.tensor_tensor(out=t2, in0=x1, in1=sinb, op=ALU.mult), rk, ['tp1'])
        P.op('dve', lambda e: e.tensor_tensor(out=dv[:, :, 1, :], in0=t1, in1=t2, op=ALU.add),
             ['tp0', 'tp1'], ['zr%d' % zi])

    zc = {'i': 0}

    def nextz():
        zc['i'] += 1
        return zc['i'] % 2

    for gi in range(NG):
        P.dma('sp', lambda e, gi=gi: e.dma_start(
            out=csA[:, 0:GT, :], in_=csA_p[gi * GT * 128:(gi + 1) * GT * 128, :].rearrange("(n p) f -> p n f", p=128)),
            [], ['cs'])
        P.dma('sp', lambda e, gi=gi: e.dma_start(
            out=csI[:, 0:GT, :], in_=csI_p[gi * GT * 128:(gi + 1) * GT * 128, :].rearrange("(n p) f -> p n f", p=128)),
            [], ['cs'])
        for t in range(GT):
            build_xT(xp, (gi * GT + t) * 128, t, 0)
        for blk in range(4):
            wb = load_w(w_dsa, C_KA + blk * 256, 256)
            for t in range(GT):
                tt = gi * GT + t
                z = proj_tok(wb, 256, t)
                zi = nextz()
                rope(z, 2, 64, csA[:, t, :], zi)
                P.op('act', lambda e, zi=zi: e.copy(zb[zi][:, :], zr[zi][:, :]), ['zr%d' % zi], ['zb%d' % zi])
                k2 = zi
                for h in range(2):
                    tr(pk[k2][:, h, :], zb[zi][:, h * 128:(h + 1) * 128], ident_b[:, :],
                       ['zb%d' % zi, 'ident_b'], ['pk%d' % k2], inc=(h == 1))
                P.op('act', lambda e, k2=k2: e.copy(ktst[k2][:, :, :], pk[k2][:, :, :]), ['pk%d' % k2], ['ktst%d' % k2])
                P.dma('act', lambda e, k2=k2, blk=blk, tt=tt: e.dma_start(
                    out=KT[blk * 2:blk * 2 + 2, :, tt * 128:(tt + 1) * 128].rearrange("h d s -> d h s"),
                    in_=ktst[k2][:, :, :]), ['ktst%d' % k2], ['KT'])
        for blk in range(4):
            wb = load_w(w_dsa, C_VA + blk * 256, 256)
            for t in range(GT):
                tt = gi * GT + t
                z = proj_tok(wb, 256, t)
                zi = nextz()
                P.op('act', lambda e, zi=zi, z=z: e.copy(zb[zi][:, :], pz[z][:, 0:256]), ['pz%d' % z], ['zb%d' % zi])
                P.dma('act', lambda e, zi=zi, blk=blk, tt=tt: e.dma_start(
                    out=Vs[tt * 128:(tt + 1) * 128, blk * 256:(blk + 1) * 256], in_=zb[zi][:, :]),
                    ['zb%d' % zi], ['Vs'])
        wb = load_w(w_dsa, C_KI, 64)
        for t in range(GT):
            tt = gi * GT + t
            z = proj_tok(wb, 64, t)
            zi = nextz()
            rope(z, 1, 32, csI[:, t, :], zi)
            P.op('act', lambda e, zi=zi: e.copy(zb[zi][:, 0:64], zr[zi][:, 0:64]), ['zr%d' % zi], ['zb%d' % zi])
            P.op('act', lambda e, zi=zi: e.copy(zb[zi][:, 64:128], zr[zi][:, 0:64]), ['zr%d' % zi], ['zb%d' % zi])
            tr(pk[zi][:, 0, :], zb[zi][:, 0:128], ident_b[:, :], ['zb%d' % zi, 'ident_b'], ['pk%d' % zi])
            P.op('act', lambda e, zi=zi, tt=tt: e.copy(kiT2[:, tt * 128:(tt + 1) * 128], pk[zi][:, 0, :]),
                 ['pk%d' % zi], ['kiT2'])
        for hh in range(HB):
            wb = load_w(w_gla, hh * 768, 256)
            for which, dstD, scl in ((0, QTg, DKB ** -0.5), (1, KTg, 1.0)):
                for c4 in range(GT * 128 // 512):
                    z = cnt['z'] % 2
                    cnt['z'] += 1
                    for kc in range(KC):
                        mm(pz[z][:, 0:512], wbf[wb][:, kc, which * 128:(which + 1) * 128],
                           xT[:, kc, c4 * 512:(c4 + 1) * 512], kc == 0, kc == KC - 1,
                           ['xT', 'wbf%d' % wb], ['pz%d' % z], inc=(kc == KC - 1))
                    t0 = gi * GT * 128 + c4 * 512
                    fi = nextz()
                    P.op('act', lambda e, z=z, fi=fi, scl=scl: e.activation(
                        out=ftst[fi][:, :], in_=pz[z][:, 0:512], func=AF.Copy, scale=scl),
                        ['pz%d' % z], ['ftst%d' % fi])
                    P.dma('act', lambda e, fi=fi, dstD=dstD, hh=hh, t0=t0: e.dma_start(
                        out=dstD[hh, :, t0:t0 + 512], in_=ftst[fi][:, :]), ['ftst%d' % fi], ['QKTg'])
            wb = load_w(w_gla, hh * 768 + 256, 256)
            for t in range(GT):
                tt = gi * GT + t
                z = proj_tok(wb, 256, t)
                zi = nextz()
                P.op('act', lambda e, zi=zi, z=z: e.copy(zb[zi][:, :], pz[z][:, 0:256]), ['pz%d' % z], ['zb%d' % zi])
                P.dma('act', lambda e, zi=zi, tt=tt, hh=hh: e.dma_start(
                    out=Vg[tt * 128:(tt + 1) * 128, hh * DVB:(hh + 1) * DVB], in_=zb[zi][:, :]), ['zb%d' % zi], ['Vg'])
            wb = load_w(w_gla, hh * 768 + 512, 256)
            for t in range(GT):
                tt = gi * GT + t
                z = proj_tok(wb, 256, t)
                zi = nextz()
                P.op('act', lambda e, zi=zi, z=z: e.activation(out=zb[zi][:, :], in_=pz[z][:, 0:256], func=AF.Silu),
                     ['pz%d' % z], ['zb%d' % zi])
                P.dma('act', lambda e, zi=zi, tt=tt, hh=hh: e.dma_start(
                    out=Gg[tt * 128:(tt + 1) * 128, hh * DVB:(hh + 1) * DVB], in_=zb[zi][:, :]), ['zb%d' % zi], ['Gg'])
        wb = load_w(w_gla, HB * 768, 16)
        for c4 in range(GT * 128 // 512):
            z = cnt['z'] % 2
            cnt['z'] += 1
            for kc in range(KC):
                mm(pz[z][0:16, 0:512], wbf[wb][:, kc, 0:16], xT[:, kc, c4 * 512:(c4 + 1) * 512],
                   kc == 0, kc == KC - 1, ['xT', 'wbf%d' % wb], ['pz%d' % z], inc=(kc == KC - 1))
            t0 = gi * GT * 128 + c4 * 512
            P.op('act', lambda e, z=z, t0=t0: e.copy(abT1[0:16, t0:t0 + 512], pz[z][0:16, 0:512]),
                 ['pz%d' % z], ['abT1'])

    assert NOWN <= GT
    P.dma('sp', lambda e: e.dma_start(out=csA[:, 0:NOWN, :], in_=csA_o.rearrange("(n p) f -> p n f", p=128)), [], ['cs'])
    P.dma('sp', lambda e: e.dma_start(out=csI[:, 0:NOWN, :], in_=csI_o.rearrange("(n p) f -> p n f", p=128)), [], ['cs'])
    P.dma('sp', lambda e: e.dma_start(out=csA[:, NOWN, :], in_=csA_s[:, :]), [], ['cs'])
    P.dma('sp', lambda e: e.dma_start(out=csI[:, NOWN, :], in_=csI_s[:, :]), [], ['cs'])
    for t in range(NOWN):
        build_xT(xo, t * 128, t, 0)
    build_xT(xsm, 0, NOWN, 0)
    wb = load_w(w_dsa, C_WI, 16)
    for t in range(NOWN + 1):
        z = proj_tok(wb, 16, t)
        P.op('act', lambda e, z=z, t=t: e.activation(out=wsgn[:, t, :], in_=pz[z][:, 0:16], func=AF.Sign),
             ['pz%d' % z], ['wsgn'])
        P.op('act', lambda e, z=z, t=t: e.activation(out=wabs[:, t, :], in_=pz[z][:, 0:16], func=AF.Abs,
                                                     scale=(HIDX ** -0.5) * (DIDX ** -0.5)),
             ['pz%d' % z], ['wabs'])
    for blk in range(4):
        wb = load_w(w_dsa, C_QA + blk * 256, 256)
        for t in range(NOWN + 1):
            z = proj_tok(wb, 256, t)
            zi = nextz()
            rope(z, 2, 64, csA[:, t, :], zi)
            P.op('act', lambda e, zi=zi: e.copy(zb[zi][:, :], zr[zi][:, :]), ['zr%d' % zi], ['zb%d' % zi])
            if t == NOWN:
                P.dma('act', lambda e, zi=zi, blk=blk: e.dma_start(out=QaS[:, blk * 256:(blk + 1) * 256], in_=zb[zi][:, :]),
                      ['zb%d' % zi], ['QaS'])
                continue
            for h in range(2):
                tr(pk[zi][:, h, :], zb[zi][:, h * 128:(h + 1) * 128], ident_b[:, :],
                   ['zb%d' % zi, 'ident_b'], ['pk%d' % zi], inc=(h == 1))
            P.op('act', lambda e, zi=zi, blk=blk, t=t: e.copy(
                qaT[:, blk * 2:blk * 2 + 2, t * 128:(t + 1) * 128], pk[zi][:, :, :]), ['pk%d' % zi], ['qaT'])
    for blk in range(4):
        wb = load_w(w_dsa, C_KA + blk * 256, 256)
        for t in range(NOWN + 1):
            z = proj_tok(wb, 256, t)
            zi = nextz()
            rope(z, 2, 64, csA[:, t, :], zi)
            dst = nk_p[t * 128:(t + 1) * 128, blk * 256:(blk + 1) * 256] if t < NOWN else nk_s[:, blk * 256:(blk + 1) * 256]
            P.dma('act', lambda e, zi=zi, dst=dst: e.dma_start(out=dst, in_=zr[zi][:, :]), ['zr%d' % zi], ['nk_p'])
            if t == NOWN:
                P.op('act', lambda e, zi=zi: e.copy(zb[zi][:, :], zr[zi][:, :]), ['zr%d' % zi], ['zb%d' % zi])
                P.dma('act', lambda e, zi=zi, blk=blk: e.dma_start(out=KsN[:, blk * 256:(blk + 1) * 256], in_=zb[zi][:, :]),
                      ['zb%d' % zi], ['KsN'])
    for blk in range(4):
        wb = load_w(w_dsa, C_VA + blk * 256, 256)
        for t in range(NOWN + 1):
            z = proj_tok(wb, 256, t)
            zi = nextz()
            P.op('act', lambda e, zi=zi, z=z: e.copy(zr[zi][:, :], pz[z][:, 0:256]), ['pz%d' % z], ['zr%d' % zi])
            dst = nv_p[t * 128:(t + 1) * 128, blk * 256:(blk + 1) * 256] if t < NOWN else nv_s[:, blk * 256:(blk + 1) * 256]
            P.dma('act', lambda e, zi=zi, dst=dst: e.dma_start(out=dst, in_=zr[zi][:, :]), ['zr%d' % zi], ['nv_p'])
            if t == NOWN:
                P.op('act', lambda e, zi=zi: e.copy(zb[zi][:, :], zr[zi][:, :]), ['zr%d' % zi], ['zb%d' % zi])
                P.dma('act', lambda e, zi=zi, blk=blk: e.dma_start(out=VsN[:, blk * 256:(blk + 1) * 256], in_=zb[zi][:, :]),
                      ['zb%d' % zi], ['VsN'])
    for blk in range(4):
        wb = load_w(w_dsa, C_GA + blk * 256, 256)
        for t in range(NOWN + 1):
            z = proj_tok(wb, 256, t)
            if t == NOWN:
                zi = nextz()
                P.op('act', lambda e, z=z, zi=zi: e.activation(out=zb[zi][:, :], in_=pz[z][:, 0:256], func=AF.Silu),
                     ['pz%d' % z], ['zb%d' % zi])
                P.dma('act', lambda e, zi=zi, blk=blk: e.dma_start(out=GaS[:, blk * 256:(blk + 1) * 256], in_=zb[zi][:, :]),
                      ['zb%d' % zi], ['GaS'])
                continue
            P.op('act', lambda e, z=z, blk=blk, t=t: e.activation(
                out=gaS[:, t, blk * 256:(blk + 1) * 256], in_=pz[z][:, 0:256], func=AF.Silu), ['pz%d' % z], ['gaS'])
    for blk in range(4):
        wb = load_w(w_dsa, C_QI + blk * 256, 256)
        for t in range(NOWN + 1):
            z = proj_tok(wb, 256, t)
            zi = nextz()
            rope(z, 4, 32, csI[:, t, :], zi)
            P.op('dve', lambda e, zi=zi, blk=blk, t=t: e.tensor_tensor(
                out=zb[zi][:, :].rearrange("p (h f) -> p h f", h=4),
                in0=zr[zi][:, :].rearrange("p (h f) -> p h f", h=4),
                in1=wabs[:, t, blk * 4:(blk + 1) * 4, None].to_broadcast([128, 4, 64]), op=ALU.mult),
                ['zr%d' % zi, 'wabs'], ['zb%d' % zi])
            if t == NOWN:
                P.dma('act', lambda e, zi=zi, blk=blk: e.dma_start(out=QiS[:, blk * 256:(blk + 1) * 256], in_=zb[zi][:, :]),
                      ['zb%d' % zi], ['QiS'])
                continue
            for h in range(2):
                tr(pk[zi][:, h, :], zb[zi][:, h * 128:(h + 1) * 128], ident_b[:, :],
                   ['zb%d' % zi, 'ident_b'], ['pk%d' % zi], inc=(h == 1))
            P.op('act', lambda e, zi=zi, blk=blk, t=t: e.copy(
                qiT[:, blk * 2:blk * 2 + 2, t * 128:(t + 1) * 128], pk[zi][:, :, :]), ['pk%d' % zi], ['qiT'])
    wb = load_w(w_dsa, C_KI, 64)
    for t in range(NOWN + 1):
        z = proj_tok(wb, 64, t)
        zi = nextz()
        rope(z, 1, 32, csI[:, t, :], zi)
        dst = nik_p[t * 128:(t + 1) * 128, :] if t < NOWN else nik_s[:, :]
        P.dma('act', lambda e, zi=zi, dst=dst: e.dma_start(out=dst, in_=zr[zi][:, 0:64]),
              ['zr%d' % zi], ['nik_p'])
        if t == NOWN:
            P.op('act', lambda e, zi=zi: e.copy(zb[zi][:, 0:64], zr[zi][:, 0:64]), ['zr%d' % zi], ['zb%d' % zi])
            P.dma('act', lambda e, zi=zi: e.dma_start(out=KiN[:, :], in_=zb[zi][:, 0:64]), ['zb%d' % zi], ['KiN'])
    P.dma('act', lambda e: e.dma_start(out=WsS[:, :], in_=wsgn[:, NOWN, :]), ['wsgn'], ['WsS'])
    tS = NOWN
    for hh in range(HB):
        wb = load_w(w_gla, hh * 768, 256)
        z = proj_tok(wb, 256, tS)
        zi = nextz()
        P.op('act', lambda e, zi=zi, z=z: e.activation(out=zb[zi][:, 0:128], in_=pz[z][:, 0:128], func=AF.Copy,
                                                        scale=DKB ** -0.5), ['pz%d' % z], ['zb%d' % zi])
        P.op('act', lambda e, zi=zi, z=z: e.copy(zb[zi][:, 128:256], pz[z][:, 128:256]), ['pz%d' % z], ['zb%d' % zi])
        for h2 in range(2):
            tr(pk[zi][:, h2, :], zb[zi][:, h2 * 128:(h2 + 1) * 128], ident_b[:, :],
               ['zb%d' % zi, 'ident_b'], ['pk%d' % zi], inc=(h2 == 1))
        P.op('act', lambda e, zi=zi: e.copy(ktst[zi][:, :, :], pk[zi][:, :, :]), ['pk%d' % zi], ['ktst%d' % zi])
        P.dma('act', lambda e, zi=zi, hh=hh: e.dma_start(out=QTs[hh, :, :], in_=ktst[zi][:, 0, :]), ['ktst%d' % zi], ['QKTs'])
        P.dma('act', lambda e, zi=zi, hh=hh: e.dma_start(out=KTs[hh, :, :], in_=ktst[zi][:, 1, :]), ['ktst%d' % zi], ['QKTs'])
        wb = load_w(w_gla, hh * 768 + 256, 256)
        z = proj_tok(wb, 256, tS)
        zi = nextz()
        P.op('act', lambda e, zi=zi, z=z: e.copy(zb[zi][:, :], pz[z][:, 0:256]), ['pz%d' % z], ['zb%d' % zi])
        P.dma('act', lambda e, zi=zi, hh=hh: e.dma_start(out=Vgs[:, hh * DVB:(hh + 1) * DVB], in_=zb[zi][:, :]),
              ['zb%d' % zi], ['Vgs'])
        wb = load_w(w_gla, hh * 768 + 512, 256)
        z = proj_tok(wb, 256, tS)
        zi = nextz()
        P.op('act', lambda e, zi=zi, z=z: e.activation(out=zb[zi][:, :], in_=pz[z][:, 0:256], func=AF.Silu),
             ['pz%d' % z], ['zb%d' % zi])
        P.dma('act', lambda e, zi=zi, hh=hh: e.dma_start(out=Ggs[:, hh * DVB:(hh + 1) * DVB], in_=zb[zi][:, :]),
              ['zb%d' % zi], ['Ggs'])
    wb = load_w(w_gla, HB * 768, 16)
    z = cnt['z'] % 2
    cnt['z'] += 1
    for kc in range(KC):
        mm(pz[z][0:16, 0:128], wbf[wb][:, kc, 0:16], xT[:, kc, tS * 128:(tS + 1) * 128],
           kc == 0, kc == KC - 1, ['xT', 'wbf%d' % wb], ['pz%d' % z], inc=(kc == KC - 1))
    P.op('act', lambda e, z=z: e.copy(abT1s[0:16, :], pz[z][0:16, 0:128]), ['pz%d' % z], ['abT1s'])
    P.barrier()
    ep.close()
    if stage < 2:
        P.barrier(final=True)
        P.emit()
        return nc

    ea = ExitStack()
    scores = sb(ea, "scores", [128, STOT], F32)
    lo = sb(ea, "lo", [128, NOWN], F32)
    mid = sb(ea, "mid", [128, NOWN], F32)
    cntt = sb(ea, "cntt", [128, NOWN], F32)
    gw = sb(ea, "gw", [128, NOWN], F32)
    cjunk = sb(ea, "cjunk", [128, S], BF16)
    e2 = ExitStack()
    diag = sb(e2, "diag", [128, HIDX, 128], BF16)
    Rb = [sb(e2, "Rb%d" % i, [128, 512], BF16) for i in range(4)]
    mk = sb(e2, "mk", [128, 512], F32)
    pd = [ps(e2, "pd%d" % i, [128, 512], F32) for i in range(4)]
    pi = [ps(e2, "pi%d" % i, [128, 512], F32) for i in range(2)]
    ci = 0
    for k in range(NOWN):
        for h in range(HIDX):
            P.op('dve', lambda e, h=h, k=k: e.tensor_scalar(out=diag[:, h, :], in0=ident_b[:, :],
                                                            scalar1=wsgn[:, k, h:h + 1], scalar2=None, op0=ALU.mult),
                 ['ident_b', 'wsgn'], ['diag'])
        P.dma('sp', lambda e, k=k: e.dma_start(out=mk[:, :], in_=mask_o[k, :, :]), [], ['mk'])
        nchk = LK[k] // 512
        for c in range(nchk):
            pib = ci % 2
            ci += 1
            for m in range(8):
                a, b = 2 * (m % 2), 2 * (m % 2) + 1
                mm(pd[a][:, :], qiT[0:64, m, k * 128:(k + 1) * 128], kiT2[0:64, c * 512:(c + 1) * 512],
                   True, True, ['qiT', 'kiT2'], ['pd%d' % a])
                mm(pd[b][:, :], qiT[64:128, m, k * 128:(k + 1) * 128], kiT2[64:128, c * 512:(c + 1) * 512],
                   True, True, ['qiT', 'kiT2'], ['pd%d' % b])
                P.op('act', lambda e, a=a: e.activation(out=Rb[a][:, :], in_=pd[a][:, :], func=AF.Relu),
                     ['pd%d' % a], ['Rb%d' % a])
                P.op('dve', lambda e, b=b: e.tensor_scalar(out=Rb[b][:, :], in0=pd[b][:, :], scalar1=0.0,
                                                           scalar2=None, op0=ALU.max),
                     ['pd%d' % b], ['Rb%d' % b])
                mm(pi[pib][:, :], diag[:, 2 * m, :], Rb[a][:, :], m == 0, False,
                   ['diag', 'Rb%d' % a], ['pi%d' % pib], inc=False)
                mm(pi[pib][:, :], diag[:, 2 * m + 1, :], Rb[b][:, :], False, m == 7,
                   ['diag', 'Rb%d' % b], ['pi%d' % pib], inc=True)
            dst = scores[:, SOFF[k] + c * 512:SOFF[k] + (c + 1) * 512]
            if c == nchk - 1:
                P.op('dve', lambda e, dst=dst, pib=pib: e.tensor_tensor(out=dst, in0=pi[pib][:, :], in1=mk[:, :],
                                                                        op=ALU.add),
                     ['pi%d' % pib, 'mk'], ['scores'])
            else:
                P.op('act', lambda e, dst=dst, pib=pib: e.copy(dst, pi[pib][:, :]), ['pi%d' % pib], ['scores'])
    P.barrier()
    e2.close()

    W0 = 64.0
    P.op('dve', lambda e: e.memset(lo[:, :], -W0), [], ['lo'])
    for it in range(NBIS):
        w = W0 / (2 ** it)
        P.op('dve', lambda e, w=w: e.tensor_scalar(out=mid[:, :], in0=lo[:, :], scalar1=w, scalar2=None, op0=ALU.add),
             ['lo'], ['mid'])
        for k in range(NOWN):
            P.op('dve', lambda e, k=k: e.tensor_scalar(
                out=cjunk[:, 0:LK[k]], in0=scores[:, SOFF[k]:SOFF[k] + LK[k]], scalar1=mid[:, k:k + 1], scalar2=0.0,
                op0=ALU.is_ge, op1=ALU.add, accum_out=cntt[:, k:k + 1]), ['scores', 'mid'], ['cjunk', 'cntt'])
        P.op('dve', lambda e, w=w: e.tensor_scalar(out=gw[:, :], in0=cntt[:, :], scalar1=TOPK - 0.5, scalar2=w,
                                                   op0=ALU.is_ge, op1=ALU.mult), ['cntt'], ['gw'])
        P.op('dve', lambda e: e.tensor_tensor(out=lo[:, :], in0=lo[:, :], in1=gw[:, :], op=ALU.add),
             ['gw', 'lo'], ['lo'])

    e4 = ExitStack()
    KTb = [sb(e4, "KTb%d" % i, [128, HA, 512], BF16) for i in range(2)]
    Vb = [sb(e4, "Vb%d" % i, [128, 4, HA, 132], BF16) for i in range(2)]
    mkb = sb(e4, "mkb", [128, 512], BF16)
    mT = [sb(e4, "mT%d" % i, [128, 4, 128], BF16) for i in range(2)]
    Pe = [sb(e4, "Pe%d" % i, [128, 4, 128], BF16) for i in range(2)]
    Pm = [sb(e4, "Pm%d" % i, [128, 4, 128], BF16) for i in range(2)]
    den = sb(e4, "den", [128, HA], F32)
    rec = sb(e4, "rec", [128, HA], F32)
    pS = [ps(e4, "pS%d" % i, [128, 4, 128], F32) for i in range(2)]
    pOf = [ps(e4, "pO%d" % i, [128, 512], F32) for i in range(3)]
    pO = [t[:, 0:396].rearrange("p (a b) -> p a b", a=3) for t in pOf]
    pMf = ps(e4, "pM", [128, 1024], BF16)
    pM = pMf[:, 0:512].rearrange("p (a b) -> p a b", a=4)
    for i in range(2):
        P.op('dve', lambda e, i=i: e.memset(Vb[i][:, :, :, 128:129], 1.0), [], ['Vb%d' % i])
    zeroL = sb(e4, "zeroL", [128, 128], BF16)
    zeroR = sb(e4, "zeroR", [128, 512], BF16)
    P.op('dve', lambda e: e.memset(zeroL[:, :], 0.0), [], ['zeroL'])
    P.op('dve', lambda e: e.memset(zeroR[:, :], 0.0), [], ['zeroR'])
    ld = 0
    for k in range(NOWN):
        nq = LK[k] // 512
        nsb = LK[k] // 128
        for i in range(3):
            mm(pOf[i][:, 0:396], zeroL[:, :], zeroR[:, 0:396], True, False, ['zeroL', 'zeroR'], ['pO'], inc=(i == 2))
        for q4 in range(nq):
            bi = ld % 2
            ld += 1
            P.dma('sp', lambda e, bi=bi, q4=q4: e.dma_start(
                out=KTb[bi][:, :, :], in_=KT[:, :, q4 * 512:(q4 + 1) * 512].rearrange("h d s -> d h s")),
                ['KT'], ['KTb%d' % bi])
            for j4 in range(4):
                P.dma('sp', lambda e, bi=bi, q4=q4, j4=j4: e.dma_start(
                    out=Vb[bi][:, j4, :, 0:128],
                    in_=Vs[q4 * 512 + j4 * 128:q4 * 512 + (j4 + 1) * 128, :].rearrange("p (h d) -> p h d", h=HA)),
                    ['Vs'], ['Vb%d' % bi])
            P.op('dve', lambda e, k=k, q4=q4: e.tensor_scalar(
                out=mkb[:, :], in0=scores[:, SOFF[k] + q4 * 512:SOFF[k] + (q4 + 1) * 512],
                scalar1=lo[:, k:k + 1], scalar2=None, op0=ALU.is_ge), ['scores', 'lo'], ['mkb'])
            for j4 in range(4):
                tr(pM[:, j4, :], mkb[:, j4 * 128:(j4 + 1) * 128], ident_b[:, :], ['mkb', 'ident_b'], ['pM'], inc=(j4 == 3))
            P.op('act', lambda e, bi=bi: e.copy(mT[bi][:, :, :], pM[:, :, :]), ['pM'], ['mT%d' % bi])
            for j4 in range(4):
                sbi = q4 * 4 + j4
                for hg in range(2):
                    for h4 in range(4):
                        h = hg * 4 + h4
                        mm(pS[hg][:, h4, :], KTb[bi][:, h, j4 * 128:(j4 + 1) * 128], qaT[:, h, k * 128:(k + 1) * 128],
                           True, True, ['KTb%d' % bi, 'qaT'], ['pS%d' % hg], inc=(h4 == 3))
                    P.op('act', lambda e, hg=hg: e.activation(out=Pe[hg][:, :, :], in_=pS[hg][:, :, :], func=AF.Exp,
                                                              scale=DH ** -0.5), ['pS%d' % hg], ['Pe%d' % hg])
                    P.op('dve', lambda e, hg=hg, bi=bi, j4=j4: e.tensor_tensor(
                        out=Pm[hg][:, :, :], in0=Pe[hg][:, :, :],
                        in1=mT[bi][:, j4:j4 + 1, :].to_broadcast([128, 4, 128]), op=ALU.mult),
                        ['Pe%d' % hg, 'mT%d' % bi], ['Pm%d' % hg])
                    for h4 in range(4):
                        h = hg * 4 + h4
                        mm(pO[h // 3][:, h % 3, 0:129], Pm[hg][:, h4, :], Vb[bi][:, j4, h, 0:129],
                           False, sbi == nsb - 1, ['Pm%d' % hg, 'Vb%d' % bi], ['pO'], inc=(h4 == 3))
        for h in range(HA):
            P.op('act', lambda e, h=h: e.copy(den[:, h:h + 1], pO[h // 3][:, h % 3, 128:129]), ['pO'], ['den'])
        P.op('dve', lambda e: e.reciprocal(rec[:, :], den[:, :]), ['den'], ['rec'])
        for h in range(HA):
            P.op('dve', lambda e, h=h, k=k: e.scalar_tensor_tensor(
                out=gaS[:, k, h * 128:(h + 1) * 128], in0=pO[h // 3][:, h % 3, 0:128], scalar=rec[:, h:h + 1],
                in1=gaS[:, k, h * 128:(h + 1) * 128], op0=ALU.mult, op1=ALU.mult), ['pO', 'rec', 'gaS'], ['gaS'])
    if stage == 4:
        dbg_a = dout("dbg_a", [NO, 1024], BF16)
        dbg_lo = dout("dbg_lo", [128, NOWN])
        dbg_sc = dout("dbg_sc", [128, STOT])
        dbg_den = dout("dbg_den", [128, HA])
        P.dma('sp', lambda e: e.dma_start(out=dbg_sc[:, :], in_=scores[:, :]), ['scores'], ['dbg_sc'])
        P.dma('sp', lambda e: e.dma_start(out=dbg_den[:, :], in_=den[:, :]), ['den'], ['dbg_den'])
        P.dma('sp', lambda e: e.dma_start(out=dbg_a.rearrange("(n p) f -> p n f", p=128), in_=gaS[:, :, :]), ['gaS'], ['dbg_a'])
        P.dma('sp', lambda e: e.dma_start(out=dbg_lo[:, :], in_=lo[:, :]), ['lo'], ['dbg_lo'])
        P.barrier(final=True)
        P.emit()
        return nc
    P.dma('sp', lambda e: e.dma_start(out=gaD.rearrange("(n p) f -> p n f", p=128), in_=gaS[:, :, :]), ['gaS'], ['gaD'])
    P.barrier()
    e4.close()
    ea.close()
    es1.close()

    eg = ExitStack()
    CH = 64
    wgb_f = sb(eg, "wgb_f", [32, HB * DKB], F32)
    wgb_b = sb(eg, "wgb_b", [32, HB * DKB], BF16)
    qT_g = sb(eg, "qT_g", [128, S], BF16)
    kT_g = sb(eg, "kT_g", [128, S], BF16)
    laT = sb(eg, "laT", [128, S], F32)
    cumT = sb(eg, "cumT", [128, S], F32)
    Et = sb(eg, "Et", [128, S], F32)
    flagT = sb(eg, "flagT", [128, S], F32)
    qdT = sb(eg, "qdT", [128, S], BF16)
    kdT = sb(eg, "kdT", [128, S], BF16)
    klT = sb(eg, "klT", [128, S], BF16)
    elast = sb(eg, "elast", [128, NCH], F32)
    kl = sb(eg, "kl", [64, NCH, 128], BF16)
    vch = sb(eg, "vch", [64, NCH, DVB], BF16)
    attT = sb(eg, "attT", [64, NCH, 64], BF16)
    Sf = sb(eg, "Sf", [128, DVB], F32)
    Sb = sb(eg, "Sb", [128, DVB], BF16)
    ob = [sb(eg, "ob%d" % i, [64, 8, DVB], F32) for i in range(2)]
    osq = sb(eg, "osq", [64, 8, DVB], F32)
    gch = sb(eg, "gch", [64, 8, DVB], BF16)
    bpb = sb(eg, "bpb", [64, 8, DVB], BF16)
    gn = sb(eg, "gn", [64, DVB], F32)
    sq8 = sb(eg, "sq8", [64, 8, 3], F32)
    pzg = ps(eg, "pzg", [128, 512], F32)
    pTg = ps(eg, "pTg", [128, 1024], BF16)
    pA = ps(eg, "pA", [128, 512], F32)
    pU = [ps(eg, "pU%d" % i, [128, 512], F32) for i in range(2)]
    pOg = [ps(eg, "pOg%d" % i, [128, 512], F32) for i in range(2)]

    P.dma('sp', lambda e: e.dma_start(out=wgb_f[:, :], in_=wgb[:, :]), [], ['wgb_f'])
    P.op('dve', lambda e: e.tensor_copy(wgb_b[:, :], wgb_f[:, :]), ['wgb_f'], ['wgb_b'])
    P.dma('sp', lambda e: e.dma_start(out=gn[:, :], in_=gla_g[0:1, :].partition_broadcast(64)), [], ['gn'])
    P.op('dve', lambda e: e.memset(flagT[:, :], 1.0), [], ['flagT'])
    P.op('dve', lambda e: e.memset(flagT[:, :].rearrange("p (n c) -> p n c", c=CH)[:, :, 0:1], 0.0), [], ['flagT'])
    for hh in range(HB):
        P.dma('sp', lambda e, hh=hh: e.dma_start(out=qT_g[:, :], in_=QTg[hh, :, :]), ['QKTg'], ['qkT_g'])
        P.dma('sp', lambda e, hh=hh: e.dma_start(out=kT_g[:, :], in_=KTg[hh, :, :]), ['QKTg'], ['qkT_g'])
        P.dma('sp', lambda e, hh=hh: e.dma_start(out=vch[:, :, :], in_=Vg[:, hh * DVB:(hh + 1) * DVB].rearrange("(n c) v -> c n v", c=CH)), ['Vg'], ['vch'])
        for c4 in range(S // 512):
            mm(pzg[:, :], wgb_b[:, hh * DKB:(hh + 1) * DKB], abT1[:, c4 * 512:(c4 + 1) * 512], True, True, ['wgb_b', 'abT1'], ['pzg'])
            P.op('act', lambda e, c4=c4: e.activation(out=Et[:, c4 * 512:(c4 + 1) * 512], in_=pzg[:, :], func=AF.Sigmoid),
                 ['pzg'], ['Et'])
        P.op('act', lambda e: e.activation(out=laT[:, :], in_=Et[:, :], func=AF.Ln), ['Et'], ['laT'])
        P.op('dve', lambda e: e.tensor_scalar(out=laT[:, :], in0=laT[:, :], scalar1=1.0 / 16.0, scalar2=None, op0=ALU.mult),
             ['laT'], ['laT'])
        P.op('dve', lambda e: e.tensor_tensor_scan(out=cumT[:, :], data0=flagT[:, :], data1=laT[:, :], initial=0.0,
                                                    op0=ALU.mult, op1=ALU.add), ['flagT', 'laT'], ['cumT'])
        cum3 = cumT[:, :].rearrange("p (n c) -> p n c", c=CH)
        P.op('act', lambda e: e.activation(out=Et[:, :], in_=cumT[:, :], func=AF.Exp), ['cumT'], ['Et'])
        P.op('dve', lambda e: e.tensor_tensor(out=qdT[:, :], in0=qT_g[:, :], in1=Et[:, :], op=ALU.mult),
             ['qkT_g', 'Et'], ['qdT'])
        P.op('act', lambda e: e.activation(out=Et[:, :], in_=cumT[:, :], func=AF.Exp, scale=-1.0), ['cumT', 'qdT'], ['Et'])
        P.op('dve', lambda e: e.tensor_tensor(out=kdT[:, :], in0=kT_g[:, :], in1=Et[:, :], op=ALU.mult),
             ['qkT_g', 'Et'], ['kdT'])
        P.op('dve', lambda e: e.tensor_tensor(out=Et[:, :].rearrange("p (n c) -> p n c", c=CH),
                                              in0=cum3[:, :, CH - 1:CH].to_broadcast([128, NCH, CH]), in1=cum3,
                                              op=ALU.subtract), ['cumT', 'kdT'], ['Et'])
        P.op('act', lambda e: e.activation(out=Et[:, :], in_=Et[:, :], func=AF.Exp), ['Et'], ['Et'])
        P.op('dve', lambda e: e.tensor_tensor(out=klT[:, :], in0=kT_g[:, :], in1=Et[:, :], op=ALU.mult),
             ['qkT_g', 'Et'], ['klT'])
        P.op('act', lambda e: e.activation(out=elast[:, :].rearrange("p (n o) -> p n o", o=1), in_=cum3[:, :, CH - 1:CH],
                                           func=AF.Exp), ['cumT'], ['elast'])
        pT3 = pTg[0:64, :].rearrange("p (a b) -> p a b", a=8)
        for n8 in range(NCH // 8):
            for i in range(8):
                n = n8 * 8 + i
                tr(pT3[:, i, :], klT[:, n * CH:(n + 1) * CH], ident_b[:, :], ['klT', 'ident_b'], ['pTg'], inc=(i == 7))
            P.op('act', lambda e, n8=n8: e.copy(kl[:, n8 * 8:(n8 + 1) * 8, :], pT3[:, :, :]), ['pTg'], ['kl'])
        pA3 = pA[0:64, :].rearrange("p (a b) -> p a b", a=8)
        for n8 in range(NCH // 8):
            for i in range(8):
                n = n8 * 8 + i
                mm(pA3[:, i, :], kdT[:, n * CH:(n + 1) * CH], qdT[:, n * CH:(n + 1) * CH], True, True,
                   ['kdT', 'qdT'], ['pA'], inc=(i == 7))
            P.op('dve', lambda e, n8=n8: e.tensor_tensor(out=attT[:, n8 * 8:(n8 + 1) * 8, :], in0=pA3[:, :, :],
                                                         in1=tri_b[:, None, :].to_broadcast([64, 8, 64]), op=ALU.mult),
                 ['pA', 'tri'], ['attT'])
        for n in range(NCH):
            u = n % 2
            mm(pU[u][:, 0:DVB], kl[:, n, :], vch[:, n, :], True, True, ['kl', 'vch'], ['pU%d' % u])
            mm(pOg[u][0:64, 0:DVB], attT[:, n, :], vch[:, n, :], True, n == 0, ['attT', 'vch'], ['pOg%d' % u],
               inc=(n == 0))
            if n > 0:
                mm(pOg[u][0:64, 0:DVB], qdT[:, n * CH:(n + 1) * CH], Sb[:, :], False, True, ['qdT', 'Sb'], ['pOg%d' % u])
                P.op('dve', lambda e, n=n, u=u: e.scalar_tensor_tensor(out=Sf[:, :], in0=Sf[:, :], scalar=elast[:, n:n + 1],
                                                                       in1=pU[u][:, 0:DVB], op0=ALU.mult, op1=ALU.add),
                     ['Sf', 'elast', 'pU%d' % u], ['Sf'])
            else:
                P.op('dve', lambda e, u=u: e.tensor_copy(Sf[:, :], pU[u][:, 0:DVB]), ['pU%d' % u], ['Sf'])
            if n < NCH - 1:
                P.op('act', lambda e: e.copy(Sb[:, :], Sf[:, :]), ['Sf'], ['Sb'])
            bsel = (n // 8) % 2
            P.op('act', lambda e, n=n, u=u, bsel=bsel: e.copy(ob[bsel][:, n % 8, :], pOg[u][0:64, 0:DVB]),
                 ['pOg%d' % u], ['ob%d' % bsel])
            if n % 8 == 7:
                n0 = n - 7
                o8 = ob[bsel]
                P.dma('sp', lambda e, n0=n0, hh=hh: e.dma_start(
                    out=gch[:, :, :], in_=Gg[n0 * CH:(n0 + 8) * CH, hh * DVB:(hh + 1) * DVB].rearrange("(n c) v -> c n v", c=CH)), ['Gg'], ['gch'])
                P.op('dve', lambda e, o8=o8: e.tensor_tensor(out=osq[:, :, :], in0=o8[:, :, :], in1=o8[:, :, :], op=ALU.mult),
                     ['ob%d' % bsel], ['osq'])
                P.op('dve', lambda e: e.tensor_reduce(out=sq8[:, :, 0], in_=osq[:, :, :], axis=AX.X, op=ALU.add),
                     ['osq'], ['sq8a'])
                P.op('act', lambda e: e.activation(out=sq8[:, :, 1], in_=sq8[:, :, 0], func=AF.Sqrt, bias=eps_t[0:64, 0:1],
                                                   scale=1.0 / DVB), ['sq8a', 'eps'], ['sq8b'])
                P.op('dve', lambda e: e.reciprocal(sq8[:, :, 2], sq8[:, :, 1]), ['sq8b'], ['sq8c'])
                P.op('dve', lambda e, o8=o8: e.tensor_tensor(out=osq[:, :, :], in0=o8[:, :, :],
                                                             in1=sq8[:, :, 2:3].to_broadcast([64, 8, DVB]), op=ALU.mult),
                     ['ob%d' % bsel, 'sq8c'], ['osq'])
                P.op('dve', lambda e: e.tensor_tensor(out=osq[:, :, :], in0=osq[:, :, :],
                                                      in1=gn[:, None, :].to_broadcast([64, 8, DVB]), op=ALU.mult),
                     ['osq', 'gn'], ['osq'])
                P.op('dve', lambda e: e.tensor_tensor(out=bpb[:, :, :], in0=osq[:, :, :], in1=gch[:, :, :], op=ALU.mult),
                     ['osq', 'gch'], ['bpb'])
                P.dma('sp', lambda e, n0=n0, hh=hh: e.dma_start(
                    out=bp_loc[n0 * CH:(n0 + 8) * CH, hh * DVB:(hh + 1) * DVB].rearrange("(n c) v -> c n v", c=CH), in_=bpb[:, :, :]),
                    ['bpb'], ['bp_loc'])
        P.dma('sp', lambda e, hh=hh: e.dma_start(out=st_p[hh, :, :], in_=Sf[:, :]), ['Sf'], ['st_p'])
    if stage == 5:
        dbg_b = dout("dbg_b", [S, HB * DVB], BF16)
        P.dma('sp', lambda e: e.dma_start(out=dbg_b[:, :], in_=bp_loc[:, :]), ['bp_loc'], ['dbg_b'])
        P.barrier(final=True)
        P.emit()
        return nc
    P.barrier()
    eg.close()

    egs = ExitStack()
    wgbS_f = sb(egs, "wgbS_f", [32, HB * DKB], F32)
    wgbS_b = sb(egs, "wgbS_b", [32, HB * DKB], BF16)
    gnS = sb(egs, "gnS", [64, DVB], F32)
    pzS = ps(egs, "pzS", [128, 512], F32)
    pTS = ps(egs, "pTS", [128, 1024], BF16)
    pAS = ps(egs, "pAS", [128, 512], F32)
    pUS = [ps(egs, "pUS%d" % i, [128, 512], F32) for i in range(2)]
    pOS = [ps(egs, "pOS%d" % i, [128, 512], F32) for i in range(2)]
    P.dma('sp', lambda e: e.dma_start(out=wgbS_f[:, :], in_=wgb[:, :]), [], ['wgbS_f'])
    P.op('dve', lambda e: e.tensor_copy(wgbS_b[:, :], wgbS_f[:, :]), ['wgbS_f'], ['wgbS_b'])
    P.dma('sp', lambda e: e.dma_start(out=gnS[:, :], in_=gla_g[0:1, :].partition_broadcast(64)), [], ['gnS'])
    CS, NSQ, TS = 8, 4, 32
    qTs = sb(egs, "qTs", [128, TS], BF16)
    kTs = sb(egs, "kTs", [128, TS], BF16)
    vs = sb(egs, "vs", [CS, NSQ, DVB], BF16)
    gs = sb(egs, "gs", [CS, NSQ, DVB], BF16)
    laS = sb(egs, "laS", [128, TS], F32)
    cumS = sb(egs, "cumS", [128, TS], F32)
    EtS = sb(egs, "EtS", [128, TS], F32)
    flagS = sb(egs, "flagS", [128, TS], F32)
    qdS = sb(egs, "qdS", [128, TS], BF16)
    kdS = sb(egs, "kdS", [128, TS], BF16)
    klS = sb(egs, "klS", [128, TS], BF16)
    elS = sb(egs, "elS", [128, NSQ], F32)
    klSt = sb(egs, "klSt", [CS, NSQ, 128], BF16)
    attS = sb(egs, "attS", [CS, NSQ, CS], BF16)
    S0f = [sb(egs, "S0f%d" % i, [128, DVB], F32) for i in range(2)]
    S0b = [sb(egs, "S0b%d" % i, [128, DVB], BF16) for i in range(2)]
    SfS = [sb(egs, "SfS%d" % i, [128, DVB], F32) for i in range(2)]
    obS = sb(egs, "obS", [CS, NSQ, DVB], F32)
    oqS = sb(egs, "oqS", [CS, NSQ, DVB], F32)
    bpS = sb(egs, "bpS", [CS, NSQ, DVB], BF16)
    sqS = sb(egs, "sqS", [CS, NSQ, 3], F32)
    cum3s = cumS[:, :].rearrange("p (n c) -> p n c", c=CS)
    pT3s = pTS[0:CS, 0:NSQ * 128].rearrange("p (a b) -> p a b", a=NSQ)
    pA3s = pAS[0:CS, 0:NSQ * CS].rearrange("p (a b) -> p a b", a=NSQ)
    P.op('dve', lambda e: e.memset(flagS[:, :], 1.0), [], ['flagS'])
    P.op('dve', lambda e: e.memset(flagS[:, :].rearrange("p (n c) -> p n c", c=CS)[:, :, 0:1], 0.0), [], ['flagS'])
    si = 0
    for hh in range(HB):
        P.dma('sp', lambda e, hh=hh: e.dma_start(out=qTs[:, :], in_=QTs[hh, :, 0:TS]), ['QKTs'], ['qTs'])
        P.dma('sp', lambda e, hh=hh: e.dma_start(out=kTs[:, :], in_=KTs[hh, :, 0:TS]), ['QKTs'], ['kTs'])
        P.dma('sp', lambda e, hh=hh: e.dma_start(
            out=vs[:, :, :], in_=Vgs[0:TS, hh * DVB:(hh + 1) * DVB].rearrange("(n c) v -> c n v", c=CS)), ['Vgs'], ['vs'])
        P.dma('sp', lambda e, hh=hh: e.dma_start(
            out=gs[:, :, :], in_=Ggs[0:TS, hh * DVB:(hh + 1) * DVB].rearrange("(n c) v -> c n v", c=CS)), ['Ggs'], ['gs'])
        mm(pzS[:, 0:TS], wgbS_b[:, hh * DKB:(hh + 1) * DKB], abT1s[:, 0:TS], True, True, ['wgbS_b', 'abT1s'], ['pzS'])
        P.op('act', lambda e: e.activation(out=cumS[:, :], in_=pzS[:, 0:TS], func=AF.Sigmoid), ['pzS'], ['cumS'])
        P.op('act', lambda e: e.activation(out=laS[:, :], in_=cumS[:, :], func=AF.Ln), ['cumS'], ['laS'])
        P.op('dve', lambda e: e.tensor_scalar(out=laS[:, :], in0=laS[:, :], scalar1=1.0 / 16.0, scalar2=None, op0=ALU.mult),
             ['laS'], ['laS'])
        P.op('dve', lambda e: e.tensor_tensor_scan(out=cumS[:, :], data0=flagS[:, :], data1=laS[:, :], initial=0.0,
                                                    op0=ALU.mult, op1=ALU.add), ['flagS', 'laS'], ['cumS'])
        P.op('act', lambda e: e.activation(out=EtS[:, :], in_=cumS[:, :], func=AF.Exp), ['cumS'], ['EtS'])
        P.op('dve', lambda e: e.tensor_tensor(out=qdS[:, :], in0=qTs[:, :], in1=EtS[:, :], op=ALU.mult), ['qTs', 'EtS'], ['qdS'])
        P.op('act', lambda e: e.activation(out=EtS[:, :], in_=cumS[:, :], func=AF.Exp, scale=-1.0), ['cumS'], ['EtS'])
        P.op('dve', lambda e: e.tensor_tensor(out=kdS[:, :], in0=kTs[:, :], in1=EtS[:, :], op=ALU.mult), ['kTs', 'EtS'], ['kdS'])
        P.op('dve', lambda e: e.tensor_tensor(out=EtS[:, :].rearrange("p (n c) -> p n c", c=CS),
                                              in0=cum3s[:, :, CS - 1:CS].to_broadcast([128, NSQ, CS]), in1=cum3s,
                                              op=ALU.subtract), ['cumS'], ['EtS'])
        P.op('act', lambda e: e.activation(out=EtS[:, :], in_=EtS[:, :], func=AF.Exp), ['EtS'], ['EtS'])
        P.op('dve', lambda e: e.tensor_tensor(out=klS[:, :], in0=kTs[:, :], in1=EtS[:, :], op=ALU.mult), ['kTs', 'EtS'], ['klS'])
        P.op('act', lambda e: e.activation(out=elS[:, :].rearrange("p (n o) -> p n o", o=1), in_=cum3s[:, :, CS - 1:CS],
                                           func=AF.Exp), ['cumS'], ['elS'])
        for n in range(NSQ):
            tr(pT3s[:, n, :], klS[:, n * CS:(n + 1) * CS], ident_b[:, :], ['klS', 'ident_b'], ['pTS'], inc=(n == NSQ - 1))
        P.op('act', lambda e: e.copy(klSt[:, :, :], pT3s[:, :, :]), ['pTS'], ['klSt'])
        for n in range(NSQ):
            mm(pA3s[:, n, :], kdS[:, n * CS:(n + 1) * CS], qdS[:, n * CS:(n + 1) * CS], True, True,
               ['kdS', 'qdS'], ['pAS'], inc=(n == NSQ - 1))
        P.op('dve', lambda e: e.tensor_tensor(out=attS[:, :, :], in0=pA3s[:, :, :],
                                              in1=tri_b[0:CS, None, 0:CS].to_broadcast([CS, NSQ, CS]), op=ALU.mult),
             ['pAS', 'tri'], ['attS'])
        for n in range(NSQ):
            u = si % 2
            si += 1
            P.dma('sp', lambda e, n=n, hh=hh, u=u: e.dma_start(out=S0f[u][:, :], in_=st_s_in[n, hh, :, :]), [], ['S0f%d' % u])
            P.op('act', lambda e, u=u: e.copy(S0b[u][:, :], S0f[u][:, :]), ['S0f%d' % u], ['S0b%d' % u])
            mm(pUS[u][:, 0:DVB], klSt[:, n, :], vs[:, n, :], True, True, ['klSt', 'vs'], ['pUS%d' % u])
            mm(pOS[u][0:CS, 0:DVB], attS[:, n, :], vs[:, n, :], True, False, ['attS', 'vs'], ['pOS%d' % u], inc=False)
            mm(pOS[u][0:CS, 0:DVB], qdS[:, n * CS:(n + 1) * CS], S0b[u][:, :], False, True, ['qdS', 'S0b%d' % u], ['pOS%d' % u])
            P.op('dve', lambda e, n=n, u=u: e.scalar_tensor_tensor(out=SfS[u][:, :], in0=S0f[u][:, :], scalar=elS[:, n:n + 1],
                                                                   in1=pUS[u][:, 0:DVB], op0=ALU.mult, op1=ALU.add),
                 ['S0f%d' % u, 'elS', 'pUS%d' % u], ['SfS%d' % u])
            P.dma('sp', lambda e, n=n, hh=hh, u=u: e.dma_start(out=st_s[n, hh, :, :], in_=SfS[u][:, :]), ['SfS%d' % u], ['st_s'])
            P.op('act', lambda e, n=n, u=u: e.copy(obS[:, n, :], pOS[u][0:CS, 0:DVB]), ['pOS%d' % u], ['obS'])
        P.op('dve', lambda e: e.tensor_tensor(out=oqS[:, :, :], in0=obS[:, :, :], in1=obS[:, :, :], op=ALU.mult), ['obS'], ['oqS'])
        P.op('dve', lambda e: e.tensor_reduce(out=sqS[:, :, 0], in_=oqS[:, :, :], axis=AX.X, op=ALU.add), ['oqS'], ['sqSa'])
        P.op('act', lambda e: e.activation(out=sqS[:, :, 1], in_=sqS[:, :, 0], func=AF.Sqrt, bias=eps_t[0:CS, 0:1],
                                           scale=1.0 / DVB), ['sqSa', 'eps'], ['sqSb'])
        P.op('dve', lambda e: e.reciprocal(sqS[:, :, 2], sqS[:, :, 1]), ['sqSb'], ['sqSc'])
        P.op('dve', lambda e: e.tensor_tensor(out=oqS[:, :, :], in0=obS[:, :, :],
                                              in1=sqS[:, :, 2:3].to_broadcast([CS, NSQ, DVB]), op=ALU.mult),
             ['obS', 'sqSc'], ['oqS'])
        P.op('dve', lambda e: e.tensor_tensor(out=oqS[:, :, :], in0=oqS[:, :, :],
                                              in1=gnS[0:CS, None, :].to_broadcast([CS, NSQ, DVB]), op=ALU.mult),
             ['oqS', 'gnS'], ['oqS'])
        P.op('dve', lambda e: e.tensor_tensor(out=bpS[:, :, :], in0=oqS[:, :, :], in1=gs[:, :, :], op=ALU.mult),
             ['oqS', 'gs'], ['bpS'])
        P.dma('sp', lambda e, hh=hh: e.dma_start(
            out=bps[0:TS, hh * DVB:(hh + 1) * DVB].rearrange("(n c) v -> c n v", c=CS), in_=bpS[:, :, :]), ['bpS'], ['bps'])
    P.barrier()
    egs.close()

    if SDSA:
        ed = ExitStack()
        LB = NPG + 1
        LP = LB * 128
        widths = [512] * (LP // 512) + ([LP % 512] if LP % 512 else [])
        kiT2s = sb(ed, "kiT2s", [128, LP], BF16)
        scS = sb(ed, "scS", [128, LP], F32)
        cjS = sb(ed, "cjS", [128, LP], BF16)
        ptb = sb(ed, "ptb", [128, NPG], I32)
        idxi = sb(ed, "idxi", [128, NPG], I32)
        iot = sb(ed, "iot", [128, 1], F32)
        mkS = sb(ed, "mkS", [128, 128], F32)
        kig = [sb(ed, "kig%d" % i, [128, DIDX], F32) for i in range(2)]
        kib = [sb(ed, "kib%d" % i, [128, 128], BF16) for i in range(2)]
        kinb = sb(ed, "kinb", [128, DIDX], BF16)
        qtok = sb(ed, "qtok", [128, 1024], BF16)
        qiTs = sb(ed, "qiTs", [128, 8, 128], BF16)
        qaTs = sb(ed, "qaTs", [128, HA, 128], BF16)
        wsS = sb(ed, "wsS", [128, HIDX], F32)
        diagS = sb(ed, "diagS", [128, HIDX, 128], BF16)
        RbS = [sb(ed, "RbS%d" % i, [128, 512], BF16) for i in range(4)]
        loS = sb(ed, "loS", [128, 1], F32)
        midS = sb(ed, "midS", [128, 1], F32)
        cnS = sb(ed, "cnS", [128, 1], F32)
        gwS = sb(ed, "gwS", [128, 1], F32)
        kpg = [sb(ed, "kpg%d" % i, [128, HA * DH], F32) for i in range(4)]
        vpg = [sb(ed, "vpg%d" % i, [128, HA * DH], F32) for i in range(4)]
        kpb = [sb(ed, "kpb%d" % i, [128, HA * DH], BF16) for i in range(2)]
        KTbS = [sb(ed, "KTbS%d" % i, [128, HA, 128], BF16) for i in range(2)]
        VbS = [sb(ed, "VbS%d" % i, [128, HA, 132], BF16) for i in range(2)]
        mkbS = sb(ed, "mkbS", [128, 128], BF16)
        mTs = [sb(ed, "mTs%d" % i, [128, 128], BF16) for i in range(2)]
        PeS = [sb(ed, "PeS%d" % i, [128, 4, 128], BF16) for i in range(2)]
        PmS = [sb(ed, "PmS%d" % i, [128, 4, 128], BF16) for i in range(2)]
        denS = sb(ed, "denS", [128, HA], F32)
        recS = sb(ed, "recS", [128, HA], F32)
        gaT = sb(ed, "gaT", [128, 1024], BF16)
        zLs = sb(ed, "zLs", [128, 128], BF16)
        zRs = sb(ed, "zRs", [128, 512], BF16)
        bfA = ps(ed, "bfA", [128, 1024], BF16)
        bfB = ps(ed, "bfB", [128, 1024], BF16)
        Fb = [ps(ed, "Fb%d" % i, [128, 512], F32) for i in range(6)]
        pk8 = bfA[:, :].rearrange("p (a b) -> p a b", a=8)
        pKT8 = pk8
        pdS = Fb[0:4]
        piS = Fb[4:6]
        pS4 = [t[:, :].rearrange("p (a b) -> p a b", a=4) for t in Fb[0:2]]
        pOfs = Fb[2:5]
        pOs = [t[:, 0:396].rearrange("p (a b) -> p a b", a=3) for t in pOfs]
        pMs = bfB
        P.dma('sp', lambda e: e.dma_start(out=iot[:, :], in_=iota_p[:, :]), [], ['iot'])
        P.dma('sp', lambda e: e.dma_start(out=mkS[:, :], in_=mask_s[:, :]), [], ['mkS'])
        for tl, ky in ((kinb, 'kinb'), (qtok, 'qtok'), (wsS, 'wsS'), (gaT, 'gaT'), (zLs, 'zLs'), (zRs, 'zRs')):
            P.op('dve', lambda e, tl=tl: e.memset(tl[:, :], 0.0), [], [ky])
        for i in range(2):
            P.op('dve', lambda e, i=i: e.memset(kpb[i][:, :], 0.0), [], ['kpb%d' % i])
            P.op('dve', lambda e, i=i: e.memset(VbS[i][:, :, 0:128], 0.0), [], ['VbS%d' % i])
            P.op('dve', lambda e, i=i: e.memset(VbS[i][:, :, 128:129], 1.0), [], ['VbS%d' % i])
        for s in range(4):
            r0 = 8 * s
            P.dma('sp', lambda e, s=s: e.dma_start(out=ptb[:, :], in_=pt_s[s:s + 1, :].partition_broadcast(128)), [], ['ptb'])
            P.op('dve', lambda e: e.tensor_scalar(out=idxi[:, :], in0=ptb[:, :], scalar1=128.0, scalar2=iot[:, 0:1],
                                                  op0=ALU.mult, op1=ALU.add), ['ptb', 'iot'], ['idxi'])
            for srcD, dstT, ky in ((QiS, qiTs, 'qiTs'), (QaS, qaTs, 'qaTs')):
                P.dma('sp', lambda e, srcD=srcD, r0=r0: e.dma_start(out=qtok[0:8, :], in_=srcD[r0:r0 + 8, :]), [srcD is QiS and 'QiS' or 'QaS'], ['qtok'])
                for i in range(8):
                    tr(pk8[:, i, :], qtok[:, i * 128:(i + 1) * 128], ident_b[:, :], ['qtok', 'ident_b'], ['pkS'], inc=(i == 7))
                P.op('act', lambda e, dstT=dstT: e.copy(dstT[:, :, :], pk8[:, :, :]), ['pkS'], [ky])
            P.dma('sp', lambda e, r0=r0: e.dma_start(out=wsS[0:8, :], in_=WsS[r0:r0 + 8, :]), ['WsS'], ['wsS'])
            for h in range(HIDX):
                P.op('dve', lambda e, h=h: e.tensor_scalar(out=diagS[:, h, :], in0=ident_b[:, :], scalar1=wsS[:, h:h + 1],
                                                           scalar2=None, op0=ALU.mult), ['ident_b', 'wsS'], ['diagS'])
            for bl in range(LB):
                u = bl % 2
                if bl < NPG:
                    P.dma('pool', lambda e, u=u, bl=bl: e.indirect_dma_start(
                        out=kig[u][:, :], out_offset=None, in_=cik[:, :],
                        in_offset=bass.IndirectOffsetOnAxis(ap=idxi[:, bl:bl + 1], axis=0)), ['idxi'], ['kig%d' % u])
                    P.op('act', lambda e, u=u: e.copy(kib[u][:, 0:64], kig[u][:, :]), ['kig%d' % u], ['kib%d' % u])
                    P.op('act', lambda e, u=u: e.copy(kib[u][:, 64:128], kig[u][:, :]), ['kig%d' % u], ['kib%d' % u])
                else:
                    P.dma('sp', lambda e, r0=r0: e.dma_start(out=kinb[0:8, :], in_=KiN[r0:r0 + 8, :]), ['KiN'], ['kinb'])
                    P.op('act', lambda e, u=u: e.copy(kib[u][:, 0:64], kinb[:, :]), ['kinb'], ['kib%d' % u])
                    P.op('act', lambda e, u=u: e.copy(kib[u][:, 64:128], kinb[:, :]), ['kinb'], ['kib%d' % u])
                tr(pk8[:, 0, :], kib[u][:, :], ident_b[:, :], ['kib%d' % u, 'ident_b'], ['pkS'])
                P.op('act', lambda e, bl=bl: e.copy(kiT2s[:, bl * 128:(bl + 1) * 128], pk8[:, 0, :]), ['pkS'], ['kiT2s'])
            c0 = 0
            for ci_, wdt in enumerate(widths):
                pib = ci_ % 2
                for m in range(8):
                    a, b = 2 * (m % 2), 2 * (m % 2) + 1
                    mm(pdS[a][:, 0:wdt], qiTs[0:64, m, :], kiT2s[0:64, c0:c0 + wdt], True, True, ['qiTs', 'kiT2s'], ['pdS%d' % a])
                    mm(pdS[b][:, 0:wdt], qiTs[64:128, m, :], kiT2s[64:128, c0:c0 + wdt], True, True, ['qiTs', 'kiT2s'], ['pdS%d' % b])
                    P.op('act', lambda e, a=a, wdt=wdt: e.activation(out=RbS[a][:, 0:wdt], in_=pdS[a][:, 0:wdt], func=AF.Relu),
                         ['pdS%d' % a], ['RbS%d' % a])
                    P.op('dve', lambda e, b=b, wdt=wdt: e.tensor_scalar(out=RbS[b][:, 0:wdt], in0=pdS[b][:, 0:wdt], scalar1=0.0,
                                                                        scalar2=None, op0=ALU.max), ['pdS%d' % b], ['RbS%d' % b])
                    mm(piS[pib][:, 0:wdt], diagS[:, 2 * m, :], RbS[a][:, 0:wdt], m == 0, False, ['diagS', 'RbS%d' % a], ['piS%d' % pib], inc=False)
                    mm(piS[pib][:, 0:wdt], diagS[:, 2 * m + 1, :], RbS[b][:, 0:wdt], False, m == 7, ['diagS', 'RbS%d' % b], ['piS%d' % pib])
                P.op('act', lambda e, pib=pib, c0=c0, wdt=wdt: e.copy(scS[:, c0:c0 + wdt], piS[pib][:, 0:wdt]), ['piS%d' % pib], ['scS'])
                c0 += wdt
            P.op('dve', lambda e: e.tensor_tensor(out=scS[:, NPG * 128:LP], in0=scS[:, NPG * 128:LP], in1=mkS[:, :], op=ALU.add),
                 ['scS', 'mkS'], ['scS'])
            P.barrier()
            P.op('dve', lambda e: e.memset(loS[:, :], -64.0), [], ['loS'])
            for it in range(NBIS):
                w = 64.0 / (2 ** it)
                P.op('dve', lambda e, w=w: e.tensor_scalar(out=midS[:, :], in0=loS[:, :], scalar1=w, scalar2=None, op0=ALU.add),
                     ['loS'], ['midS'])
                P.op('dve', lambda e: e.tensor_scalar(out=cjS[:, :], in0=scS[:, :], scalar1=midS[:, 0:1], scalar2=0.0,
                                                      op0=ALU.is_ge, op1=ALU.add, accum_out=cnS[:, 0:1]), ['scS', 'midS'], ['cjS', 'cnS'])
                P.op('dve', lambda e, w=w: e.tensor_scalar(out=gwS[:, :], in0=cnS[:, :], scalar1=TOPK - 0.5, scalar2=w,
                                                           op0=ALU.is_ge, op1=ALU.mult), ['cnS'], ['gwS'])
                P.op('dve', lambda e: e.tensor_tensor(out=loS[:, :], in0=loS[:, :], in1=gwS[:, :], op=ALU.add), ['gwS', 'loS'], ['loS'])
            P.dma('sp', lambda e, r0=r0: e.dma_start(out=gaT[0:8, :], in_=GaS[r0:r0 + 8, :]), ['GaS'], ['gaT'])
            for i in range(3):
                mm(pOfs[i][:, 0:396], zLs[:, :], zRs[:, 0:396], True, False, ['zLs', 'zRs'], ['pOs'], inc=(i == 2))
            def prep_blk(bl):
                u = bl % 2
                g = bl % 4
                if bl < NPG:
                    P.dma('pool', lambda e, g=g, bl=bl: e.indirect_dma_start(
                        out=kpg[g][:, :], out_offset=None, in_=ck[:, :],
                        in_offset=bass.IndirectOffsetOnAxis(ap=idxi[:, bl:bl + 1], axis=0)), ['idxi'], ['kpg%d' % g])
                    P.dma('pool', lambda e, g=g, bl=bl: e.indirect_dma_start(
                        out=vpg[g][:, :], out_offset=None, in_=cv[:, :],
                        in_offset=bass.IndirectOffsetOnAxis(ap=idxi[:, bl:bl + 1], axis=0)), ['idxi'], ['vpg%d' % g])
                    P.op('act', lambda e, u=u, g=g: e.copy(kpb[u][:, :], kpg[g][:, :]), ['kpg%d' % g], ['kpb%d' % u])
                    P.op('dve', lambda e, u=u, g=g: e.tensor_copy(VbS[u][:, :, 0:128], vpg[g][:, :].rearrange("p (h d) -> p h d", h=HA)),
                         ['vpg%d' % g], ['VbS%d' % u])
                else:
                    P.op('dve', lambda e, u=u: e.memset(kpb[u][:, :], 0.0), [], ['kpb%d' % u])
                    P.op('dve', lambda e, u=u: e.memset(VbS[u][:, :, 0:128], 0.0), [], ['VbS%d' % u])
                    P.dma('sp', lambda e, u=u, r0=r0: e.dma_start(out=kpb[u][0:8, :], in_=KsN[r0:r0 + 8, :]), ['KsN'], ['kpb%d' % u])
                    P.dma('sp', lambda e, u=u, r0=r0: e.dma_start(out=VbS[u][0:8, :, 0:128],
                                                           in_=VsN[r0:r0 + 8, :].rearrange("p (h d) -> p h d", h=HA)),
                          ['VsN'], ['VbS%d' % u])
                for h in range(HA):
                    tr(pKT8[:, h, :], kpb[u][:, h * 128:(h + 1) * 128], ident_b[:, :], ['kpb%d' % u, 'ident_b'], ['pKT'], inc=(h == HA - 1))
                P.op('act', lambda e, u=u: e.copy(KTbS[u][:, :, :], pKT8[:, :, :]), ['pKT'], ['KTbS%d' % u])
                P.op('dve', lambda e, bl=bl: e.tensor_scalar(out=mkbS[:, :], in0=scS[:, bl * 128:(bl + 1) * 128], scalar1=loS[:, 0:1],
                                                             scalar2=None, op0=ALU.is_ge), ['scS', 'loS'], ['mkbS'])
                tr(pMs[:, 0:128], mkbS[:, :], ident_b[:, :], ['mkbS', 'ident_b'], ['pMs'])
                P.op('act', lambda e, u=u: e.copy(mTs[u][:, :], pMs[:, 0:128]), ['pMs'], ['mTs%d' % u])

            def attend_blk(bl):
                u = bl % 2
                for hg in range(2):
                    for h4 in range(4):
                        h = hg * 4 + h4
                        mm(pS4[hg][:, h4, :], KTbS[u][:, h, :], qaTs[:, h, :], True, True, ['KTbS%d' % u, 'qaTs'], ['pSs%d' % hg], inc=(h4 == 3))
                    P.op('act', lambda e, hg=hg: e.activation(out=PeS[hg][:, :, :], in_=pS4[hg][:, :, :], func=AF.Exp, scale=DH ** -0.5),
                         ['pSs%d' % hg], ['PeS%d' % hg])
                    P.op('dve', lambda e, hg=hg, u=u: e.tensor_tensor(out=PmS[hg][:, :, :], in0=PeS[hg][:, :, :],
                                                                      in1=mTs[u][:, None, :].to_broadcast([128, 4, 128]), op=ALU.mult),
                         ['PeS%d' % hg, 'mTs%d' % u], ['PmS%d' % hg])
                    for h4 in range(4):
                        h = hg * 4 + h4
                        mm(pOs[h // 3][:, h % 3, 0:129], PmS[hg][:, h4, :], VbS[u][:, h, 0:129], False, bl == LB - 1,
                           ['PmS%d' % hg, 'VbS%d' % u], ['pOs'], inc=(h4 == 3))

            prep_blk(0)
            for bl in range(LB):
                if bl + 1 < LB:
                    prep_blk(bl + 1)
                attend_blk(bl)
            for h in range(HA):
                P.op('act', lambda e, h=h: e.copy(denS[:, h:h + 1], pOs[h // 3][:, h % 3, 128:129]), ['pOs'], ['denS'])
            P.op('dve', lambda e: e.reciprocal(recS[:, :], denS[:, :]), ['denS'], ['recS'])
            for h in range(HA):
                P.op('dve', lambda e, h=h: e.scalar_tensor_tensor(
                    out=gaT[:, h * 128:(h + 1) * 128], in0=pOs[h // 3][:, h % 3, 0:128], scalar=recS[:, h:h + 1],
                    in1=gaT[:, h * 128:(h + 1) * 128], op0=ALU.mult, op1=ALU.mult), ['pOs', 'recS', 'gaT'], ['gaT'])
            P.dma('sp', lambda e, r0=r0: e.dma_start(out=aS[r0:r0 + 8, :], in_=gaT[0:8, :]), ['gaT'], ['aS'])
            P.barrier()
        P.barrier()
        ed.close()

    eo = ExitStack()
    NTL = NOWN + (1 if SDSA else 0)
    mergedT = sb(eo, "mergedT", [128, KC, NTL * 128], BF16)
    aSt = sb(eo, "aSt", [128, 1024], BF16)
    bSt = sb(eo, "bSt", [128, 4, DVB], BF16)
    gaS6 = sb(eo, "gaS6", [128, NOWN, 1024], BF16)
    P.dma('sp', lambda e: e.dma_start(out=gaS6[:, :, :], in_=gaD.rearrange("(n p) f -> p n f", p=128)), ['gaD'], ['gaS6'])
    yacc = sb(eo, "yacc", [128, NTL, D], F32)
    wost = sb(eo, "wost", [128, KC, 256], F32)
    wobf = [sb(eo, "wobf%d" % i, [128, KC, 256], BF16) for i in range(2)]
    bidx = sb(eo, "bidx_sb", [128, NOWN], I32)
    Bt = [sb(eo, "Bt%d" % i, [128, 4, DVB], BF16) for i in range(2)]
    gfb = sb(eo, "gfb", [128, D], F32)
    yjunk = sb(eo, "yjunk", [128, D], BF16)
    ys = sb(eo, "ys", [128, 4], F32)
    yo = [sb(eo, "yo%d" % i, [128, D], F32) for i in range(2)]
    pT6 = [ps(eo, "pT6%d" % i, [128, 1024], BF16) for i in range(2)]
    py = [ps(eo, "py%d" % i, [128, 512], F32) for i in range(2)]
    P.dma('sp', lambda e: e.dma_start(out=bidx[:, :], in_=bidx_d[:, :]), [], ['bidx'])
    P.dma('sp', lambda e: e.dma_start(out=gfb[:, :], in_=g_f[0:1, :].partition_broadcast(128)), [], ['gfb'])
    P.dma('sp', lambda e: e.dma_start(out=yacc[:, 0:NOWN, :], in_=xo.rearrange("(n p) f -> p n f", p=128)), [], ['yacc'])
    if SDSA:
        P.dma('sp', lambda e: e.dma_start(out=yacc[:, NOWN, :], in_=xsm[:, :]), [], ['yacc'])
        P.op('dve', lambda e: e.memset(aSt[:, :], 0.0), [], ['aSt'])
        P.op('dve', lambda e: e.memset(bSt[:, :, :], 0.0), [], ['bSt'])
        P.dma('sp', lambda e: e.dma_start(out=aSt[0:32, :], in_=aS[0:32, :]), ['aS'], ['aSt'])
        P.dma('sp', lambda e: e.dma_start(out=bSt[0:32, :, :].rearrange("p h v -> p (h v)"), in_=bps[0:32, :]), ['bps'], ['bSt'])
    for k in range(NTL):
        bt = Bt[k % 2] if k < NOWN else bSt
        if k < NOWN:
            P.dma('pool', lambda e, bt=bt, k=k: e.indirect_dma_start(
                out=bt[:, :, :].rearrange("p h v -> p (h v)"), out_offset=None, in_=bp_loc[:, :],
                in_offset=bass.IndirectOffsetOnAxis(ap=bidx[:, k:k + 1], axis=0)),
                ['bp_loc', 'bidx'], ['Bt%d' % (k % 2)])
        for half in range(2):
            p6 = pT6[half].rearrange("p (a b) -> p a b", a=8)
            for i in range(8):
                if half == 0:
                    src_ap = gaS6[:, k, i * 128:(i + 1) * 128] if k < NOWN else aSt[:, i * 128:(i + 1) * 128]
                    rk = ['gaS6' if k < NOWN else 'aSt', 'ident_b']
                else:
                    src_ap = bt[:, i // 2, (i % 2) * 128:(i % 2 + 1) * 128]
                    rk = ['Bt%d' % (k % 2) if k < NOWN else 'bSt', 'ident_b']
                tr(p6[:, i, :], src_ap, ident_b[:, :], rk, ['pT6%d' % half], inc=(i == 7))
            P.op('act' if half == 0 else 'dve',
                 (lambda e, p6=p6, k=k, half=half: e.copy(mergedT[:, half * 8:(half + 1) * 8, k * 128:(k + 1) * 128], p6[:, :, :]))
                 if half == 0 else
                 (lambda e, p6=p6, k=k, half=half: e.tensor_copy(mergedT[:, half * 8:(half + 1) * 8, k * 128:(k + 1) * 128], p6[:, :, :])),
                 ['pT6%d' % half], ['mergedT'])
    wi_ = 0
    yi = 0
    for nb in range(D // 256):
        wb = wi_ % 2
        wi_ += 1
        P.dma('sp', lambda e, nb=nb: e.dma_start(
            out=wost[:, :, :], in_=w_out.rearrange("(k p) c -> p k c", p=128)[:, :, nb * 256:(nb + 1) * 256]), [], ['wost'])
        P.op('pool', lambda e, wb=wb: e.tensor_copy(wobf[wb][:, :, :], wost[:, :, :]), ['wost'], ['wobf%d' % wb])
        for k in range(NTL):
            pb = yi % 2
            yi += 1
            for kc in range(KC):
                mm(py[pb][:, 0:256], mergedT[:, kc, k * 128:(k + 1) * 128], wobf[wb][:, kc, :], kc == 0, kc == KC - 1,
                   ['mergedT', 'wobf%d' % wb], ['py%d' % pb], inc=(kc == KC - 1))
            P.op('dve', lambda e, pb=pb, k=k, nb=nb: e.tensor_tensor(
                out=yacc[:, k, nb * 256:(nb + 1) * 256], in0=yacc[:, k, nb * 256:(nb + 1) * 256], in1=py[pb][:, 0:256],
                op=ALU.add), ['py%d' % pb, 'yacc'], ['yacc'])
    for k in range(NTL):
        o = yo[k % 2]
        P.op('dve', lambda e, k=k: e.scalar_tensor_tensor(out=yjunk[:, :], in0=yacc[:, k, :], scalar=1.0, in1=yacc[:, k, :],
                                                          op0=ALU.mult, op1=ALU.mult, accum_out=ys[:, 0:1]),
             ['yacc'], ['yjunk', 'ys0'])
        P.op('act', lambda e: e.activation(out=ys[:, 1:2], in_=ys[:, 0:1], func=AF.Sqrt, bias=eps_t[:, 0:1], scale=1.0 / D),
             ['ys0', 'eps'], ['ys1'])
        P.op('dve', lambda e: e.reciprocal(ys[:, 2:3], ys[:, 1:2]), ['ys1'], ['ys2'])
        P.op('dve', lambda e, k=k, o=o: e.scalar_tensor_tensor(out=o[:, :], in0=yacc[:, k, :], scalar=ys[:, 2:3], in1=gfb[:, :],
                                                               op0=ALU.mult, op1=ALU.mult), ['yacc', 'ys2', 'gfb'], ['yo%d' % (k % 2)])
        ydst = y_p[k * 128:(k + 1) * 128, :] if k < NOWN else y_s[:, :]
        P.dma('sp', lambda e, ydst=ydst, o=o: e.dma_start(out=ydst, in_=o[:, :]), ['yo%d' % (k % 2)], ['y_p'])
    P.barrier(final=True)
    eo.close()
    P.emit()
    return nc


def _rope_tab(pos, half):
    inv = (10000.0 ** (-np.arange(half, dtype=np.float32) / half)).astype(np.float32)
    ang = pos.astype(np.float32)[:, None] * inv[None, :]
    return np.concatenate([np.cos(ang), np.sin(ang)], axis=1).astype(np.float32)


def own_tiles(S, j):
    return [qb for qb in range(S // 128) if zig(qb) == j]


def make_maps(S, x_prompt, norm_in, w_in, w_gate_up, b_gate, gla_norm, w_out, norm_f, x_sample=None, past=8192,
              state_gla=None, cache_k=None, cache_v=None, cache_idx_k=None, page_table=None):
    NT = S // 128
    NOWN = NT // 4
    w_in = np.asarray(w_in[0], np.float32)
    w_dsa = np.ascontiguousarray(w_in[:, :5200])
    g_in_pk = np.ascontiguousarray(np.asarray(norm_in[0], np.float32).reshape(KC, 128).T)
    pos = np.arange(S)
    csA_p = _rope_tab(pos, 64)
    csI_p = _rope_tab(pos, 32)
    ident = np.eye(128, dtype=np.float32)
    tri = np.triu(np.ones((64, 64), np.float32))
    sd = {}
    if cache_k is not None:
        npool = cache_k.shape[1]
        sd = dict(ck=np.asarray(cache_k[0], np.float32).reshape(npool * 128, HA * DH),
                  cv=np.asarray(cache_v[0], np.float32).reshape(npool * 128, HA * DH),
                  cik=np.asarray(cache_idx_k[0], np.float32).reshape(npool * 128, DIDX),
                  iota_p=np.arange(128, dtype=np.float32).reshape(128, 1))
        ms = np.full((128, 128), NEG, np.float32)
        for q in range(8):
            ms[q, :q + 1] = 0.0
        ms[8:, 0] = 0.0
        sd['mask_s'] = ms
    maps = []
    for c in range(NCORES):
        b, j = c // 4, c % 4
        tiles = own_tiles(S, j)
        rows = np.concatenate([np.arange(t * 128, (t + 1) * 128) for t in tiles])
        xb = np.asarray(x_prompt[b], np.float32)
        w_gla = np.concatenate(
            [w_in[:, o + hh * n:o + (hh + 1) * n] for hh in range(HB)
             for (o, n) in ((C_QB, 128), (C_KB, 128), (C_VB, 256), (C_GB, 256))] + [w_in[:, C_AB:C_AB + 16]], axis=1)
        wgb = np.zeros((32, HB * DKB), np.float32)
        wgb[:16] = np.asarray(w_gate_up[0], np.float32)
        wgb[16] = np.asarray(b_gate[0], np.float32)
        mask_o = np.zeros((NOWN, 128, 512), np.float32)
        for k, qb in enumerate(tiles):
            s0 = min(512 * k, S - 512)
            spos = s0 + np.arange(512)[None, :]
            tpos = qb * 128 + np.arange(128)[:, None]
            mask_o[k] = np.where(spos <= tpos, 0.0, NEG)
        bidx = np.zeros((128, NOWN), np.int32)
        for k, qb in enumerate(tiles):
            bidx[:, k] = qb * 128 + np.arange(128)
        xsm = np.zeros((128, D), np.float32)
        if x_sample is not None:
            xsm[:32] = np.asarray(x_sample, np.float32)[4 * c:4 * c + 4].reshape(32, D)
        spos = np.zeros(128, np.int64)
        spos[:32] = past + (np.arange(32) % 8)
        st_in = np.zeros((4, HB, DKB, DVB), np.float32)
        if state_gla is not None:
            st_in = np.ascontiguousarray(np.asarray(state_gla[0], np.float32)[4 * c:4 * c + 4])
        extra = dict(sd)
        if page_table is not None and cache_k is not None:
            extra['pt_s'] = np.ascontiguousarray(np.asarray(page_table, np.int32)[4 * c:4 * c + 4])
        maps.append(dict(
            st_s_in=st_in, xsm=xsm, **extra, csA_s=_rope_tab(spos, 64), csI_s=_rope_tab(spos, 32),
            xp=xb, xo=np.ascontiguousarray(xb[rows]), w_dsa=w_dsa, w_gla=np.ascontiguousarray(w_gla),
            w_out=np.asarray(w_out[0], np.float32), g_in_pk=g_in_pk,
            g_f=np.asarray(norm_f, np.float32).reshape(1, D), gla_g=np.asarray(gla_norm[0], np.float32).reshape(1, DVB),
            wgb=wgb, csA_p=csA_p, csI_p=csI_p, csA_o=np.ascontiguousarray(csA_p[rows]),
            csI_o=np.ascontiguousarray(csI_p[rows]), ident=ident, tri=tri, mask_o=mask_o, bidx=bidx))
    return maps


_STAGE = 99


def kernel(x_prompt, x_sample, cache_k, cache_v, cache_idx_k, state_gla, page_table,
           norm_in, w_in, w_gate_up, b_gate, gla_norm, w_out, norm_f):
    B, S = x_prompt.shape[0], x_prompt.shape[1]
    Bd, Ts = x_sample.shape[0], x_sample.shape[1]
    past = page_table.shape[1] * 128
    npg, npool = page_table.shape[1], cache_k.shape[1]
    nc = build(S, stage=_STAGE, NPG=npg, NPOOL=npool)
    maps = make_maps(S, x_prompt, norm_in, w_in, w_gate_up, b_gate, gla_norm, w_out, norm_f,
                     x_sample=x_sample, past=past, state_gla=state_gla,
                     cache_k=cache_k, cache_v=cache_v, cache_idx_k=cache_idx_k, page_table=page_table)
    res = run_bass_kernel_spmd(nc, maps, core_ids=list(range(NCORES))).results
    y_prompt = np.zeros((B, S, D), np.float32)
    nk = np.zeros((1, B, S, HA, DH), np.float32)
    nv = np.zeros((1, B, S, HA, DH), np.float32)
    nik = np.zeros((1, B, S, DIDX), np.float32)
    st = np.zeros((1, B, HB, DKB, DVB), np.float32)
    y_sample = np.zeros((Bd, Ts, D), np.float32)
    nks = np.zeros((1, Bd, Ts, HA, DH), np.float32)
    nvs = np.zeros((1, Bd, Ts, HA, DH), np.float32)
    niks = np.zeros((1, Bd, Ts, DIDX), np.float32)
    sts = np.zeros((1, Bd, HB, DKB, DVB), np.float32)
    for c in range(NCORES):
        b, j = c // 4, c % 4
        r = res[c]
        for k, qb in enumerate(own_tiles(S, j)):
            sl = slice(qb * 128, (qb + 1) * 128)
            y_prompt[b, sl] = r["y_p"][k * 128:(k + 1) * 128]
            nk[0, b, sl] = r["nk_p"][k * 128:(k + 1) * 128].reshape(128, HA, DH)
            nv[0, b, sl] = r["nv_p"][k * 128:(k + 1) * 128].reshape(128, HA, DH)
            nik[0, b, sl] = r["nik_p"][k * 128:(k + 1) * 128]
        if j == 0:
            st[0, b] = r["st_p"]
        nks[0, 4 * c:4 * c + 4] = r["nk_s"][:32].reshape(4, Ts, HA, DH)
        nvs[0, 4 * c:4 * c + 4] = r["nv_s"][:32].reshape(4, Ts, HA, DH)
        niks[0, 4 * c:4 * c + 4] = r["nik_s"][:32].reshape(4, Ts, DIDX)
        sts[0, 4 * c:4 * c + 4] = r["st_s"]
        y_sample[4 * c:4 * c + 4] = r["y_s"][:32].reshape(4, Ts, D)
    return (y_prompt, y_sample, nk, nv, nik, st, nks, nvs, niks, sts)
```

```python
import numpy as np
import concourse.bass as bass
import concourse.mybir as mybir
from concourse.bass_utils import run_bass_kernel_spmd

F32 = mybir.dt.float32
BF16 = mybir.dt.bfloat16
I32 = mybir.dt.int32
ALU = mybir.AluOpType
AF = mybir.ActivationFunctionType
AX = mybir.AxisListType

D = 2048
KC = 16
HA, DH = 8, 128
HIDX, DIDX = 16, 64
HB, DKB, DVB = 4, 128, 256
RANK = 16
TOPK = 256
EPS = 1e-6
NCORES = 8
C_QA, C_KA, C_VA, C_GA, C_QI, C_KI, C_WI, C_QB, C_KB, C_VB, C_GB, C_AB = (
    0, 1024, 2048, 3072, 4096, 5120, 5184, 5200, 5712, 6224, 7248, 8272)
INW = 8288
NEG = -1.0e30
NBIS = 26


def zig(qb):
    r = qb % 8
    return r if r < 4 else 7 - r


class Prog:
    NS = 10

    def __init__(self, nc):
        self.nc = nc
        self.names = ['pe', 'act', 'dve', 'pool', 'sp']
        self.ops = {k: [] for k in self.names}
        self.cnt = {k: 0 for k in self.names}
        self.waited = {k: {} for k in self.names}
        self.last_w = {}
        self.readers = {}
        self.sems = {}
        self.dcount = {k: 0 for k in self.names}
        self.dval = {}
        for k in self.names:
            self.sems['E:' + k] = nc.alloc_semaphore('e_' + k)
        for q in ('sp', 'pool', 'act'):
            for s in range(self.NS):
                key = 'D:%s:%d' % (q, s)
                self.sems[key] = nc.alloc_semaphore('d_%s_%d' % (q, s))
                self.dval[key] = 0

    def _deps(self, eng, reads, writes):
        toks = []
        for b in list(reads) + list(writes):
            t = self.last_w.get(b)
            if t is not None:
                toks.append(t)
        for b in writes:
            toks.extend(self.readers.get(b, ()))
        need = {}
        for (s, v) in toks:
            if eng == 'pe' and s == 'E:pe':
                continue
            if self.waited[eng].get(s, 0) >= v:
                continue
            if need.get(s, 0) < v:
                need[s] = v
        for s, v in need.items():
            self.waited[eng][s] = v
        return list(need.items())

    def _commit(self, tok, reads, writes):
        for b in writes:
            self.last_w[b] = tok
            self.readers[b] = []
        for b in reads:
            self.readers.setdefault(b, []).append(tok)

    def op(self, eng, fn, reads=(), writes=(), inc=True):
        waits = self._deps(eng, reads, writes)
        tok = ('E:' + eng, self.cnt[eng] + 1)
        if inc:
            self.cnt[eng] += 1
        sems = self.sems
        esem = sems['E:' + eng]

        def run(e, waits=waits, fn=fn, inc=inc):
            for s, v in waits:
                e.wait_ge(sems[s], v)
            ins = fn(e)
            if inc:
                ins.then_inc(esem, 1)
        self.ops[eng].append(run)
        self._commit(tok, reads, writes)
        return tok

    def dma(self, q, fn, reads=(), writes=()):
        i = self.dcount[q]
        self.dcount[q] += 1
        key = 'D:%s:%d' % (q, i % self.NS)
        waits = self._deps(q, reads, writes)
        prev = self.dval[key]
        if prev > 0 and self.waited[q].get(key, 0) < prev:
            waits.append((key, prev))
            self.waited[q][key] = prev
        self.dval[key] = prev + 16
        tok = (key, prev + 16)
        sems = self.sems

        def run(e, waits=waits, fn=fn, key=key):
            for s, v in waits:
                e.wait_ge(sems[s], v)
            fn(e).then_inc(sems[key], 16)
        self.ops[q].append(run)
        self._commit(tok, reads, writes)
        return tok

    def barrier(self, final=False):
        allt = [('E:' + k, self.cnt[k]) for k in self.names if self.cnt[k] > 0]
        allt += [(k, v) for k, v in self.dval.items() if v > 0]
        engs = ['sp'] if final else self.names
        for eng in engs:
            waits = []
            for s, v in allt:
                if s == 'E:' + eng:
                    continue
                if self.waited[eng].get(s, 0) < v:
                    waits.append((s, v))
                    self.waited[eng][s] = v
            sems = self.sems

            def run(e, waits=waits):
                for s, v in waits:
                    e.wait_ge(sems[s], v)
            self.ops[eng].append(run)

    def emit(self):
        nc = self.nc
        ops = self.ops
        with nc.Block() as blk:
            @blk.sync
            def _(e):
                for f in ops['sp']:
                    f(e)

            @blk.tensor
            def _(e):
                for f in ops['pe']:
                    f(e)

            @blk.scalar
            def _(e):
                for f in ops['act']:
                    f(e)

            @blk.vector
            def _(e):
                for f in ops['dve']:
                    f(e)

            @blk.gpsimd
            def _(e):
                for f in ops['pool']:
                    f(e)


def build(S, stage=99, NPG=0, NPOOL=0):
    SDSA = NPG > 0
    from contextlib import ExitStack
    NT = S // 128
    NOWN = NT // 4
    NO = NOWN * 128
    GT = min(NT, 8)
    NG = NT // GT
    NCH = S // 64
    LK = [min(S, 512 * (k + 1)) for k in range(NOWN)]
    SOFF = [sum(LK[:k]) for k in range(NOWN)]
    STOT = sum(LK)
    nc = bass.Bass("TRN2", target_bir_lowering=False)
    P = Prog(nc)

    def din(name, shape, dt=F32):
        return nc.dram_tensor(name, list(shape), dt, kind="ExternalInput").ap()

    def dout(name, shape, dt=F32):
        return nc.dram_tensor(name, list(shape), dt, kind="ExternalOutput").ap()

    def dscr(name, shape, dt):
        return nc.dram_tensor(name, list(shape), dt, kind="Internal").ap()

    xp = din("xp", [S, D])
    xo = din("xo", [NO, D])
    w_dsa = din("w_dsa", [D, 5200])
    w_gla = din("w_gla", [D, HB * 768 + 16])
    w_out = din("w_out", [D, D])
    g_in_pk = din("g_in_pk", [128, KC])
    g_f = din("g_f", [1, D])
    gla_g = din("gla_g", [1, DVB])
    wgb = din("wgb", [32, HB * DKB])
    csA_p = din("csA_p", [S, 128])
    csI_p = din("csI_p", [S, 64])
    csA_o = din("csA_o", [NO, 128])
    csI_o = din("csI_o", [NO, 64])
    ident_d = din("ident", [128, 128])
    tri_d = din("tri", [64, 64])
    mask_o = din("mask_o", [NOWN, 128, 512])
    bidx_d = din("bidx", [128, NOWN], I32)
    xsm = din("xsm", [128, D])
    csA_s = din("csA_s", [128, 128])
    csI_s = din("csI_s", [128, 64])
    st_s_in = din("st_s_in", [4, HB, DKB, DVB])
    if SDSA:
        ck = din("ck", [NPOOL * 128, HA * DH])
        cv = din("cv", [NPOOL * 128, HA * DH])
        cik = din("cik", [NPOOL * 128, DIDX])
        pt_s = din("pt_s", [4, NPG], I32)
        iota_p = din("iota_p", [128, 1])
        mask_s = din("mask_s", [128, 128])

    y_p = dout("y_p", [NO, D])
    nk_p = dout("nk_p", [NO, HA * DH])
    nv_p = dout("nv_p", [NO, HA * DH])
    nik_p = dout("nik_p", [NO, DIDX])
    st_p = dout("st_p", [HB, DKB, DVB])
    nk_s = dout("nk_s", [128, HA * DH])
    nv_s = dout("nv_s", [128, HA * DH])
    nik_s = dout("nik_s", [128, DIDX])
    st_s = dout("st_s", [4, HB, DKB, DVB])
    y_s = dout("y_s", [128, D])

    KT = dscr("KT", [HA, 128, S], BF16)
    Vs = dscr("Vs", [S, HA * DH], BF16)
    Vg = dscr("Vg", [S, HB * DVB], BF16)
    Gg = dscr("Gg", [S, HB * DVB], BF16)
    QTg = dscr("QTg", [HB, 128, S], BF16)
    KTg = dscr("KTg", [HB, 128, S], BF16)
    bp_loc = dscr("bp_loc", [S, HB * DVB], BF16)
    QTs = dscr("QTs", [HB, 128, 128], BF16)
    KTs = dscr("KTs", [HB, 128, 128], BF16)
    Vgs = dscr("Vgs", [128, HB * DVB], BF16)
    Ggs = dscr("Ggs", [128, HB * DVB], BF16)
    bps = dscr("bps", [128, HB * DVB], BF16)
    QaS = dscr("QaS", [128, 1024], BF16)
    QiS = dscr("QiS", [128, 1024], BF16)
    GaS = dscr("GaS", [128, 1024], BF16)
    KsN = dscr("KsN", [128, 1024], BF16)
    VsN = dscr("VsN", [128, 1024], BF16)
    KiN = dscr("KiN", [128, 64], BF16)
    WsS = dscr("WsS", [128, HIDX], F32)
    aS = dscr("aS", [128, 1024], BF16)
    gaD = dscr("gaD", [NO, 1024], BF16)

    es0 = ExitStack()

    def sb(es, name, shape, dt):
        return es.enter_context(nc.sbuf_tensor(name, list(shape), dt))

    def ps(es, name, shape, dt=F32):
        return es.enter_context(nc.psum_tensor(name, list(shape), dt))

    def mm(out, lhsT, rhs, start, stop, reads, writes, inc=True):
        P.op('pe', lambda e: e.matmul(out, lhsT, rhs, start=start, stop=stop), reads, writes, inc)

    def tr(out, in_, ident, reads, writes, inc=True):
        P.op('pe', lambda e: e.transpose(out, in_, ident), reads, writes, inc)

    g_pk = sb(es0, "g_pk", [128, KC], F32)
    ident_f = sb(es0, "ident_f", [128, 128], F32)
    ident_b = sb(es0, "ident_b", [128, 128], BF16)
    tri_b = sb(es0, "tri_b", [64, 64], F32)
    abT1 = sb(es0, "abT1", [32, S], BF16)
    eps_t = sb(es0, "eps_t", [128, 1], F32)
    abT1s = sb(es0, "abT1s", [32, 128], BF16)

    P.dma('sp', lambda e: e.dma_start(out=g_pk[:, :], in_=g_in_pk[:, :]), [], ['g_pk'])
    P.dma('sp', lambda e: e.dma_start(out=ident_f[:, :], in_=ident_d[:, :]), [], ['ident_f'])
    P.dma('sp', lambda e: e.dma_start(out=tri_b[:, :], in_=tri_d[:, :]), [], ['tri'])
    P.op('dve', lambda e: e.tensor_copy(ident_b[:, :], ident_f[:, :]), ['ident_f'], ['ident_b'])
    P.op('dve', lambda e: e.memset(abT1[:, :], 1.0), [], ['abT1'])
    P.op('dve', lambda e: e.memset(eps_t[:, :], EPS), [], ['eps'])
    P.op('dve', lambda e: e.memset(abT1s[:, :], 1.0), [], ['abT1s'])

    es1 = ExitStack()
    kiT2 = sb(es1, "kiT2", [128, S], BF16)
    gaS = sb(es1, "gaS", [128, NOWN, 1024], BF16)
    qaT = sb(es1, "qaT", [128, HA, NO], BF16)
    qiT = sb(es1, "qiT", [128, 8, NO], BF16)
    wabs = sb(es1, "wabs", [128, NOWN + 1, HIDX], F32)
    wsgn = sb(es1, "wsgn", [128, NOWN + 1, HIDX], F32)

    ep = ExitStack()
    xT = sb(ep, "xT", [128, KC, (GT + 1) * 128], BF16)
    xs = [sb(ep, "xs%d" % i, [128, D], F32) for i in range(2)]
    xh = sb(ep, "xh", [128, D], BF16)
    junk = sb(ep, "junk", [128, D], BF16)
    ss = sb(ep, "ss", [128, 4], F32)
    wst2 = [sb(ep, "wst%d" % i, [128, KC, 256], F32) for i in range(2)]
    wbf = [sb(ep, "wbf%d" % i, [128, KC, 256], BF16) for i in range(2)]
    zf = [sb(ep, "zf%d" % i, [128, 256], F32) for i in range(2)]
    zr = [sb(ep, "zr%d" % i, [128, 256], F32) for i in range(2)]
    zb = [sb(ep, "zb%d" % i, [128, 256], BF16) for i in range(2)]
    tp = [sb(ep, "tp%d" % i, [128, 256], F32) for i in range(2)]
    csA = sb(ep, "csA", [128, GT + 1, 128], F32)
    csI = sb(ep, "csI", [128, GT + 1, 64], F32)
    ktst = [sb(ep, "ktst%d" % i, [128, 2, 128], BF16) for i in range(2)]
    ftst = [sb(ep, "ftst%d" % i, [128, 512], BF16) for i in range(2)]
    pz = [ps(ep, "pz%d" % i, [128, 512], F32) for i in range(2)]
    pt = [ps(ep, "pt%d" % i, [128, 8, 128], BF16) for i in range(2)]
    pkf = [ps(ep, "pk%d" % i, [128, 1024], BF16) for i in range(2)]
    pk = [t[:, 0:256].rearrange("p (a b) -> p a b", a=2) for t in pkf]
    cnt = {'x': 0, 'w': 0, 'z': 0}

    def build_xT(src, row0, col, slot):
        b = cnt['x'] % 2
        cnt['x'] += 1
        xsb = xs[b]
        P.dma('sp', lambda e: e.dma_start(out=xsb[:, :], in_=src[row0:row0 + 128, :]), [], ['xs%d' % b])
        P.op('dve', lambda e: e.scalar_tensor_tensor(out=junk[:, :], in0=xsb[:, :], scalar=1.0, in1=xsb[:, :],
                                                      op0=ALU.mult, op1=ALU.mult, accum_out=ss[:, 0:1]),
             ['xs%d' % b], ['junk', 'ss0'])
        P.op('act', lambda e: e.activation(out=ss[:, 1:2], in_=ss[:, 0:1], func=AF.Sqrt,
                                            bias=eps_t[:, 0:1], scale=1.0 / D), ['ss0', 'eps'], ['ss1'])
        P.op('dve', lambda e: e.reciprocal(ss[:, 2:3], ss[:, 1:2]), ['ss1'], ['ss2'])
        P.op('act', lambda e: e.activation(out=xh[:, :], in_=xsb[:, :], func=AF.Copy, scale=ss[:, 2:3]),
             ['xs%d' % b, 'ss2'], ['xh'])
        for half in range(2):
            pb = pt[half]
            for i in range(8):
                kc = half * 8 + i
                tr(pb[:, i, :], xh[:, kc * 128:(kc + 1) * 128], ident_b[:, :],
                   ['xh', 'ident_b'], ['pt%d' % half], inc=(i == 7))
            eng = 'act' if half == 0 else 'dve'
            dst = xT[:, half * 8:(half + 1) * 8, col * 128:(col + 1) * 128]
            if eng == 'act':
                P.op('act', lambda e, dst=dst, pb=pb: e.copy(dst, pb[:, :, :]), ['pt%d' % half], ['xT'])
            else:
                P.op('dve', lambda e, dst=dst, pb=pb: e.tensor_copy(dst, pb[:, :, :]), ['pt%d' % half], ['xT'])

    def load_w(wsrc, c0, ncol):
        b = cnt['w'] % 2
        cnt['w'] += 1
        src = wsrc.rearrange("(k p) c -> p k c", p=128)[:, :, c0:c0 + ncol]
        wst = wst2[b]
        P.dma('sp', lambda e: e.dma_start(out=wst[:, :, 0:ncol], in_=src), [], ['wst%d' % b])
        wb = wbf[b]
        P.op('pool', lambda e: e.tensor_tensor(out=wb[:, :, 0:ncol], in0=wst[:, :, 0:ncol],
                                               in1=g_pk[:, :, None].to_broadcast([128, KC, ncol]), op=ALU.mult),
             ['wst%d' % b, 'g_pk'], ['wbf%d' % b])
        return b

    def proj_tok(wb, ncol, col, m=128):
        z = cnt['z'] % 2
        cnt['z'] += 1
        for kc in range(KC):
            mm(pz[z][0:m, 0:ncol], xT[:, kc, col * 128:col * 128 + m], wbf[wb][:, kc, 0:ncol],
               kc == 0, kc == KC - 1, ['xT', 'wbf%d' % wb], ['pz%d' % z], inc=(kc == KC - 1))
        return z

    def rope(z, nh, half, cs_ap, zi):
        w = nh * 2 * half
        src = zf[zi]
        P.op('act', lambda e: e.copy(src[:, 0:w], pz[z][:, 0:w]), ['pz%d' % z], ['zf%d' % zi])
        sv = src[:, 0:w].rearrange("p (h two f) -> p h two f", h=nh, two=2)
        dv = zr[zi][:, 0:w].rearrange("p (h two f) -> p h two f", h=nh, two=2)
        t1 = tp[0][:, 0:nh * half].rearrange("p (h f) -> p h f", h=nh)
        t2 = tp[1][:, 0:nh * half].rearrange("p (h f) -> p h f", h=nh)
        cosb = cs_ap[:, None, 0:half].to_broadcast([128, nh, half])
        sinb = cs_ap[:, None, half:2 * half].to_broadcast([128, nh, half])
        x1, x2 = sv[:, :, 0, :], sv[:, :, 1, :]
        rk = ['zf%d' % zi, 'cs']
        P.op('dve', lambda e: e.tensor_tensor(out=t1, in0=x1, in1=cosb, op=ALU.mult), rk, ['tp0'])
        P.op('dve', lambda e: e.tensor_tensor(out=t2, in0=x2, in1=sinb, op=ALU.mult), rk, ['tp1'])
        P.op('dve', lambda e: e.tensor_tensor(out=dv[:, :, 0, :], in0=t1, in1=t2, op=ALU.subtract),
             ['tp0', 'tp1'], ['zr%d' % zi])
        P.op('dve', lambda e: e.tensor_tensor(out=t1, in0=x2, in1=cosb, op=ALU.mult), rk, ['tp0'])
        P.op('dve', lambda e: e.tensor_tensor(out=t2, in0=x1, in1=sinb, op=ALU.mult), rk, ['tp1'])
        P.op('dve', lambda e: e.tensor_tensor(out=dv[:, :, 1, :], in0=t1, in1=t2, op=ALU.add),
             ['tp0', 'tp1'], ['zr%d' % zi])

    zc = {'i': 0}

    def nextz():
        zc['i'] += 1
        return zc['i'] % 2

    for gi in range(NG):
        P.dma('sp', lambda e, gi=gi: e.dma_start(
            out=csA[:, 0:GT, :], in_=csA_p[gi * GT * 128:(gi + 1) * GT * 128, :].rearrange("(n p) f -> p n f", p=128)),
            [], ['cs'])
        P.dma('sp', lambda e, gi=gi: e.dma_start(
            out=csI[:, 0:GT, :], in_=csI_p[gi * GT * 128:(gi + 1) * GT * 128, :].rearrange("(n p) f -> p n f", p=128)),
            [], ['cs'])
        for t in range(GT):
            build_xT(xp, (gi * GT + t) * 128, t, 0)
        for blk in range(4):
            wb = load_w(w_dsa, C_KA + blk * 256, 256)
            for t in range(GT):
                tt = gi * GT + t
                z = proj_tok(wb, 256, t)
                zi = nextz()
                rope(z, 2, 64, csA[:, t, :], zi)
                P.op('act', lambda e, zi=zi: e.copy(zb[zi][:, :], zr[zi][:, :]), ['zr%d' % zi], ['zb%d' % zi])
                k2 = zi
                for h in range(2):
                    tr(pk[k2][:, h, :], zb[zi][:, h * 128:(h + 1) * 128], ident_b[:, :],
                       ['zb%d' % zi, 'ident_b'], ['pk%d' % k2], inc=(h == 1))
                P.op('act', lambda e, k2=k2: e.copy(ktst[k2][:, :, :], pk[k2][:, :, :]), ['pk%d' % k2], ['ktst%d' % k2])
                P.dma('act', lambda e, k2=k2, blk=blk, tt=tt: e.dma_start(
                    out=KT[blk * 2:blk * 2 + 2, :, tt * 128:(tt + 1) * 128].rearrange("h d s -> d h s"),
                    in_=ktst[k2][:, :, :]), ['ktst%d' % k2], ['KT'])
        for blk in range(4):
            wb = load_w(w_dsa, C_VA + blk * 256, 256)
            for t in range(GT):
                tt = gi * GT + t
                z = proj_tok(wb, 256, t)
                zi = nextz()
                P.op('act', lambda e, zi=zi, z=z: e.copy(zb[zi][:, :], pz[z][:, 0:256]), ['pz%d' % z], ['zb%d' % zi])
                P.dma('act', lambda e, zi=zi, blk=blk, tt=tt: e.dma_start(
                    out=Vs[tt * 128:(tt + 1) * 128, blk * 256:(blk + 1) * 256], in_=zb[zi][:, :]),
                    ['zb%d' % zi], ['Vs'])
        wb = load_w(w_dsa, C_KI, 64)
        for t in range(GT):
            tt = gi * GT + t
            z = proj_tok(wb, 64, t)
            zi = nextz()
            rope(z, 1, 32, csI[:, t, :], zi)
            P.op('act', lambda e, zi=zi: e.copy(zb[zi][:, 0:64], zr[zi][:, 0:64]), ['zr%d' % zi], ['zb%d' % zi])
            P.op('act', lambda e, zi=zi: e.copy(zb[zi][:, 64:128], zr[zi][:, 0:64]), ['zr%d' % zi], ['zb%d' % zi])
            tr(pk[zi][:, 0, :], zb[zi][:, 0:128], ident_b[:, :], ['zb%d' % zi, 'ident_b'], ['pk%d' % zi])
            P.op('act', lambda e, zi=zi, tt=tt: e.copy(kiT2[:, tt * 128:(tt + 1) * 128], pk[zi][:, 0, :]),
                 ['pk%d' % zi], ['kiT2'])
        for hh in range(HB):
            wb = load_w(w_gla, hh * 768, 256)
            for which, dstD, scl in ((0, QTg, DKB ** -0.5), (1, KTg, 1.0)):
                for c4 in range(GT * 128 // 512):
                    z = cnt['z'] % 2
                    cnt['z'] += 1
                    for kc in range(KC):
                        mm(pz[z][:, 0:512], wbf[wb][:, kc, which * 128:(which + 1) * 128],
                           xT[:, kc, c4 * 512:(c4 + 1) * 512], kc == 0, kc == KC - 1,
                           ['xT', 'wbf%d' % wb], ['pz%d' % z], inc=(kc == KC - 1))
                    t0 = gi * GT * 128 + c4 * 512
                    fi = nextz()
                    P.op('act', lambda e, z=z, fi=fi, scl=scl: e.activation(
                        out=ftst[fi][:, :], in_=pz[z][:, 0:512], func=AF.Copy, scale=scl),
                        ['pz%d' % z], ['ftst%d' % fi])
                    P.dma('act', lambda e, fi=fi, dstD=dstD, hh=hh, t0=t0: e.dma_start(
                        out=dstD[hh, :, t0:t0 + 512], in_=ftst[fi][:, :]), ['ftst%d' % fi], ['QKTg'])
            wb = load_w(w_gla, hh * 768 + 256, 256)
            for t in range(GT):
                tt = gi * GT + t
                z = proj_tok(wb, 256, t)
                zi = nextz()
                P.op('act', lambda e, zi=zi, z=z: e.copy(zb[zi][:, :], pz[z][:, 0:256]), ['pz%d' % z], ['zb%d' % zi])
                P.dma('act', lambda e, zi=zi, tt=tt, hh=hh: e.dma_start(
                    out=Vg[tt * 128:(tt + 1) * 128, hh * DVB:(hh + 1) * DVB], in_=zb[zi][:, :]), ['zb%d' % zi], ['Vg'])
            wb = load_w(w_gla, hh * 768 + 512, 256)
            for t in range(GT):
                tt = gi * GT + t
                z = proj_tok(wb, 256, t)
                zi = nextz()
                P.op('act', lambda e, zi=zi, z=z: e.activation(out=zb[zi][:, :], in_=pz[z][:, 0:256], func=AF.Silu),
                     ['pz%d' % z], ['zb%d' % zi])
                P.dma('act', lambda e, zi=zi, tt=tt, hh=hh: e.dma_start(
                    out=Gg[tt * 128:(tt + 1) * 128, hh * DVB:(hh + 1) * DVB], in_=zb[zi][:, :]), ['zb%d' % zi], ['Gg'])
        wb = load_w(w_gla, HB * 768, 16)
        for c4 in range(GT * 128 // 512):
            z = cnt['z'] % 2
            cnt['z'] += 1
            for kc in range(KC):
                mm(pz[z][0:16, 0:512], wbf[wb][:, kc, 0:16], xT[:, kc, c4 * 512:(c4 + 1) * 512],
                   kc == 0, kc == KC - 1, ['xT', 'wbf%d' % wb], ['pz%d' % z], inc=(kc == KC - 1))
            t0 = gi * GT * 128 + c4 * 512
            P.op('act', lambda e, z=z, t0=t0: e.copy(abT1[0:16, t0:t0 + 512], pz[z][0:16, 0:512]),
                 ['pz%d' % z], ['abT1'])

    assert NOWN <= GT
    P.dma('sp', lambda e: e.dma_start(out=csA[:, 0:NOWN, :], in_=csA_o.rearrange("(n p) f -> p n f", p=128)), [], ['cs'])
    P.dma('sp', lambda e: e.dma_start(out=csI[:, 0:NOWN, :], in_=csI_o.rearrange("(n p) f -> p n f", p=128)), [], ['cs'])
    P.dma('sp', lambda e: e.dma_start(out=csA[:, NOWN, :], in_=csA_s[:, :]), [], ['cs'])
    P.dma('sp', lambda e: e.dma_start(out=csI[:, NOWN, :], in_=csI_s[:, :]), [], ['cs'])
    for t in range(NOWN):
        build_xT(xo, t * 128, t, 0)
    build_xT(xsm, 0, NOWN, 0)
    wb = load_w(w_dsa, C_WI, 16)
    for t in range(NOWN + 1):
        z = proj_tok(wb, 16, t)
        P.op('act', lambda e, z=z, t=t: e.activation(out=wsgn[:, t, :], in_=pz[z][:, 0:16], func=AF.Sign),
             ['pz%d' % z], ['wsgn'])
        P.op('act', lambda e, z=z, t=t: e.activation(out=wabs[:, t, :], in_=pz[z][:, 0:16], func=AF.Abs,
                                                     scale=(HIDX ** -0.5) * (DIDX ** -0.5)),
             ['pz%d' % z], ['wabs'])
    for blk in range(4):
        wb = load_w(w_dsa, C_QA + blk * 256, 256)
        for t in range(NOWN + 1):
            z = proj_tok(wb, 256, t)
            zi = nextz()
            rope(z, 2, 64, csA[:, t, :], zi)
            P.op('act', lambda e, zi=zi: e.copy(zb[zi][:, :], zr[zi][:, :]), ['zr%d' % zi], ['zb%d' % zi])
            if t == NOWN:
                P.dma('act', lambda e, zi=zi, blk=blk: e.dma_start(out=QaS[:, blk * 256:(blk + 1) * 256], in_=zb[zi][:, :]),
                      ['zb%d' % zi], ['QaS'])
                continue
            for h in range(2):
                tr(pk[zi][:, h, :], zb[zi][:, h * 128:(h + 1) * 128], ident_b[:, :],
                   ['zb%d' % zi, 'ident_b'], ['pk%d' % zi], inc=(h == 1))
            P.op('act', lambda e, zi=zi, blk=blk, t=t: e.copy(
                qaT[:, blk * 2:blk * 2 + 2, t * 128:(t + 1) * 128], pk[zi][:, :, :]), ['pk%d' % zi], ['qaT'])
    for blk in range(4):
        wb = load_w(w_dsa, C_KA + blk * 256, 256)
        for t in range(NOWN + 1):
            z = proj_tok(wb, 256, t)
            zi = nextz()
            rope(z, 2, 64, csA[:, t, :], zi)
            dst = nk_p[t * 128:(t + 1) * 128, blk * 256:(blk + 1) * 256] if t < NOWN else nk_s[:, blk * 256:(blk + 1) * 256]
            P.dma('act', lambda e, zi=zi, dst=dst: e.dma_start(out=dst, in_=zr[zi][:, :]), ['zr%d' % zi], ['nk_p'])
            if t == NOWN:
                P.op('act', lambda e, zi=zi: e.copy(zb[zi][:, :], zr[zi][:, :]), ['zr%d' % zi], ['zb%d' % zi])
                P.dma('act', lambda e, zi=zi, blk=blk: e.dma_start(out=KsN[:, blk * 256:(blk + 1) * 256], in_=zb[zi][:, :]),
                      ['zb%d' % zi], ['KsN'])
    for blk in range(4):
        wb = load_w(w_dsa, C_VA + blk * 256, 256)
        for t in range(NOWN + 1):
            z = proj_tok(wb, 256, t)
            zi = nextz()
            P.op('act', lambda e, zi=zi, z=z: e.copy(zr[zi][:, :], pz[z][:, 0:256]), ['pz%d' % z], ['zr%d' % zi])
            dst = nv_p[t * 128:(t + 1) * 128, blk * 256:(blk + 1) * 256] if t < NOWN else nv_s[:, blk * 256:(blk + 1) * 256]
            P.dma('act', lambda e, zi=zi, dst=dst: e.dma_start(out=dst, in_=zr[zi][:, :]), ['zr%d' % zi], ['nv_p'])
            if t == NOWN:
                P.op('act', lambda e, zi=zi: e.copy(zb[zi][:, :], zr[zi][:, :]), ['zr%d' % zi], ['zb%d' % zi])
                P.dma('act', lambda e, zi=zi, blk=blk: e.dma_start(out=VsN[:, blk * 256:(blk + 1) * 256], in_=zb[zi][:, :]),
                      ['zb%d' % zi], ['VsN'])
    for blk in range(4):
        wb = load_w(w_dsa, C_GA + blk * 256, 256)
        for t in range(NOWN + 1):
            z = proj_tok(wb, 256, t)
            if t == NOWN:
                zi = nextz()
                P.op('act', lambda e, z=z, zi=zi: e.activation(out=zb[zi][:, :], in_=pz[z][:, 0:256], func=AF.Silu),
                     ['pz%d' % z], ['zb%d' % zi])
                P.dma('act', lambda e, zi=zi, blk=blk: e.dma_start(out=GaS[:, blk * 256:(blk + 1) * 256], in_=zb[zi][:, :]),
                      ['zb%d' % zi], ['GaS'])
                continue
            P.op('act', lambda e, z=z, blk=blk, t=t: e.activation(
                out=gaS[:, t, blk * 256:(blk + 1) * 256], in_=pz[z][:, 0:256], func=AF.Silu), ['pz%d' % z], ['gaS'])
    for blk in range(4):
        wb = load_w(w_dsa, C_QI + blk * 256, 256)
        for t in range(NOWN + 1):
            z = proj_tok(wb, 256, t)
            zi = nextz()
            rope(z, 4, 32, csI[:, t, :], zi)
            P.op('dve', lambda e, zi=zi, blk=blk, t=t: e.tensor_tensor(
                out=zb[zi][:, :].rearrange("p (h f) -> p h f", h=4),
                in0=zr[zi][:, :].rearrange("p (h f) -> p h f", h=4),
                in1=wabs[:, t, blk * 4:(blk + 1) * 4, None].to_broadcast([128, 4, 64]), op=ALU.mult),
                ['zr%d' % zi, 'wabs'], ['zb%d' % zi])
            if t == NOWN:
                P.dma('act', lambda e, zi=zi, blk=blk: e.dma_start(out=QiS[:, blk * 256:(blk + 1) * 256], in_=zb[zi][:, :]),
                      ['zb%d' % zi], ['QiS'])
                continue
            for h in range(2):
                tr(pk[zi][:, h, :], zb[zi][:, h * 128:(h + 1) * 128], ident_b[:, :],
                   ['zb%d' % zi, 'ident_b'], ['pk%d' % zi], inc=(h == 1))
            P.op('act', lambda e, zi=zi, blk=blk, t=t: e.copy(
                qiT[:, blk * 2:blk * 2 + 2, t * 128:(t + 1) * 128], pk[zi][:, :, :]), ['pk%d' % zi], ['qiT'])
    wb = load_w(w_dsa, C_KI, 64)
    for t in range(NOWN + 1):
        z = proj_tok(wb, 64, t)
        zi = nextz()
        rope(z, 1, 32, csI[:, t, :], zi)
        dst = nik_p[t * 128:(t + 1) * 128, :] if t < NOWN else nik_s[:, :]
        P.dma('act', lambda e, zi=zi, dst=dst: e.dma_start(out=dst, in_=zr[zi][:, 0:64]),
              ['zr%d' % zi], ['nik_p'])
        if t == NOWN:
            P.op('act', lambda e, zi=zi: e.copy(zb[zi][:, 0:64], zr[zi][:, 0:64]), ['zr%d' % zi], ['zb%d' % zi])
            P.dma('act', lambda e, zi=zi: e.dma_start(out=KiN[:, :], in_=zb[zi][:, 0:64]), ['zb%d' % zi], ['KiN'])
    P.dma('act', lambda e: e.dma_start(out=WsS[:, :], in_=wsgn[:, NOWN, :]), ['wsgn'], ['WsS'])
    tS = NOWN
    for hh in range(HB):
        wb = load_w(w_gla, hh * 768, 256)
        z = proj_tok(wb, 256, tS)
        zi = nextz()
        P.op('act', lambda e, zi=zi, z=z: e.activation(out=zb[zi][:, 0:128], in_=pz[z][:, 0:128], func=AF.Copy,
                                                        scale=DKB ** -0.5), ['pz%d' % z], ['zb%d' % zi])
        P.op('act', lambda e, zi=zi, z=z: e.copy(zb[zi][:, 128:256], pz[z][:, 128:256]), ['pz%d' % z], ['zb%d' % zi])
        for h2 in range(2):
            tr(pk[zi][:, h2, :], zb[zi][:, h2 * 128:(h2 + 1) * 128], ident_b[:, :],
               ['zb%d' % zi, 'ident_b'], ['pk%d' % zi], inc=(h2 == 1))
        P.op('act', lambda e, zi=zi: e.copy(ktst[zi][:, :, :], pk[zi][:, :, :]), ['pk%d' % zi], ['ktst%d' % zi])
        P.dma('act', lambda e, zi=zi, hh=hh: e.dma_start(out=QTs[hh, :, :], in_=ktst[zi][:, 0, :]), ['ktst%d' % zi], ['QKTs'])
        P.dma('act', lambda e, zi=zi, hh=hh: e.dma_start(out=KTs[hh, :, :], in_=ktst[zi][:, 1, :]), ['ktst%d' % zi], ['QKTs'])
        wb = load_w(w_gla, hh * 768 + 256, 256)
        z = proj_tok(wb, 256, tS)
        zi = nextz()
        P.op('act', lambda e, zi=zi, z=z: e.copy(zb[zi][:, :], pz[z][:, 0:256]), ['pz%d' % z], ['zb%d' % zi])
        P.dma('act', lambda e, zi=zi, hh=hh: e.dma_start(out=Vgs[:, hh * DVB:(hh + 1) * DVB], in_=zb[zi][:, :]),
              ['zb%d' % zi], ['Vgs'])
        wb = load_w(w_gla, hh * 768 + 512, 256)
        z = proj_tok(wb, 256, tS)
        zi = nextz()
        P.op('act', lambda e, zi=zi, z=z: e.activation(out=zb[zi][:, :], in_=pz[z][:, 0:256], func=AF.Silu),
             ['pz%d' % z], ['zb%d' % zi])
        P.dma('act', lambda e, zi=zi, hh=hh: e.dma_start(out=Ggs[:, hh * DVB:(hh + 1) * DVB], in_=zb[zi][:, :]),
              ['zb%d' % zi], ['Ggs'])
    wb = load_w(w_gla, HB * 768, 16)
    z = cnt['z'] % 2
    cnt['z'] += 1
    for kc in range(KC):
        mm(pz[z][0:16, 0:128], wbf[wb][:, kc, 0:16], xT[:, kc, tS * 128:(tS + 1) * 128],
           kc == 0, kc == KC - 1, ['xT', 'wbf%d' % wb], ['pz%d' % z], inc=(kc == KC - 1))
    P.op('act', lambda e, z=z: e.copy(abT1s[0:16, :], pz[z][0:16, 0:128]), ['pz%d' % z], ['abT1s'])
    P.barrier()
    ep.close()
    if stage < 2:
        P.barrier(final=True)
        P.emit()
        return nc

    ea = ExitStack()
    scores = sb(ea, "scores", [128, STOT], F32)
    lo = sb(ea, "lo", [128, NOWN], F32)
    mid = sb(ea, "mid", [128, NOWN], F32)
    cntt = sb(ea, "cntt", [128, NOWN], F32)
    gw = sb(ea, "gw", [128, NOWN], F32)
    cjunk = sb(ea, "cjunk", [128, S], BF16)
    e2 = ExitStack()
    diag = sb(e2, "diag", [128, HIDX, 128], BF16)
    Rb = [sb(e2, "Rb%d" % i, [128, 512], BF16) for i in range(4)]
    mk = sb(e2, "mk", [128, 512], F32)
    pd = [ps(e2, "pd%d" % i, [128, 512], F32) for i in range(4)]
    pi = [ps(e2, "pi%d" % i, [128, 512], F32) for i in range(2)]
    ci = 0
    for k in range(NOWN):
        for h in range(HIDX):
            P.op('dve', lambda e, h=h, k=k: e.tensor_scalar(out=diag[:, h, :], in0=ident_b[:, :],
                                                            scalar1=wsgn[:, k, h:h + 1], scalar2=None, op0=ALU.mult),
                 ['ident_b', 'wsgn'], ['diag'])
        P.dma('sp', lambda e, k=k: e.dma_start(out=mk[:, :], in_=mask_o[k, :, :]), [], ['mk'])
        nchk = LK[k] // 512
        for c in range(nchk):
            pib = ci % 2
            ci += 1
            for m in range(8):
                a, b = 2 * (m % 2), 2 * (m % 2) + 1
                mm(pd[a][:, :], qiT[0:64, m, k * 128:(k + 1) * 128], kiT2[0:64, c * 512:(c + 1) * 512],
                   True, True, ['qiT', 'kiT2'], ['pd%d' % a])
                mm(pd[b][:, :], qiT[64:128, m, k * 128:(k + 1) * 128], kiT2[64:128, c * 512:(c + 1) * 512],
                   True, True, ['qiT', 'kiT2'], ['pd%d' % b])
                P.op('act', lambda e, a=a: e.activation(out=Rb[a][:, :], in_=pd[a][:, :], func=AF.Relu),
                     ['pd%d' % a], ['Rb%d' % a])
                P.op('dve', lambda e, b=b: e.tensor_scalar(out=Rb[b][:, :], in0=pd[b][:, :], scalar1=0.0,
                                                           scalar2=None, op0=ALU.max),
                     ['pd%d' % b], ['Rb%d' % b])
                mm(pi[pib][:, :], diag[:, 2 * m, :], Rb[a][:, :], m == 0, False,
                   ['diag', 'Rb%d' % a], ['pi%d' % pib], inc=False)
                mm(pi[pib][:, :], diag[:, 2 * m + 1, :], Rb[b][:, :], False, m == 7,
                   ['diag', 'Rb%d' % b], ['pi%d' % pib], inc=True)
            dst = scores[:, SOFF[k] + c * 512:SOFF[k] + (c + 1) * 512]
            if c == nchk - 1:
                P.op('dve', lambda e, dst=dst, pib=pib: e.tensor_tensor(out=dst, in0=pi[pib][:, :], in1=mk[:, :],
                                                                        op=ALU.add),
                     ['pi%d' % pib, 'mk'], ['scores'])
            else:
                P.op('act', lambda e, dst=dst, pib=pib: e.copy(dst, pi[pib][:, :]), ['pi%d' % pib], ['scores'])
    P.barrier()
    e2.close()

    W0 = 64.0
    P.op('dve', lambda e: e.memset(lo[:, :], -W0), [], ['lo'])
    for it in range(NBIS):
        w = W0 / (2 ** it)
        P.op('dve', lambda e, w=w: e.tensor_scalar(out=mid[:, :], in0=lo[:, :], scalar1=w, scalar2=None, op0=ALU.add),
             ['lo'], ['mid'])
        for k in range(NOWN):
            P.op('dve', lambda e, k=k: e.tensor_scalar(
                out=cjunk[:, 0:LK[k]], in0=scores[:, SOFF[k]:SOFF[k] + LK[k]], scalar1=mid[:, k:k + 1], scalar2=0.0,
                op0=ALU.is_ge, op1=ALU.add, accum_out=cntt[:, k:k + 1]), ['scores', 'mid'], ['cjunk', 'cntt'])
        P.op('dve', lambda e, w=w: e.tensor_scalar(out=gw[:, :], in0=cntt[:, :], scalar1=TOPK - 0.5, scalar2=w,
                                                   op0=ALU.is_ge, op1=ALU.mult), ['cntt'], ['gw'])
        P.op('dve', lambda e: e.tensor_tensor(out=lo[:, :], in0=lo[:, :], in1=gw[:, :], op=ALU.add),
             ['gw', 'lo'], ['lo'])

    e4 = ExitStack()
    KTb = [sb(e4, "KTb%d" % i, [128, HA, 512], BF16) for i in range(2)]
    Vb = [sb(e4, "Vb%d" % i, [128, 4, HA, 132], BF16) for i in range(2)]
    mkb = sb(e4, "mkb", [128, 512], BF16)
    mT = [sb(e4, "mT%d" % i, [128, 4, 128], BF16) for i in range(2)]
    Pe = [sb(e4, "Pe%d" % i, [128, 4, 128], BF16) for i in range(2)]
    Pm = [sb(e4, "Pm%d" % i, [128, 4, 128], BF16) for i in range(2)]
    den = sb(e4, "den", [128, HA], F32)
    rec = sb(e4, "rec", [128, HA], F32)
    pS = [ps(e4, "pS%d" % i, [128, 4, 128], F32) for i in range(2)]
    pOf = [ps(e4, "pO%d" % i, [128, 512], F32) for i in range(3)]
    pO = [t[:, 0:396].rearrange("p (a b) -> p a b", a=3) for t in pOf]
    pMf = ps(e4, "pM", [128, 1024], BF16)
    pM = pMf[:, 0:512].rearrange("p (a b) -> p a b", a=4)
    for i in range(2):
        P.op('dve', lambda e, i=i: e.memset(Vb[i][:, :, :, 128:129], 1.0), [], ['Vb%d' % i])
    zeroL = sb(e4, "zeroL", [128, 128], BF16)
    zeroR = sb(e4, "zeroR", [128, 512], BF16)
    P.op('dve', lambda e: e.memset(zeroL[:, :], 0.0), [], ['zeroL'])
    P.op('dve', lambda e: e.memset(zeroR[:, :], 0.0), [], ['zeroR'])
    ld = 0
    for k in range(NOWN):
        nq = LK[k] // 512
        nsb = LK[k] // 128
        for i in range(3):
            mm(pOf[i][:, 0:396], zeroL[:, :], zeroR[:, 0:396], True, False, ['zeroL', 'zeroR'], ['pO'], inc=(i == 2))
        ld0 = ld
        ld += nq

        def prep4(q4, k=k, ld0=ld0):
            bi = (ld0 + q4) % 2
            P.dma('sp', lambda e, bi=bi, q4=q4: e.dma_start(
                out=KTb[bi][:, :, :], in_=KT[:, :, q4 * 512:(q4 + 1) * 512].rearrange("h d s -> d h s")),
                ['KT'], ['KTb%d' % bi])
            for j4 in range(4):
                P.dma('sp', lambda e, bi=bi, q4=q4, j4=j4: e.dma_start(
                    out=Vb[bi][:, j4, :, 0:128],
                    in_=Vs[q4 * 512 + j4 * 128:q4 * 512 + (j4 + 1) * 128, :].rearrange("p (h d) -> p h d", h=HA)),
                    ['Vs'], ['Vb%d' % bi])
            P.op('dve', lambda e, k=k, q4=q4: e.tensor_scalar(
                out=mkb[:, :], in0=scores[:, SOFF[k] + q4 * 512:SOFF[k] + (q4 + 1) * 512],
                scalar1=lo[:, k:k + 1], scalar2=None, op0=ALU.is_ge), ['scores', 'lo'], ['mkb'])
            for j4 in range(4):
                tr(pM[:, j4, :], mkb[:, j4 * 128:(j4 + 1) * 128], ident_b[:, :], ['mkb', 'ident_b'], ['pM'], inc=(j4 == 3))
            P.op('act', lambda e, bi=bi: e.copy(mT[bi][:, :, :], pM[:, :, :]), ['pM'], ['mT%d' % bi])

        def attend4(q4, k=k, ld0=ld0, nsb=nsb):
            bi = (ld0 + q4) % 2
            for j4 in range(4):
                sbi = q4 * 4 + j4
                for hg in range(2):
                    for h4 in range(4):
                        h = hg * 4 + h4
                        mm(pS[hg][:, h4, :], KTb[bi][:, h, j4 * 128:(j4 + 1) * 128], qaT[:, h, k * 128:(k + 1) * 128],
                           True, True, ['KTb%d' % bi, 'qaT'], ['pS%d' % hg], inc=(h4 == 3))
                    P.op('act', lambda e, hg=hg: e.activation(out=Pe[hg][:, :, :], in_=pS[hg][:, :, :], func=AF.Exp,
                                                              scale=DH ** -0.5), ['pS%d' % hg], ['Pe%d' % hg])
                    P.op('dve', lambda e, hg=hg, bi=bi, j4=j4: e.tensor_tensor(
                        out=Pm[hg][:, :, :], in0=Pe[hg][:, :, :],
                        in1=mT[bi][:, j4:j4 + 1, :].to_broadcast([128, 4, 128]), op=ALU.mult),
                        ['Pe%d' % hg, 'mT%d' % bi], ['Pm%d' % hg])
                    for h4 in range(4):
                        h = hg * 4 + h4
                        mm(pO[h // 3][:, h % 3, 0:129], Pm[hg][:, h4, :], Vb[bi][:, j4, h, 0:129],
                           False, sbi == nsb - 1, ['Pm%d' % hg, 'Vb%d' % bi], ['pO'], inc=(h4 == 3))

        prep4(0)
        for q4 in range(nq):
            if q4 + 1 < nq:
                prep4(q4 + 1)
            attend4(q4)
        for h in range(HA):
            P.op('act', lambda e, h=h: e.copy(den[:, h:h + 1], pO[h // 3][:, h % 3, 128:129]), ['pO'], ['den'])
        P.op('dve', lambda e: e.reciprocal(rec[:, :], den[:, :]), ['den'], ['rec'])
        for h in range(HA):
            P.op('dve', lambda e, h=h, k=k: e.scalar_tensor_tensor(
                out=gaS[:, k, h * 128:(h + 1) * 128], in0=pO[h // 3][:, h % 3, 0:128], scalar=rec[:, h:h + 1],
                in1=gaS[:, k, h * 128:(h + 1) * 128], op0=ALU.mult, op1=ALU.mult), ['pO', 'rec', 'gaS'], ['gaS'])
    if stage == 4:
        dbg_a = dout("dbg_a", [NO, 1024], BF16)
        dbg_lo = dout("dbg_lo", [128, NOWN])
        dbg_sc = dout("dbg_sc", [128, STOT])
        dbg_den = dout("dbg_den", [128, HA])
        P.dma('sp', lambda e: e.dma_start(out=dbg_sc[:, :], in_=scores[:, :]), ['scores'], ['dbg_sc'])
        P.dma('sp', lambda e: e.dma_start(out=dbg_den[:, :], in_=den[:, :]), ['den'], ['dbg_den'])
        P.dma('sp', lambda e: e.dma_start(out=dbg_a.rearrange("(n p) f -> p n f", p=128), in_=gaS[:, :, :]), ['gaS'], ['dbg_a'])
        P.dma('sp', lambda e: e.dma_start(out=dbg_lo[:, :], in_=lo[:, :]), ['lo'], ['dbg_lo'])
        P.barrier(final=True)
        P.emit()
        return nc
    P.dma('sp', lambda e: e.dma_start(out=gaD.rearrange("(n p) f -> p n f", p=128), in_=gaS[:, :, :]), ['gaS'], ['gaD'])
    P.barrier()
    e4.close()
    ea.close()
    es1.close()

    eg = ExitStack()
    CH = 64
    wgb_f = sb(eg, "wgb_f", [32, HB * DKB], F32)
    wgb_b = sb(eg, "wgb_b", [32, HB * DKB], BF16)
    qT_g = sb(eg, "qT_g", [128, S], BF16)
    kT_g = sb(eg, "kT_g", [128, S], BF16)
    laT = sb(eg, "laT", [128, S], F32)
    cumT = sb(eg, "cumT", [128, S], F32)
    Et = sb(eg, "Et", [128, S], F32)
    flagT = sb(eg, "flagT", [128, S], F32)
    qdT = sb(eg, "qdT", [128, S], BF16)
    kdT = sb(eg, "kdT", [128, S], BF16)
    klT = sb(eg, "klT", [128, S], BF16)
    elast = sb(eg, "elast", [128, NCH], F32)
    kl = sb(eg, "kl", [64, NCH, 128], BF16)
    vch = sb(eg, "vch", [64, NCH, DVB], BF16)
    attT = sb(eg, "attT", [64, NCH, 64], BF16)
    Sf = sb(eg, "Sf", [128, DVB], F32)
    Sb = sb(eg, "Sb", [128, DVB], BF16)
    ob = [sb(eg, "ob%d" % i, [64, 8, DVB], F32) for i in range(2)]
    osq = sb(eg, "osq", [64, 8, DVB], F32)
    gch = sb(eg, "gch", [64, 8, DVB], BF16)
    bpb = sb(eg, "bpb", [64, 8, DVB], BF16)
    gn = sb(eg, "gn", [64, DVB], F32)
    sq8 = sb(eg, "sq8", [64, 8, 3], F32)
    pzg = ps(eg, "pzg", [128, 512], F32)
    pTg = ps(eg, "pTg", [128, 1024], BF16)
    pA = ps(eg, "pA", [128, 512], F32)
    pU = [ps(eg, "pU%d" % i, [128, 512], F32) for i in range(2)]
    pOg = [ps(eg, "pOg%d" % i, [128, 512], F32) for i in range(2)]

    P.dma('sp', lambda e: e.dma_start(out=wgb_f[:, :], in_=wgb[:, :]), [], ['wgb_f'])
    P.op('dve', lambda e: e.tensor_copy(wgb_b[:, :], wgb_f[:, :]), ['wgb_f'], ['wgb_b'])
    P.dma('sp', lambda e: e.dma_start(out=gn[:, :], in_=gla_g[0:1, :].partition_broadcast(64)), [], ['gn'])
    P.op('dve', lambda e: e.memset(flagT[:, :], 1.0), [], ['flagT'])
    P.op('dve', lambda e: e.memset(flagT[:, :].rearrange("p (n c) -> p n c", c=CH)[:, :, 0:1], 0.0), [], ['flagT'])
    for hh in range(HB):
        P.dma('sp', lambda e, hh=hh: e.dma_start(out=qT_g[:, :], in_=QTg[hh, :, :]), ['QKTg'], ['qkT_g'])
        P.dma('sp', lambda e, hh=hh: e.dma_start(out=kT_g[:, :], in_=KTg[hh, :, :]), ['QKTg'], ['qkT_g'])
        P.dma('sp', lambda e, hh=hh: e.dma_start(out=vch[:, :, :], in_=Vg[:, hh * DVB:(hh + 1) * DVB].rearrange("(n c) v -> c n v", c=CH)), ['Vg'], ['vch'])
        for c4 in range(S // 512):
            mm(pzg[:, :], wgb_b[:, hh * DKB:(hh + 1) * DKB], abT1[:, c4 * 512:(c4 + 1) * 512], True, True, ['wgb_b', 'abT1'], ['pzg'])
            P.op('act', lambda e, c4=c4: e.activation(out=Et[:, c4 * 512:(c4 + 1) * 512], in_=pzg[:, :], func=AF.Sigmoid),
                 ['pzg'], ['Et'])
        P.op('act', lambda e: e.activation(out=laT[:, :], in_=Et[:, :], func=AF.Ln), ['Et'], ['laT'])
        P.op('dve', lambda e: e.tensor_scalar(out=laT[:, :], in0=laT[:, :], scalar1=1.0 / 16.0, scalar2=None, op0=ALU.mult),
             ['laT'], ['laT'])
        P.op('dve', lambda e: e.tensor_tensor_scan(out=cumT[:, :], data0=flagT[:, :], data1=laT[:, :], initial=0.0,
                                                    op0=ALU.mult, op1=ALU.add), ['flagT', 'laT'], ['cumT'])
        cum3 = cumT[:, :].rearrange("p (n c) -> p n c", c=CH)
        P.op('act', lambda e: e.activation(out=Et[:, :], in_=cumT[:, :], func=AF.Exp), ['cumT'], ['Et'])
        P.op('dve', lambda e: e.tensor_tensor(out=qdT[:, :], in0=qT_g[:, :], in1=Et[:, :], op=ALU.mult),
             ['qkT_g', 'Et'], ['qdT'])
        P.op('act', lambda e: e.activation(out=Et[:, :], in_=cumT[:, :], func=AF.Exp, scale=-1.0), ['cumT', 'qdT'], ['Et'])
        P.op('dve', lambda e: e.tensor_tensor(out=kdT[:, :], in0=kT_g[:, :], in1=Et[:, :], op=ALU.mult),
             ['qkT_g', 'Et'], ['kdT'])
        P.op('dve', lambda e: e.tensor_tensor(out=Et[:, :].rearrange("p (n c) -> p n c", c=CH),
                                              in0=cum3[:, :, CH - 1:CH].to_broadcast([128, NCH, CH]), in1=cum3,
                                              op=ALU.subtract), ['cumT', 'kdT'], ['Et'])
        P.op('act', lambda e: e.activation(out=Et[:, :], in_=Et[:, :], func=AF.Exp), ['Et'], ['Et'])
        P.op('dve', lambda e: e.tensor_tensor(out=klT[:, :], in0=kT_g[:, :], in1=Et[:, :], op=ALU.mult),
             ['qkT_g', 'Et'], ['klT'])
        P.op('act', lambda e: e.activation(out=elast[:, :].rearrange("p (n o) -> p n o", o=1), in_=cum3[:, :, CH - 1:CH],
                                           func=AF.Exp), ['cumT'], ['elast'])
        pT3 = pTg[0:64, :].rearrange("p (a b) -> p a b", a=8)
        for n8 in range(NCH // 8):
            for i in range(8):
                n = n8 * 8 + i
                tr(pT3[:, i, :], klT[:, n * CH:(n + 1) * CH], ident_b[:, :], ['klT', 'ident_b'], ['pTg'], inc=(i == 7))
            P.op('act', lambda e, n8=n8: e.copy(kl[:, n8 * 8:(n8 + 1) * 8, :], pT3[:, :, :]), ['pTg'], ['kl'])
        pA3 = pA[0:64, :].rearrange("p (a b) -> p a b", a=8)
        for n8 in range(NCH // 8):
            for i in range(8):
                n = n8 * 8 + i
                mm(pA3[:, i, :], kdT[:, n * CH:(n + 1) * CH], qdT[:, n * CH:(n + 1) * CH], True, True,
                   ['kdT', 'qdT'], ['pA'], inc=(i == 7))
            P.op('dve', lambda e, n8=n8: e.tensor_tensor(out=attT[:, n8 * 8:(n8 + 1) * 8, :], in0=pA3[:, :, :],
                                                         in1=tri_b[:, None, :].to_broadcast([64, 8, 64]), op=ALU.mult),
                 ['pA', 'tri'], ['attT'])
        for n in range(NCH):
            u = n % 2
            mm(pU[u][:, 0:DVB], kl[:, n, :], vch[:, n, :], True, True, ['kl', 'vch'], ['pU%d' % u])
            mm(pOg[u][0:64, 0:DVB], attT[:, n, :], vch[:, n, :], True, n == 0, ['attT', 'vch'], ['pOg%d' % u],
               inc=(n == 0))
            if n > 0:
                mm(pOg[u][0:64, 0:DVB], qdT[:, n * CH:(n + 1) * CH], Sb[:, :], False, True, ['qdT', 'Sb'], ['pOg%d' % u])
                P.op('dve', lambda e, n=n, u=u: e.scalar_tensor_tensor(out=Sf[:, :], in0=Sf[:, :], scalar=elast[:, n:n + 1],
                                                                       in1=pU[u][:, 0:DVB], op0=ALU.mult, op1=ALU.add),
                     ['Sf', 'elast', 'pU%d' % u], ['Sf'])
            else:
                P.op('dve', lambda e, u=u: e.tensor_copy(Sf[:, :], pU[u][:, 0:DVB]), ['pU%d' % u], ['Sf'])
            if n < NCH - 1:
                P.op('act', lambda e: e.copy(Sb[:, :], Sf[:, :]), ['Sf'], ['Sb'])
            bsel = (n // 8) % 2
            P.op('act', lambda e, n=n, u=u, bsel=bsel: e.copy(ob[bsel][:, n % 8, :], pOg[u][0:64, 0:DVB]),
                 ['pOg%d' % u], ['ob%d' % bsel])
            if n % 8 == 7:
                n0 = n - 7
                o8 = ob[bsel]
                P.dma('sp', lambda e, n0=n0, hh=hh: e.dma_start(
                    out=gch[:, :, :], in_=Gg[n0 * CH:(n0 + 8) * CH, hh * DVB:(hh + 1) * DVB].rearrange("(n c) v -> c n v", c=CH)), ['Gg'], ['gch'])
                P.op('dve', lambda e, o8=o8: e.tensor_tensor(out=osq[:, :, :], in0=o8[:, :, :], in1=o8[:, :, :], op=ALU.mult),
                     ['ob%d' % bsel], ['osq'])
                P.op('dve', lambda e: e.tensor_reduce(out=sq8[:, :, 0], in_=osq[:, :, :], axis=AX.X, op=ALU.add),
                     ['osq'], ['sq8a'])
                P.op('act', lambda e: e.activation(out=sq8[:, :, 1], in_=sq8[:, :, 0], func=AF.Sqrt, bias=eps_t[0:64, 0:1],
                                                   scale=1.0 / DVB), ['sq8a', 'eps'], ['sq8b'])
                P.op('dve', lambda e: e.reciprocal(sq8[:, :, 2], sq8[:, :, 1]), ['sq8b'], ['sq8c'])
                P.op('dve', lambda e, o8=o8: e.tensor_tensor(out=osq[:, :, :], in0=o8[:, :, :],
                                                             in1=sq8[:, :, 2:3].to_broadcast([64, 8, DVB]), op=ALU.mult),
                     ['ob%d' % bsel, 'sq8c'], ['osq'])
                P.op('dve', lambda e: e.tensor_tensor(out=osq[:, :, :], in0=osq[:, :, :],
                                                      in1=gn[:, None, :].to_broadcast([64, 8, DVB]), op=ALU.mult),
                     ['osq', 'gn'], ['osq'])
                P.op('dve', lambda e: e.tensor_tensor(out=bpb[:, :, :], in0=osq[:, :, :], in1=gch[:, :, :], op=ALU.mult),
                     ['osq', 'gch'], ['bpb'])
                P.dma('sp', lambda e, n0=n0, hh=hh: e.dma_start(
                    out=bp_loc[n0 * CH:(n0 + 8) * CH, hh * DVB:(hh + 1) * DVB].rearrange("(n c) v -> c n v", c=CH), in_=bpb[:, :, :]),
                    ['bpb'], ['bp_loc'])
        P.dma('sp', lambda e, hh=hh: e.dma_start(out=st_p[hh, :, :], in_=Sf[:, :]), ['Sf'], ['st_p'])
    if stage == 5:
        dbg_b = dout("dbg_b", [S, HB * DVB], BF16)
        P.dma('sp', lambda e: e.dma_start(out=dbg_b[:, :], in_=bp_loc[:, :]), ['bp_loc'], ['dbg_b'])
        P.barrier(final=True)
        P.emit()
        return nc
    P.barrier()
    eg.close()

    egs = ExitStack()
    wgbS_f = sb(egs, "wgbS_f", [32, HB * DKB], F32)
    wgbS_b = sb(egs, "wgbS_b", [32, HB * DKB], BF16)
    gnS = sb(egs, "gnS", [64, DVB], F32)
    pzS = ps(egs, "pzS", [128, 512], F32)
    pTS = ps(egs, "pTS", [128, 1024], BF16)
    pAS = ps(egs, "pAS", [128, 512], F32)
    pUS = [ps(egs, "pUS%d" % i, [128, 512], F32) for i in range(2)]
    pOS = [ps(egs, "pOS%d" % i, [128, 512], F32) for i in range(2)]
    P.dma('sp', lambda e: e.dma_start(out=wgbS_f[:, :], in_=wgb[:, :]), [], ['wgbS_f'])
    P.op('dve', lambda e: e.tensor_copy(wgbS_b[:, :], wgbS_f[:, :]), ['wgbS_f'], ['wgbS_b'])
    P.dma('sp', lambda e: e.dma_start(out=gnS[:, :], in_=gla_g[0:1, :].partition_broadcast(64)), [], ['gnS'])
    CS, NSQ, TS = 8, 4, 32
    qTs = sb(egs, "qTs", [128, TS], BF16)
    kTs = sb(egs, "kTs", [128, TS], BF16)
    vs = sb(egs, "vs", [CS, NSQ, DVB], BF16)
    gs = sb(egs, "gs", [CS, NSQ, DVB], BF16)
    laS = sb(egs, "laS", [128, TS], F32)
    cumS = sb(egs, "cumS", [128, TS], F32)
    EtS = sb(egs, "EtS", [128, TS], F32)
    flagS = sb(egs, "flagS", [128, TS], F32)
    qdS = sb(egs, "qdS", [128, TS], BF16)
    kdS = sb(egs, "kdS", [128, TS], BF16)
    klS = sb(egs, "klS", [128, TS], BF16)
    elS = sb(egs, "elS", [128, NSQ], F32)
    klSt = sb(egs, "klSt", [CS, NSQ, 128], BF16)
    attS = sb(egs, "attS", [CS, NSQ, CS], BF16)
    S0f = [sb(egs, "S0f%d" % i, [128, DVB], F32) for i in range(2)]
    S0b = [sb(egs, "S0b%d" % i, [128, DVB], BF16) for i in range(2)]
    SfS = [sb(egs, "SfS%d" % i, [128, DVB], F32) for i in range(2)]
    obS = sb(egs, "obS", [CS, NSQ, DVB], F32)
    oqS = sb(egs, "oqS", [CS, NSQ, DVB], F32)
    bpS = sb(egs, "bpS", [CS, NSQ, DVB], BF16)
    sqS = sb(egs, "sqS", [CS, NSQ, 3], F32)
    cum3s = cumS[:, :].rearrange("p (n c) -> p n c", c=CS)
    pT3s = pTS[0:CS, 0:NSQ * 128].rearrange("p (a b) -> p a b", a=NSQ)
    pA3s = pAS[0:CS, 0:NSQ * CS].rearrange("p (a b) -> p a b", a=NSQ)
    P.op('dve', lambda e: e.memset(flagS[:, :], 1.0), [], ['flagS'])
    P.op('dve', lambda e: e.memset(flagS[:, :].rearrange("p (n c) -> p n c", c=CS)[:, :, 0:1], 0.0), [], ['flagS'])
    si = 0
    for hh in range(HB):
        P.dma('sp', lambda e, hh=hh: e.dma_start(out=qTs[:, :], in_=QTs[hh, :, 0:TS]), ['QKTs'], ['qTs'])
        P.dma('sp', lambda e, hh=hh: e.dma_start(out=kTs[:, :], in_=KTs[hh, :, 0:TS]), ['QKTs'], ['kTs'])
        P.dma('sp', lambda e, hh=hh: e.dma_start(
            out=vs[:, :, :], in_=Vgs[0:TS, hh * DVB:(hh + 1) * DVB].rearrange("(n c) v -> c n v", c=CS)), ['Vgs'], ['vs'])
        P.dma('sp', lambda e, hh=hh: e.dma_start(
            out=gs[:, :, :], in_=Ggs[0:TS, hh * DVB:(hh + 1) * DVB].rearrange("(n c) v -> c n v", c=CS)), ['Ggs'], ['gs'])
        mm(pzS[:, 0:TS], wgbS_b[:, hh * DKB:(hh + 1) * DKB], abT1s[:, 0:TS], True, True, ['wgbS_b', 'abT1s'], ['pzS'])
        P.op('act', lambda e: e.activation(out=cumS[:, :], in_=pzS[:, 0:TS], func=AF.Sigmoid), ['pzS'], ['cumS'])
        P.op('act', lambda e: e.activation(out=laS[:, :], in_=cumS[:, :], func=AF.Ln), ['cumS'], ['laS'])
        P.op('dve', lambda e: e.tensor_scalar(out=laS[:, :], in0=laS[:, :], scalar1=1.0 / 16.0, scalar2=None, op0=ALU.mult),
             ['laS'], ['laS'])
        P.op('dve', lambda e: e.tensor_tensor_scan(out=cumS[:, :], data0=flagS[:, :], data1=laS[:, :], initial=0.0,
                                                    op0=ALU.mult, op1=ALU.add), ['flagS', 'laS'], ['cumS'])
        P.op('act', lambda e: e.activation(out=EtS[:, :], in_=cumS[:, :], func=AF.Exp), ['cumS'], ['EtS'])
        P.op('dve', lambda e: e.tensor_tensor(out=qdS[:, :], in0=qTs[:, :], in1=EtS[:, :], op=ALU.mult), ['qTs', 'EtS'], ['qdS'])
        P.op('act', lambda e: e.activation(out=EtS[:, :], in_=cumS[:, :], func=AF.Exp, scale=-1.0), ['cumS'], ['EtS'])
        P.op('dve', lambda e: e.tensor_tensor(out=kdS[:, :], in0=kTs[:, :], in1=EtS[:, :], op=ALU.mult), ['kTs', 'EtS'], ['kdS'])
        P.op('dve', lambda e: e.tensor_tensor(out=EtS[:, :].rearrange("p (n c) -> p n c", c=CS),
                                              in0=cum3s[:, :, CS - 1:CS].to_broadcast([128, NSQ, CS]), in1=cum3s,
                                              op=ALU.subtract), ['cumS'], ['EtS'])
        P.op('act', lambda e: e.activation(out=EtS[:, :], in_=EtS[:, :], func=AF.Exp), ['EtS'], ['EtS'])
        P.op('dve', lambda e: e.tensor_tensor(out=klS[:, :], in0=kTs[:, :], in1=EtS[:, :], op=ALU.mult), ['kTs', 'EtS'], ['klS'])
        P.op('act', lambda e: e.activation(out=elS[:, :].rearrange("p (n o) -> p n o", o=1), in_=cum3s[:, :, CS - 1:CS],
                                           func=AF.Exp), ['cumS'], ['elS'])
        for n in range(NSQ):
            tr(pT3s[:, n, :], klS[:, n * CS:(n + 1) * CS], ident_b[:, :], ['klS', 'ident_b'], ['pTS'], inc=(n == NSQ - 1))
        P.op('act', lambda e: e.copy(klSt[:, :, :], pT3s[:, :, :]), ['pTS'], ['klSt'])
        for n in range(NSQ):
            mm(pA3s[:, n, :], kdS[:, n * CS:(n + 1) * CS], qdS[:, n * CS:(n + 1) * CS], True, True,
               ['kdS', 'qdS'], ['pAS'], inc=(n == NSQ - 1))
        P.op('dve', lambda e: e.tensor_tensor(out=attS[:, :, :], in0=pA3s[:, :, :],
                                              in1=tri_b[0:CS, None, 0:CS].to_broadcast([CS, NSQ, CS]), op=ALU.mult),
             ['pAS', 'tri'], ['attS'])
        for n in range(NSQ):
            u = si % 2
            si += 1
            P.dma('sp', lambda e, n=n, hh=hh, u=u: e.dma_start(out=S0f[u][:, :], in_=st_s_in[n, hh, :, :]), [], ['S0f%d' % u])
            P.op('act', lambda e, u=u: e.copy(S0b[u][:, :], S0f[u][:, :]), ['S0f%d' % u], ['S0b%d' % u])
            mm(pUS[u][:, 0:DVB], klSt[:, n, :], vs[:, n, :], True, True, ['klSt', 'vs'], ['pUS%d' % u])
            mm(pOS[u][0:CS, 0:DVB], attS[:, n, :], vs[:, n, :], True, False, ['attS', 'vs'], ['pOS%d' % u], inc=False)
            mm(pOS[u][0:CS, 0:DVB], qdS[:, n * CS:(n + 1) * CS], S0b[u][:, :], False, True, ['qdS', 'S0b%d' % u], ['pOS%d' % u])
            P.op('dve', lambda e, n=n, u=u: e.scalar_tensor_tensor(out=SfS[u][:, :], in0=S0f[u][:, :], scalar=elS[:, n:n + 1],
                                                                   in1=pUS[u][:, 0:DVB], op0=ALU.mult, op1=ALU.add),
                 ['S0f%d' % u, 'elS', 'pUS%d' % u], ['SfS%d' % u])
            P.dma('sp', lambda e, n=n, hh=hh, u=u: e.dma_start(out=st_s[n, hh, :, :], in_=SfS[u][:, :]), ['SfS%d' % u], ['st_s'])
            P.op('act', lambda e, n=n, u=u: e.copy(obS[:, n, :], pOS[u][0:CS, 0:DVB]), ['pOS%d' % u], ['obS'])
        P.op('dve', lambda e: e.tensor_tensor(out=oqS[:, :, :], in0=obS[:, :, :], in1=obS[:, :, :], op=ALU.mult), ['obS'], ['oqS'])
        P.op('dve', lambda e: e.tensor_reduce(out=sqS[:, :, 0], in_=oqS[:, :, :], axis=AX.X, op=ALU.add), ['oqS'], ['sqSa'])
        P.op('act', lambda e: e.activation(out=sqS[:, :, 1], in_=sqS[:, :, 0], func=AF.Sqrt, bias=eps_t[0:CS, 0:1],
                                           scale=1.0 / DVB), ['sqSa', 'eps'], ['sqSb'])
        P.op('dve', lambda e: e.reciprocal(sqS[:, :, 2], sqS[:, :, 1]), ['sqSb'], ['sqSc'])
        P.op('dve', lambda e: e.tensor_tensor(out=oqS[:, :, :], in0=obS[:, :, :],
                                              in1=sqS[:, :, 2:3].to_broadcast([CS, NSQ, DVB]), op=ALU.mult),
             ['obS', 'sqSc'], ['oqS'])
        P.op('dve', lambda e: e.tensor_tensor(out=oqS[:, :, :], in0=oqS[:, :, :],
                                              in1=gnS[0:CS, None, :].to_broadcast([CS, NSQ, DVB]), op=ALU.mult),
             ['oqS', 'gnS'], ['oqS'])
        P.op('dve', lambda e: e.tensor_tensor(out=bpS[:, :, :], in0=oqS[:, :, :], in1=gs[:, :, :], op=ALU.mult),
             ['oqS', 'gs'], ['bpS'])
        P.dma('sp', lambda e, hh=hh: e.dma_start(
            out=bps[0:TS, hh * DVB:(hh + 1) * DVB].rearrange("(n c) v -> c n v", c=CS), in_=bpS[:, :, :]), ['bpS'], ['bps'])
    P.barrier()
    egs.close()

    if SDSA:
        ed = ExitStack()
        LB = NPG + 1
        LP = LB * 128
        widths = [512] * (LP // 512) + ([LP % 512] if LP % 512 else [])
        kiT2s = sb(ed, "kiT2s", [128, LP], BF16)
        scS = sb(ed, "scS", [128, LP], F32)
        cjS = sb(ed, "cjS", [128, LP], BF16)
        ptb = sb(ed, "ptb", [128, NPG], I32)
        idxi = sb(ed, "idxi", [128, NPG], I32)
        iot = sb(ed, "iot", [128, 1], F32)
        mkS = sb(ed, "mkS", [128, 128], F32)
        kig = [sb(ed, "kig%d" % i, [128, DIDX], F32) for i in range(2)]
        kib = [sb(ed, "kib%d" % i, [128, 128], BF16) for i in range(2)]
        kinb = sb(ed, "kinb", [128, DIDX], BF16)
        qtok = sb(ed, "qtok", [128, 1024], BF16)
        qiTs = sb(ed, "qiTs", [128, 8, 128], BF16)
        qaTs = sb(ed, "qaTs", [128, HA, 128], BF16)
        wsS = sb(ed, "wsS", [128, HIDX], F32)
        diagS = sb(ed, "diagS", [128, HIDX, 128], BF16)
        RbS = [sb(ed, "RbS%d" % i, [128, 512], BF16) for i in range(4)]
        loS = sb(ed, "loS", [128, 1], F32)
        midS = sb(ed, "midS", [128, 1], F32)
        cnS = sb(ed, "cnS", [128, 1], F32)
        gwS = sb(ed, "gwS", [128, 1], F32)
        kpg = [sb(ed, "kpg%d" % i, [128, HA * DH], F32) for i in range(4)]
        vpg = [sb(ed, "vpg%d" % i, [128, HA * DH], F32) for i in range(4)]
        kpb = [sb(ed, "kpb%d" % i, [128, HA * DH], BF16) for i in range(2)]
        KTbS = [sb(ed, "KTbS%d" % i, [128, HA, 128], BF16) for i in range(2)]
        VbS = [sb(ed, "VbS%d" % i, [128, HA, 132], BF16) for i in range(2)]
        mkbS = sb(ed, "mkbS", [128, 128], BF16)
        mTs = [sb(ed, "mTs%d" % i, [128, 128], BF16) for i in range(2)]
        PeS = [sb(ed, "PeS%d" % i, [128, 4, 128], BF16) for i in range(2)]
        PmS = [sb(ed, "PmS%d" % i, [128, 4, 128], BF16) for i in range(2)]
        denS = sb(ed, "denS", [128, HA], F32)
        recS = sb(ed, "recS", [128, HA], F32)
        gaT = sb(ed, "gaT", [128, 1024], BF16)
        zLs = sb(ed, "zLs", [128, 128], BF16)
        zRs = sb(ed, "zRs", [128, 512], BF16)
        bfA = ps(ed, "bfA", [128, 1024], BF16)
        bfB = ps(ed, "bfB", [128, 1024], BF16)
        Fb = [ps(ed, "Fb%d" % i, [128, 512], F32) for i in range(6)]
        pk8 = bfA[:, :].rearrange("p (a b) -> p a b", a=8)
        pKT8 = pk8
        pdS = Fb[0:4]
        piS = Fb[4:6]
        pS4 = [t[:, :].rearrange("p (a b) -> p a b", a=4) for t in Fb[0:2]]
        pOfs = Fb[2:5]
        pOs = [t[:, 0:396].rearrange("p (a b) -> p a b", a=3) for t in pOfs]
        pMs = bfB
        P.dma('sp', lambda e: e.dma_start(out=iot[:, :], in_=iota_p[:, :]), [], ['iot'])
        P.dma('sp', lambda e: e.dma_start(out=mkS[:, :], in_=mask_s[:, :]), [], ['mkS'])
        for tl, ky in ((kinb, 'kinb'), (qtok, 'qtok'), (wsS, 'wsS'), (gaT, 'gaT'), (zLs, 'zLs'), (zRs, 'zRs')):
            P.op('dve', lambda e, tl=tl: e.memset(tl[:, :], 0.0), [], [ky])
        for i in range(2):
            P.op('dve', lambda e, i=i: e.memset(kpb[i][:, :], 0.0), [], ['kpb%d' % i])
            P.op('dve', lambda e, i=i: e.memset(VbS[i][:, :, 0:128], 0.0), [], ['VbS%d' % i])
            P.op('dve', lambda e, i=i: e.memset(VbS[i][:, :, 128:129], 1.0), [], ['VbS%d' % i])
        for s in range(4):
            r0 = 8 * s
            P.dma('sp', lambda e, s=s: e.dma_start(out=ptb[:, :], in_=pt_s[s:s + 1, :].partition_broadcast(128)), [], ['ptb'])
            P.op('dve', lambda e: e.tensor_scalar(out=idxi[:, :], in0=ptb[:, :], scalar1=128.0, scalar2=iot[:, 0:1],
                                                  op0=ALU.mult, op1=ALU.add), ['ptb', 'iot'], ['idxi'])
            for srcD, dstT, ky in ((QiS, qiTs, 'qiTs'), (QaS, qaTs, 'qaTs')):
                P.dma('sp', lambda e, srcD=srcD, r0=r0: e.dma_start(out=qtok[0:8, :], in_=srcD[r0:r0 + 8, :]), [srcD is QiS and 'QiS' or 'QaS'], ['qtok'])
                for i in range(8):
                    tr(pk8[:, i, :], qtok[:, i * 128:(i + 1) * 128], ident_b[:, :], ['qtok', 'ident_b'], ['pkS'], inc=(i == 7))
                P.op('act', lambda e, dstT=dstT: e.copy(dstT[:, :, :], pk8[:, :, :]), ['pkS'], [ky])
            P.dma('sp', lambda e, r0=r0: e.dma_start(out=wsS[0:8, :], in_=WsS[r0:r0 + 8, :]), ['WsS'], ['wsS'])
            for h in range(HIDX):
                P.op('dve', lambda e, h=h: e.tensor_scalar(out=diagS[:, h, :], in0=ident_b[:, :], scalar1=wsS[:, h:h + 1],
                                                           scalar2=None, op0=ALU.mult), ['ident_b', 'wsS'], ['diagS'])
            for bl in range(LB):
                u = bl % 2
                if bl < NPG:
                    P.dma('pool', lambda e, u=u, bl=bl: e.indirect_dma_start(
                        out=kig[u][:, :], out_offset=None, in_=cik[:, :],
                        in_offset=bass.IndirectOffsetOnAxis(ap=idxi[:, bl:bl + 1], axis=0)), ['idxi'], ['kig%d' % u])
                    P.op('act', lambda e, u=u: e.copy(kib[u][:, 0:64], kig[u][:, :]), ['kig%d' % u], ['kib%d' % u])
                    P.op('act', lambda e, u=u: e.copy(kib[u][:, 64:128], kig[u][:, :]), ['kig%d' % u], ['kib%d' % u])
                else:
                    P.dma('sp', lambda e, r0=r0: e.dma_start(out=kinb[0:8, :], in_=KiN[r0:r0 + 8, :]), ['KiN'], ['kinb'])
                    P.op('act', lambda e, u=u: e.copy(kib[u][:, 0:64], kinb[:, :]), ['kinb'], ['kib%d' % u])
                    P.op('act', lambda e, u=u: e.copy(kib[u][:, 64:128], kinb[:, :]), ['kinb'], ['kib%d' % u])
                tr(pk8[:, 0, :], kib[u][:, :], ident_b[:, :], ['kib%d' % u, 'ident_b'], ['pkS'])
                P.op('act', lambda e, bl=bl: e.copy(kiT2s[:, bl * 128:(bl + 1) * 128], pk8[:, 0, :]), ['pkS'], ['kiT2s'])
            c0 = 0
            for ci_, wdt in enumerate(widths):
                pib = ci_ % 2
                for m in range(8):
                    a, b = 2 * (m % 2), 2 * (m % 2) + 1
                    mm(pdS[a][:, 0:wdt], qiTs[0:64, m, :], kiT2s[0:64, c0:c0 + wdt], True, True, ['qiTs', 'kiT2s'], ['pdS%d' % a])
                    mm(pdS[b][:, 0:wdt], qiTs[64:128, m, :], kiT2s[64:128, c0:c0 + wdt], True, True, ['qiTs', 'kiT2s'], ['pdS%d' % b])
                    P.op('act', lambda e, a=a, wdt=wdt: e.activation(out=RbS[a][:, 0:wdt], in_=pdS[a][:, 0:wdt], func=AF.Relu),
                         ['pdS%d' % a], ['RbS%d' % a])
                    P.op('dve', lambda e, b=b, wdt=wdt: e.tensor_scalar(out=RbS[b][:, 0:wdt], in0=pdS[b][:, 0:wdt], scalar1=0.0,
                                                                        scalar2=None, op0=ALU.max), ['pdS%d' % b], ['RbS%d' % b])
                    mm(piS[pib][:, 0:wdt], diagS[:, 2 * m, :], RbS[a][:, 0:wdt], m == 0, False, ['diagS', 'RbS%d' % a], ['piS%d' % pib], inc=False)
                    mm(piS[pib][:, 0:wdt], diagS[:, 2 * m + 1, :], RbS[b][:, 0:wdt], False, m == 7, ['diagS', 'RbS%d' % b], ['piS%d' % pib])
                P.op('act', lambda e, pib=pib, c0=c0, wdt=wdt: e.copy(scS[:, c0:c0 + wdt], piS[pib][:, 0:wdt]), ['piS%d' % pib], ['scS'])
                c0 += wdt
            P.op('dve', lambda e: e.tensor_tensor(out=scS[:, NPG * 128:LP], in0=scS[:, NPG * 128:LP], in1=mkS[:, :], op=ALU.add),
                 ['scS', 'mkS'], ['scS'])
            P.barrier()
            P.op('dve', lambda e: e.memset(loS[:, :], -64.0), [], ['loS'])
            for it in range(NBIS):
                w = 64.0 / (2 ** it)
                P.op('dve', lambda e, w=w: e.tensor_scalar(out=midS[:, :], in0=loS[:, :], scalar1=w, scalar2=None, op0=ALU.add),
                     ['loS'], ['midS'])
                P.op('dve', lambda e: e.tensor_scalar(out=cjS[:, :], in0=scS[:, :], scalar1=midS[:, 0:1], scalar2=0.0,
                                                      op0=ALU.is_ge, op1=ALU.add, accum_out=cnS[:, 0:1]), ['scS', 'midS'], ['cjS', 'cnS'])
                P.op('dve', lambda e, w=w: e.tensor_scalar(out=gwS[:, :], in0=cnS[:, :], scalar1=TOPK - 0.5, scalar2=w,
                                                           op0=ALU.is_ge, op1=ALU.mult), ['cnS'], ['gwS'])
                P.op('dve', lambda e: e.tensor_tensor(out=loS[:, :], in0=loS[:, :], in1=gwS[:, :], op=ALU.add), ['gwS', 'loS'], ['loS'])
            P.dma('sp', lambda e, r0=r0: e.dma_start(out=gaT[0:8, :], in_=GaS[r0:r0 + 8, :]), ['GaS'], ['gaT'])
            for i in range(3):
                mm(pOfs[i][:, 0:396], zLs[:, :], zRs[:, 0:396], True, False, ['zLs', 'zRs'], ['pOs'], inc=(i == 2))
            def prep_blk(bl):
                u = bl % 2
                g = bl % 4
                if bl < NPG:
                    P.dma('pool', lambda e, g=g, bl=bl: e.indirect_dma_start(
                        out=kpg[g][:, :], out_offset=None, in_=ck[:, :],
                        in_offset=bass.IndirectOffsetOnAxis(ap=idxi[:, bl:bl + 1], axis=0)), ['idxi'], ['kpg%d' % g])
                    P.dma('pool', lambda e, g=g, bl=bl: e.indirect_dma_start(
                        out=vpg[g][:, :], out_offset=None, in_=cv[:, :],
                        in_offset=bass.IndirectOffsetOnAxis(ap=idxi[:, bl:bl + 1], axis=0)), ['idxi'], ['vpg%d' % g])
                    P.op('act', lambda e, u=u, g=g: e.copy(kpb[u][:, :], kpg[g][:, :]), ['kpg%d' % g], ['kpb%d' % u])
                    P.op('dve', lambda e, u=u, g=g: e.tensor_copy(VbS[u][:, :, 0:128], vpg[g][:, :].rearrange("p (h d) -> p h d", h=HA)),
                         ['vpg%d' % g], ['VbS%d' % u])
                else:
                    P.op('dve', lambda e, u=u: e.memset(kpb[u][:, :], 0.0), [], ['kpb%d' % u])
                    P.op('dve', lambda e, u=u: e.memset(VbS[u][:, :, 0:128], 0.0), [], ['VbS%d' % u])
                    P.dma('sp', lambda e, u=u, r0=r0: e.dma_start(out=kpb[u][0:8, :], in_=KsN[r0:r0 + 8, :]), ['KsN'], ['kpb%d' % u])
                    P.dma('sp', lambda e, u=u, r0=r0: e.dma_start(out=VbS[u][0:8, :, 0:128],
                                                           in_=VsN[r0:r0 + 8, :].rearrange("p (h d) -> p h d", h=HA)),
                          ['VsN'], ['VbS%d' % u])
                for h in range(HA):
                    tr(pKT8[:, h, :], kpb[u][:, h * 128:(h + 1) * 128], ident_b[:, :], ['kpb%d' % u, 'ident_b'], ['pKT'], inc=(h == HA - 1))
                P.op('act', lambda e, u=u: e.copy(KTbS[u][:, :, :], pKT8[:, :, :]), ['pKT'], ['KTbS%d' % u])
                P.op('dve', lambda e, bl=bl: e.tensor_scalar(out=mkbS[:, :], in0=scS[:, bl * 128:(bl + 1) * 128], scalar1=loS[:, 0:1],
                                                             scalar2=None, op0=ALU.is_ge), ['scS', 'loS'], ['mkbS'])
                tr(pMs[:, 0:128], mkbS[:, :], ident_b[:, :], ['mkbS', 'ident_b'], ['pMs'])
                P.op('act', lambda e, u=u: e.copy(mTs[u][:, :], pMs[:, 0:128]), ['pMs'], ['mTs%d' % u])

            def attend_blk(bl):
                u = bl % 2
                for hg in range(2):
                    for h4 in range(4):
                        h = hg * 4 + h4
                        mm(pS4[hg][:, h4, :], KTbS[u][:, h, :], qaTs[:, h, :], True, True, ['KTbS%d' % u, 'qaTs'], ['pSs%d' % hg], inc=(h4 == 3))
                    P.op('act', lambda e, hg=hg: e.activation(out=PeS[hg][:, :, :], in_=pS4[hg][:, :, :], func=AF.Exp, scale=DH ** -0.5),
                         ['pSs%d' % hg], ['PeS%d' % hg])
                    P.op('dve', lambda e, hg=hg, u=u: e.tensor_tensor(out=PmS[hg][:, :, :], in0=PeS[hg][:, :, :],
                                                                      in1=mTs[u][:, None, :].to_broadcast([128, 4, 128]), op=ALU.mult),
                         ['PeS%d' % hg, 'mTs%d' % u], ['PmS%d' % hg])
                    for h4 in range(4):
                        h = hg * 4 + h4
                        mm(pOs[h // 3][:, h % 3, 0:129], PmS[hg][:, h4, :], VbS[u][:, h, 0:129], False, bl == LB - 1,
                           ['PmS%d' % hg, 'VbS%d' % u], ['pOs'], inc=(h4 == 3))

            prep_blk(0)
            for bl in range(LB):
                if bl + 1 < LB:
                    prep_blk(bl + 1)
                attend_blk(bl)
            for h in range(HA):
                P.op('act', lambda e, h=h: e.copy(denS[:, h:h + 1], pOs[h // 3][:, h % 3, 128:129]), ['pOs'], ['denS'])
            P.op('dve', lambda e: e.reciprocal(recS[:, :], denS[:, :]), ['denS'], ['recS'])
            for h in range(HA):
                P.op('dve', lambda e, h=h: e.scalar_tensor_tensor(
                    out=gaT[:, h * 128:(h + 1) * 128], in0=pOs[h // 3][:, h % 3, 0:128], scalar=recS[:, h:h + 1],
                    in1=gaT[:, h * 128:(h + 1) * 128], op0=ALU.mult, op1=ALU.mult), ['pOs', 'recS', 'gaT'], ['gaT'])
            P.dma('sp', lambda e, r0=r0: e.dma_start(out=aS[r0:r0 + 8, :], in_=gaT[0:8, :]), ['gaT'], ['aS'])
            P.barrier()
        P.barrier()
        ed.close()

    eo = ExitStack()
    NTL = NOWN + (1 if SDSA else 0)
    mergedT = sb(eo, "mergedT", [128, KC, NTL * 128], BF16)
    aSt = sb(eo, "aSt", [128, 1024], BF16)
    bSt = sb(eo, "bSt", [128, 4, DVB], BF16)
    gaS6 = sb(eo, "gaS6", [128, NOWN, 1024], BF16)
    P.dma('sp', lambda e: e.dma_start(out=gaS6[:, :, :], in_=gaD.rearrange("(n p) f -> p n f", p=128)), ['gaD'], ['gaS6'])
    yacc = sb(eo, "yacc", [128, NTL, D], F32)
    wost = sb(eo, "wost", [128, KC, 256], F32)
    wobf = [sb(eo, "wobf%d" % i, [128, KC, 256], BF16) for i in range(2)]
    bidx = sb(eo, "bidx_sb", [128, NOWN], I32)
    Bt = [sb(eo, "Bt%d" % i, [128, 4, DVB], BF16) for i in range(2)]
    gfb = sb(eo, "gfb", [128, D], F32)
    yjunk = sb(eo, "yjunk", [128, D], BF16)
    ys = sb(eo, "ys", [128, 4], F32)
    yo = [sb(eo, "yo%d" % i, [128, D], F32) for i in range(2)]
    pT6 = [ps(eo, "pT6%d" % i, [128, 1024], BF16) for i in range(2)]
    py = [ps(eo, "py%d" % i, [128, 512], F32) for i in range(2)]
    P.dma('sp', lambda e: e.dma_start(out=bidx[:, :], in_=bidx_d[:, :]), [], ['bidx'])
    P.dma('sp', lambda e: e.dma_start(out=gfb[:, :], in_=g_f[0:1, :].partition_broadcast(128)), [], ['gfb'])
    P.dma('sp', lambda e: e.dma_start(out=yacc[:, 0:NOWN, :], in_=xo.rearrange("(n p) f -> p n f", p=128)), [], ['yacc'])
    if SDSA:
        P.dma('sp', lambda e: e.dma_start(out=yacc[:, NOWN, :], in_=xsm[:, :]), [], ['yacc'])
        P.op('dve', lambda e: e.memset(aSt[:, :], 0.0), [], ['aSt'])
        P.op('dve', lambda e: e.memset(bSt[:, :, :], 0.0), [], ['bSt'])
        P.dma('sp', lambda e: e.dma_start(out=aSt[0:32, :], in_=aS[0:32, :]), ['aS'], ['aSt'])
        P.dma('sp', lambda e: e.dma_start(out=bSt[0:32, :, :].rearrange("p h v -> p (h v)"), in_=bps[0:32, :]), ['bps'], ['bSt'])
    for k in range(NTL):
        bt = Bt[k % 2] if k < NOWN else bSt
        if k < NOWN:
            P.dma('pool', lambda e, bt=bt, k=k: e.indirect_dma_start(
                out=bt[:, :, :].rearrange("p h v -> p (h v)"), out_offset=None, in_=bp_loc[:, :],
                in_offset=bass.IndirectOffsetOnAxis(ap=bidx[:, k:k + 1], axis=0)),
                ['bp_loc', 'bidx'], ['Bt%d' % (k % 2)])
        for half in range(2):
            p6 = pT6[half].rearrange("p (a b) -> p a b", a=8)
            for i in range(8):
                if half == 0:
                    src_ap = gaS6[:, k, i * 128:(i + 1) * 128] if k < NOWN else aSt[:, i * 128:(i + 1) * 128]
                    rk = ['gaS6' if k < NOWN else 'aSt', 'ident_b']
                else:
                    src_ap = bt[:, i // 2, (i % 2) * 128:(i % 2 + 1) * 128]
                    rk = ['Bt%d' % (k % 2) if k < NOWN else 'bSt', 'ident_b']
                tr(p6[:, i, :], src_ap, ident_b[:, :], rk, ['pT6%d' % half], inc=(i == 7))
            P.op('act' if half == 0 else 'dve',
                 (lambda e, p6=p6, k=k, half=half: e.copy(mergedT[:, half * 8:(half + 1) * 8, k * 128:(k + 1) * 128], p6[:, :, :]))
                 if half == 0 else
                 (lambda e, p6=p6, k=k, half=half: e.tensor_copy(mergedT[:, half * 8:(half + 1) * 8, k * 128:(k + 1) * 128], p6[:, :, :])),
                 ['pT6%d' % half], ['mergedT'])
    wi_ = 0
    yi = 0
    for nb in range(D // 256):
        wb = wi_ % 2
        wi_ += 1
        P.dma('sp', lambda e, nb=nb: e.dma_start(
            out=wost[:, :, :], in_=w_out.rearrange("(k p) c -> p k c", p=128)[:, :, nb * 256:(nb + 1) * 256]), [], ['wost'])
        P.op('pool', lambda e, wb=wb: e.tensor_copy(wobf[wb][:, :, :], wost[:, :, :]), ['wost'], ['wobf%d' % wb])
        for k in range(NTL):
            pb = yi % 2
            yi += 1
            for kc in range(KC):
                mm(py[pb][:, 0:256], mergedT[:, kc, k * 128:(k + 1) * 128], wobf[wb][:, kc, :], kc == 0, kc == KC - 1,
                   ['mergedT', 'wobf%d' % wb], ['py%d' % pb], inc=(kc == KC - 1))
            P.op('dve', lambda e, pb=pb, k=k, nb=nb: e.tensor_tensor(
                out=yacc[:, k, nb * 256:(nb + 1) * 256], in0=yacc[:, k, nb * 256:(nb + 1) * 256], in1=py[pb][:, 0:256],
                op=ALU.add), ['py%d' % pb, 'yacc'], ['yacc'])
    for k in range(NTL):
        o = yo[k % 2]
        P.op('dve', lambda e, k=k: e.scalar_tensor_tensor(out=yjunk[:, :], in0=yacc[:, k, :], scalar=1.0, in1=yacc[:, k, :],
                                                          op0=ALU.mult, op1=ALU.mult, accum_out=ys[:, 0:1]),
             ['yacc'], ['yjunk', 'ys0'])
        P.op('act', lambda e: e.activation(out=ys[:, 1:2], in_=ys[:, 0:1], func=AF.Sqrt, bias=eps_t[:, 0:1], scale=1.0 / D),
             ['ys0', 'eps'], ['ys1'])
        P.op('dve', lambda e: e.reciprocal(ys[:, 2:3], ys[:, 1:2]), ['ys1'], ['ys2'])
        P.op('dve', lambda e, k=k, o=o: e.scalar_tensor_tensor(out=o[:, :], in0=yacc[:, k, :], scalar=ys[:, 2:3], in1=gfb[:, :],
                                                               op0=ALU.mult, op1=ALU.mult), ['yacc', 'ys2', 'gfb'], ['yo%d' % (k % 2)])
        ydst = y_p[k * 128:(k + 1) * 128, :] if k < NOWN else y_s[:, :]
        P.dma('sp', lambda e, ydst=ydst, o=o: e.dma_start(out=ydst, in_=o[:, :]), ['yo%d' % (k % 2)], ['y_p'])
    P.barrier(final=True)
    eo.close()
    P.emit()
    return nc


def _rope_tab(pos, half):
    inv = (10000.0 ** (-np.arange(half, dtype=np.float32) / half)).astype(np.float32)
    ang = pos.astype(np.float32)[:, None] * inv[None, :]
    return np.concatenate([np.cos(ang), np.sin(ang)], axis=1).astype(np.float32)


def own_tiles(S, j):
    return [qb for qb in range(S // 128) if zig(qb) == j]


def make_maps(S, x_prompt, norm_in, w_in, w_gate_up, b_gate, gla_norm, w_out, norm_f, x_sample=None, past=8192,
              state_gla=None, cache_k=None, cache_v=None, cache_idx_k=None, page_table=None):
    NT = S // 128
    NOWN = NT // 4
    w_in = np.asarray(w_in[0], np.float32)
    w_dsa = np.ascontiguousarray(w_in[:, :5200])
    g_in_pk = np.ascontiguousarray(np.asarray(norm_in[0], np.float32).reshape(KC, 128).T)
    pos = np.arange(S)
    csA_p = _rope_tab(pos, 64)
    csI_p = _rope_tab(pos, 32)
    ident = np.eye(128, dtype=np.float32)
    tri = np.triu(np.ones((64, 64), np.float32))
    sd = {}
    if cache_k is not None:
        npool = cache_k.shape[1]
        sd = dict(ck=np.asarray(cache_k[0], np.float32).reshape(npool * 128, HA * DH),
                  cv=np.asarray(cache_v[0], np.float32).reshape(npool * 128, HA * DH),
                  cik=np.asarray(cache_idx_k[0], np.float32).reshape(npool * 128, DIDX),
                  iota_p=np.arange(128, dtype=np.float32).reshape(128, 1))
        ms = np.full((128, 128), NEG, np.float32)
        for q in range(8):
            ms[q, :q + 1] = 0.0
        ms[8:, 0] = 0.0
        sd['mask_s'] = ms
    maps = []
    for c in range(NCORES):
        b, j = c // 4, c % 4
        tiles = own_tiles(S, j)
        rows = np.concatenate([np.arange(t * 128, (t + 1) * 128) for t in tiles])
        xb = np.asarray(x_prompt[b], np.float32)
        w_gla = np.concatenate(
            [w_in[:, o + hh * n:o + (hh + 1) * n] for hh in range(HB)
             for (o, n) in ((C_QB, 128), (C_KB, 128), (C_VB, 256), (C_GB, 256))] + [w_in[:, C_AB:C_AB + 16]], axis=1)
        wgb = np.zeros((32, HB * DKB), np.float32)
        wgb[:16] = np.asarray(w_gate_up[0], np.float32)
        wgb[16] = np.asarray(b_gate[0], np.float32)
        mask_o = np.zeros((NOWN, 128, 512), np.float32)
        for k, qb in enumerate(tiles):
            s0 = min(512 * k, S - 512)
            spos = s0 + np.arange(512)[None, :]
            tpos = qb * 128 + np.arange(128)[:, None]
            mask_o[k] = np.where(spos <= tpos, 0.0, NEG)
        bidx = np.zeros((128, NOWN), np.int32)
        for k, qb in enumerate(tiles):
            bidx[:, k] = qb * 128 + np.arange(128)
        xsm = np.zeros((128, D), np.float32)
        if x_sample is not None:
            xsm[:32] = np.asarray(x_sample, np.float32)[4 * c:4 * c + 4].reshape(32, D)
        spos = np.zeros(128, np.int64)
        spos[:32] = past + (np.arange(32) % 8)
        st_in = np.zeros((4, HB, DKB, DVB), np.float32)
        if state_gla is not None:
            st_in = np.ascontiguousarray(np.asarray(state_gla[0], np.float32)[4 * c:4 * c + 4])
        extra = dict(sd)
        if page_table is not None and cache_k is not None:
            extra['pt_s'] = np.ascontiguousarray(np.asarray(page_table, np.int32)[4 * c:4 * c + 4])
        maps.append(dict(
            st_s_in=st_in, xsm=xsm, **extra, csA_s=_rope_tab(spos, 64), csI_s=_rope_tab(spos, 32),
            xp=xb, xo=np.ascontiguousarray(xb[rows]), w_dsa=w_dsa, w_gla=np.ascontiguousarray(w_gla),
            w_out=np.asarray(w_out[0], np.float32), g_in_pk=g_in_pk,
            g_f=np.asarray(norm_f, np.float32).reshape(1, D), gla_g=np.asarray(gla_norm[0], np.float32).reshape(1, DVB),
            wgb=wgb, csA_p=csA_p, csI_p=csI_p, csA_o=np.ascontiguousarray(csA_p[rows]),
            csI_o=np.ascontiguousarray(csI_p[rows]), ident=ident, tri=tri, mask_o=mask_o, bidx=bidx))
    return maps


_STAGE = 99


def kernel(x_prompt, x_sample, cache_k, cache_v, cache_idx_k, state_gla, page_table,
           norm_in, w_in, w_gate_up, b_gate, gla_norm, w_out, norm_f):
    B, S = x_prompt.shape[0], x_prompt.shape[1]
    Bd, Ts = x_sample.shape[0], x_sample.shape[1]
    past = page_table.shape[1] * 128
    npg, npool = page_table.shape[1], cache_k.shape[1]
    nc = build(S, stage=_STAGE, NPG=npg, NPOOL=npool)
    maps = make_maps(S, x_prompt, norm_in, w_in, w_gate_up, b_gate, gla_norm, w_out, norm_f,
                     x_sample=x_sample, past=past, state_gla=state_gla,
                     cache_k=cache_k, cache_v=cache_v, cache_idx_k=cache_idx_k, page_table=page_table)
    res = run_bass_kernel_spmd(nc, maps, core_ids=list(range(NCORES))).results
    y_prompt = np.zeros((B, S, D), np.float32)
    nk = np.zeros((1, B, S, HA, DH), np.float32)
    nv = np.zeros((1, B, S, HA, DH), np.float32)
    nik = np.zeros((1, B, S, DIDX), np.float32)
    st = np.zeros((1, B, HB, DKB, DVB), np.float32)
    y_sample = np.zeros((Bd, Ts, D), np.float32)
    nks = np.zeros((1, Bd, Ts, HA, DH), np.float32)
    nvs = np.zeros((1, Bd, Ts, HA, DH), np.float32)
    niks = np.zeros((1, Bd, Ts, DIDX), np.float32)
    sts = np.zeros((1, Bd, HB, DKB, DVB), np.float32)
    for c in range(NCORES):
        b, j = c // 4, c % 4
        r = res[c]
        for k, qb in enumerate(own_tiles(S, j)):
            sl = slice(qb * 128, (qb + 1) * 128)
            y_prompt[b, sl] = r["y_p"][k * 128:(k + 1) * 128]
            nk[0, b, sl] = r["nk_p"][k * 128:(k + 1) * 128].reshape(128, HA, DH)
            nv[0, b, sl] = r["nv_p"][k * 128:(k + 1) * 128].reshape(128, HA, DH)
            nik[0, b, sl] = r["nik_p"][k * 128:(k + 1) * 128]
        if j == 0:
            st[0, b] = r["st_p"]
        nks[0, 4 * c:4 * c + 4] = r["nk_s"][:32].reshape(4, Ts, HA, DH)
        nvs[0, 4 * c:4 * c + 4] = r["nv_s"][:32].reshape(4, Ts, HA, DH)
        niks[0, 4 * c:4 * c + 4] = r["nik_s"][:32].reshape(4, Ts, DIDX)
        sts[0, 4 * c:4 * c + 4] = r["st_s"]
        y_sample[4 * c:4 * c + 4] = r["y_s"][:32].reshape(4, Ts, D)
    return (y_prompt, y_sample, nk, nv, nik, st, nks, nvs, niks, sts)
```
